# Optimizing a Trainium2 kernel written in Bass

```python
import jax, jax.numpy as jnp
from jax import lax
import numpy as np

D_MODEL = 2048
BATCH = 4
SEQ = 2048
DEPTH = 4

GRID_W = 64
CTX_LEN = 256
EPS = 1e-6
N_BRANCH = 3
W_BRANCH = D_MODEL
LRU_BLOCKS = 16
LRU_BS = W_BRANCH // LRU_BLOCKS
CONV_W = 4
CONV_PAD_L = 2
LRU_C = 8.0
ML_HEADS = 8
ML_HD = W_BRANCH // ML_HEADS
ML_CHUNK = 64
M_INIT = -1e30
ATT_HD = 128
ATT_HEADS = W_BRANCH // ATT_HD
ATT_KV = 4
GQA = ATT_HEADS // ATT_KV
W_KV = ATT_KV * ATT_HD
Q_BLOCK = 128
ROPE_THETA = 10000.0
IN_SIZES = (W_BRANCH, W_BRANCH,
            W_BRANCH, W_BRANCH, W_BRANCH, W_BRANCH, W_BRANCH, 4 * ML_HEADS,
            W_BRANCH, W_KV, W_KV, W_BRANCH,
            N_BRANCH * D_MODEL)
N_IN = sum(IN_SIZES)

kernel_name = "hybrid_rglru_mlstm_gqa_prefix_dit"


def rmsnorm(x, g):
    xf = x.astype(jnp.float32)
    y = xf * lax.rsqrt(jnp.mean(xf * xf, axis=-1, keepdims=True) + EPS)
    return y * g.astype(jnp.float32)


def split_proj(p):
    parts, off = [], 0
    for n in IN_SIZES:
        parts.append(p[..., off:off + n])
        off += n
    return parts


def conv_centred(x, w, b):
    T = x.shape[1]
    xp = jnp.pad(x.astype(jnp.float32), ((0, 0), (CONV_PAD_L, CONV_W - 1 - CONV_PAD_L), (0, 0)))
    out = b.astype(jnp.float32)
    for k in range(CONV_W):
        out = out + xp[:, k:k + T] * w[k]
    return out


def blockdiag(x, w, b):
    xb = x.reshape(x.shape[0], x.shape[1], LRU_BLOCKS, LRU_BS)
    return jnp.einsum('btnc,ncd->btnd', xb, w).reshape(x.shape) + b


def linear_scan(a, bx, h0, reverse):
    def comb(e1, e2):
        a1, b1 = e1
        a2, b2 = e2
        return a1 * a2, a2 * b1 + b2
    A, Bc = lax.associative_scan(comb, (a, bx), axis=1, reverse=reverse)
    return A * h0[:, None] + Bc


def rglru(xc, wr, br, wi, bi, lam, h0, reverse):
    r = jax.nn.sigmoid(blockdiag(xc, wr, br))
    i = jax.nn.sigmoid(blockdiag(xc, wi, bi))
    log_a = -LRU_C * r * jax.nn.softplus(-lam)
    a = jnp.exp(log_a)
    mult = jnp.sqrt(-jnp.expm1(2.0 * log_a))
    h = linear_scan(a, mult * (i * xc), h0, reverse)
    final = h[:, 0] if reverse else h[:, -1]
    return h, final


def mlstm_chunkwise(q, k, v, ig, lf, state, with_out):
    B, H, T, d = q.shape
    nc = T // ML_CHUNK

    def to_chunks(a):
        return jnp.moveaxis(a.reshape(B, H, nc, ML_CHUNK, *a.shape[3:]), 2, 0)

    causal = jnp.tril(jnp.ones((ML_CHUNK, ML_CHUNK), dtype=bool))

    def step(carry, inp):
        C, n, m = carry
        qc, kc, vc, ic, fc = inp
        b = jnp.cumsum(fc, axis=-1)
        out = None
        if with_out:
            logD = jnp.where(causal, b[..., :, None] - b[..., None, :] + ic[..., None, :], -jnp.inf)
            m_inter = b + m[..., None]
            m_t = jnp.maximum(jnp.max(logD, axis=-1), m_inter)
            Dm = jnp.exp(logD - m_t[..., None])
            w_inter = jnp.exp(m_inter - m_t)
            s = jnp.einsum('bhtd,bhsd->bhts', qc, kc) * Dm
            num = jnp.einsum('bhts,bhse->bhte', s, vc) + w_inter[..., None] * jnp.einsum('bhtd,bhde->bhte', qc, C)
            den = jnp.sum(s, axis=-1) + w_inter * jnp.einsum('bhtd,bhd->bht', qc, n)
            out = num / jnp.maximum(jnp.abs(den), jnp.exp(-m_t))[..., None]
        bL = b[..., -1]
        log_w = bL[..., None] - b + ic
        m_new = jnp.maximum(bL + m, jnp.max(log_w, axis=-1))
        wk = jnp.exp(log_w - m_new[..., None])
        decay = jnp.exp(bL + m - m_new)
        C_new = decay[..., None, None] * C + jnp.einsum('bhs,bhsd,bhse->bhde', wk, kc, vc)
        n_new = decay[..., None] * n + jnp.einsum('bhs,bhsd->bhd', wk, kc)
        return (C_new, n_new, m_new), out

    state, hs = lax.scan(step, state, tuple(to_chunks(a) for a in (q, k, v, ig, lf)))
    h = jnp.moveaxis(hs, 0, 2).reshape(B, H, T, d) if with_out else None
    return h, state


def axial_rope_tables(row, col):
    n_freq = ATT_HD // 4
    inv = 1.0 / (ROPE_THETA ** (jnp.arange(n_freq, dtype=jnp.float32) / n_freq))
    ang = jnp.concatenate([row.astype(jnp.float32)[:, None] * inv,
                           col.astype(jnp.float32)[:, None] * inv], axis=-1)
    return jnp.cos(ang), jnp.sin(ang)


def rope_2d(x, cos, sin):
    half = ATT_HD // 2
    x1, x2 = x[..., :half], x[..., half:]
    return jnp.concatenate([x1 * cos - x2 * sin, x2 * cos + x1 * sin], axis=-1)


def attend_latent(q, k, v, k_ctx, v_ctx):
    B, H, S, d = q.shape
    keys = jnp.concatenate([k, k_ctx], axis=2)
    vals = jnp.concatenate([v, v_ctx], axis=2)
    nb = S // Q_BLOCK
    qb = q.reshape(B, ATT_KV, GQA, nb, Q_BLOCK, d).transpose(3, 0, 1, 2, 4, 5)

    def block(qi):
        s = jnp.einsum('bkgqd,bktd->bkgqt', qi, keys).astype(jnp.float32) * (ATT_HD ** -0.5)
        p = jax.nn.softmax(s, axis=-1)
        return jnp.einsum('bkgqt,bktd->bkgqd', p.astype(vals.dtype), vals)

    o = lax.map(block, qb)
    return o.transpose(1, 0, 4, 2, 3, 5).reshape(B, S, H * d)


def attend_ctx(q, k, v):
    B, H, C, d = q.shape
    qg = q.reshape(B, ATT_KV, GQA, C, d)
    s = jnp.einsum('bkgqd,bktd->bkgqt', qg, k).astype(jnp.float32) * (ATT_HD ** -0.5)
    p = jax.nn.softmax(s, axis=-1)
    o = jnp.einsum('bkgqt,bktd->bkgqd', p.astype(v.dtype), v)
    return o.transpose(0, 3, 1, 2, 4).reshape(B, C, H * d)


def hybrid_mixer(hx, hc, need_ctx, w_in, lru_conv_w, lru_conv_b, lru_wr, lru_br, lru_wi, lru_bi,
                 lru_lam, ml_gate_b, ml_norm, q_norm, k_norm, w_br, w_out, cos, sin):
    B = hx.shape[0]
    px = split_proj(hx @ w_in)
    pc = split_proj(hc @ w_in)

    xl = conv_centred(px[0], lru_conv_w, lru_conv_b)
    xcl = conv_centred(pc[0], lru_conv_w, lru_conv_b)
    zero = jnp.zeros((B, W_BRANCH), jnp.float32)
    lru_x, lru_c = 0.0, 0.0
    for dr, rev in enumerate((False, True)):
        h_c, fin = rglru(xcl, lru_wr[dr], lru_br[dr], lru_wi[dr], lru_bi[dr], lru_lam[dr], zero, rev)
        h_x, _ = rglru(xl, lru_wr[dr], lru_br[dr], lru_wi[dr], lru_bi[dr], lru_lam[dr], fin, rev)
        lru_x = lru_x + h_x
        if need_ctx:
            lru_c = lru_c + h_c
    y_lru_x = lru_x * jax.nn.silu(px[1])

    def ml_heads(a):
        return a.reshape(B, a.shape[1], ML_HEADS, ML_HD).transpose(0, 2, 1, 3).astype(jnp.float32)

    def ml_gates(a):
        g = a.reshape(B, a.shape[1], 2, 2, ML_HEADS).astype(jnp.float32) + ml_gate_b
        return g.transpose(2, 3, 0, 4, 1)

    qmx, kmx, vmx = ml_heads(px[2]), ml_heads(px[3]) * (ML_HD ** -0.5), ml_heads(px[4])
    qmc, kmc, vmc = ml_heads(pc[2]), ml_heads(pc[3]) * (ML_HD ** -0.5), ml_heads(pc[4])
    gx, gc = ml_gates(px[7]), ml_gates(pc[7])
    ml_x, ml_c = 0.0, 0.0
    for dr in range(2):
        fl = (lambda a: jnp.flip(a, axis=2)) if dr == 1 else (lambda a: a)
        st0 = (jnp.zeros((B, ML_HEADS, ML_HD, ML_HD), jnp.float32),
               jnp.zeros((B, ML_HEADS, ML_HD), jnp.float32),
               jnp.full((B, ML_HEADS), M_INIT, jnp.float32))
        h_c, st = mlstm_chunkwise(fl(qmc), fl(kmc), fl(vmc), fl(gc[dr, 0]),
                                  fl(jax.nn.log_sigmoid(gc[dr, 1])), st0, need_ctx)
        h_x, _ = mlstm_chunkwise(fl(qmx), fl(kmx), fl(vmx), fl(gx[dr, 0]),
                                 fl(jax.nn.log_sigmoid(gx[dr, 1])), st, True)
        ml_x = ml_x + fl(h_x)
        if need_ctx:
            ml_c = ml_c + fl(h_c)

    def ml_out(h, o, z):
        T = h.shape[2]
        h = jax.nn.sigmoid(o.astype(jnp.float32)) * h.transpose(0, 2, 1, 3).reshape(B, T, W_BRANCH)
        h = rmsnorm(h.reshape(B, T, ML_HEADS, ML_HD), ml_norm.reshape(ML_HEADS, ML_HD)).reshape(B, T, W_BRANCH)
        return h * jax.nn.silu(z)

    y_ml_x = ml_out(ml_x, px[5], px[6])

    def att_heads(a, H):
        return a.reshape(B, a.shape[1], H, ATT_HD).transpose(0, 2, 1, 3)

    qax = rope_2d(rmsnorm(att_heads(px[8], ATT_HEADS), q_norm), cos, sin)
    kax = rope_2d(rmsnorm(att_heads(px[9], ATT_KV), k_norm), cos, sin)
    vax = att_heads(px[10], ATT_KV).astype(jnp.float32)
    kac = rmsnorm(att_heads(pc[9], ATT_KV), k_norm)
    vac = att_heads(pc[10], ATT_KV).astype(jnp.float32)
    y_att_x = attend_latent(qax, kax, vax, kac, vac) * jax.nn.silu(px[11])

    def merge(ys, gate_logits):
        Y = jnp.stack(ys, axis=2)
        proj = jnp.einsum('btnw,nwd->btnd', Y, w_br)
        g = jax.nn.sigmoid(gate_logits.reshape(B, gate_logits.shape[1], N_BRANCH, D_MODEL))
        return jnp.einsum('btd,de->bte', jnp.sum(g * proj, axis=2), w_out)

    yx = merge([y_lru_x, y_ml_x, y_att_x], px[12])
    yc = None
    if need_ctx:
        y_lru_c = lru_c * jax.nn.silu(pc[1])
        y_ml_c = ml_out(ml_c, pc[5], pc[6])
        qac = rmsnorm(att_heads(pc[8], ATT_HEADS), q_norm)
        y_att_c = attend_ctx(qac, kac, vac) * jax.nn.silu(pc[11])
        yc = merge([y_lru_c, y_ml_c, y_att_c], pc[12])
    return yx, yc


def setup_inputs(seed: int = 0) -> dict:
    key = jax.random.key(seed)
    ks = jax.random.split(key, 24)
    f32 = jnp.float32
    D = D_MODEL

    def nrm(k, shape, scale):
        return jax.random.normal(k, shape, f32) * scale

    u = jax.random.uniform(ks[15], (DEPTH, 2, W_BRANCH), f32, 0.9, 0.999)
    a0 = u ** (1.0 / LRU_C)
    ml_i_b = nrm(ks[16], (DEPTH, 2, ML_HEADS), 0.1)
    ml_f_b = jnp.linspace(3.0, 6.0, ML_HEADS, dtype=f32) + nrm(ks[17], (DEPTH, 2, ML_HEADS), 0.1)
    return {
        "x": nrm(ks[0], (BATCH, SEQ, D), 1.0),
        "c": nrm(ks[1], (BATCH, D), 1.0),
        "ctx": nrm(ks[2], (BATCH, CTX_LEN, D), 1.0),
        "c_ctx": nrm(ks[3], (D,), 1.0),
        "ada_w": nrm(ks[4], (DEPTH, D, 3 * D), 0.5 * D ** -0.5),
        "ada_b": nrm(ks[5], (DEPTH, 3 * D), 0.02),
        "norm_pre": 1.0 + nrm(ks[6], (DEPTH, D), 0.05),
        "norm_post": 1.0 + nrm(ks[7], (DEPTH, D), 0.05),
        "w_in": nrm(ks[8], (DEPTH, D, N_IN), D ** -0.5),
        "lru_conv_w": nrm(ks[9], (DEPTH, CONV_W, W_BRANCH), CONV_W ** -0.5),
        "lru_conv_b": nrm(ks[10], (DEPTH, W_BRANCH), 0.02),
        "lru_wr": nrm(ks[11], (DEPTH, 2, LRU_BLOCKS, LRU_BS, LRU_BS), LRU_BS ** -0.5),
        "lru_br": nrm(ks[12], (DEPTH, 2, W_BRANCH), 0.02),
        "lru_wi": nrm(ks[13], (DEPTH, 2, LRU_BLOCKS, LRU_BS, LRU_BS), LRU_BS ** -0.5),
        "lru_bi": nrm(ks[14], (DEPTH, 2, W_BRANCH), 0.02),
        "lru_lam": jnp.log(a0) - jnp.log1p(-a0),
        "ml_gate_b": jnp.stack([ml_i_b, ml_f_b], axis=2),
        "ml_norm": 1.0 + nrm(ks[18], (DEPTH, W_BRANCH), 0.05),
        "q_norm": 1.0 + nrm(ks[19], (DEPTH, ATT_HD), 0.05),
        "k_norm": 1.0 + nrm(ks[20], (DEPTH, ATT_HD), 0.05),
        "w_br": nrm(ks[21], (DEPTH, N_BRANCH, W_BRANCH, D), W_BRANCH ** -0.5),
        "w_out": nrm(ks[22], (DEPTH, D, D), D ** -0.5),
    }


def reference(x, c, ctx, c_ctx, ada_w, ada_b, norm_pre, norm_post, w_in, lru_conv_w, lru_conv_b,
              lru_wr, lru_br, lru_wi, lru_bi, lru_lam, ml_gate_b, ml_norm, q_norm, k_norm, w_br, w_out):
    S = x.shape[1]
    rows = S // GRID_W
    row = jnp.repeat(jnp.arange(rows, dtype=jnp.int32), GRID_W)
    col = jnp.tile(jnp.arange(GRID_W, dtype=jnp.int32), rows)
    cos, sin = axial_rope_tables(row, col)
    u = ctx
    for l in range(DEPTH):
        need_ctx = l < DEPTH - 1
        shift_x, scale_x, gate_x = jnp.split(jax.nn.silu(c) @ ada_w[l] + ada_b[l], 3, axis=-1)
        shift_c, scale_c, gate_c = jnp.split(jax.nn.silu(c_ctx) @ ada_w[l] + ada_b[l], 3, axis=-1)
        hx = rmsnorm(x, norm_pre[l]) * (1.0 + scale_x[:, None]) + shift_x[:, None]
        hc = rmsnorm(u, norm_pre[l]) * (1.0 + scale_c) + shift_c
        yx, yc = hybrid_mixer(hx.astype(x.dtype), hc.astype(x.dtype), need_ctx, w_in[l], lru_conv_w[l],
                              lru_conv_b[l], lru_wr[l], lru_br[l], lru_wi[l], lru_bi[l], lru_lam[l],
                              ml_gate_b[l], ml_norm[l], q_norm[l], k_norm[l], w_br[l], w_out[l], cos, sin)
        x = (x + gate_x[:, None] * rmsnorm(yx, norm_post[l])).astype(x.dtype)
        if need_ctx:
            u = (u + gate_c * rmsnorm(yc, norm_post[l])).astype(u.dtype)
    return x
```

```python
import numpy as np
import ml_dtypes
import concourse.bass as bass
import concourse.mybir as mybir
from concourse.bass_utils import run_bass_kernel_spmd

F32 = mybir.dt.float32
BF16 = mybir.dt.bfloat16
AF = mybir.ActivationFunctionType
ALU = mybir.AluOpType

D = 2048
NCH = 16
TCX = 256
TLAT = 2048
T = TCX + TLAT
NTILE = T // 128
NIN = 25632
EPS = 1e-6
TT = [(0, 256), (256, 512), (768, 512), (1280, 512), (1792, 512)]
O_LX, O_LZ, O_MQ, O_MK, O_MV, O_MO, O_MZ, O_MG, O_AQ, O_AK, O_AV, O_AZ, O_G = (
    0, 2048, 4096, 6144, 8192, 10240, 12288, 14336, 14368, 16416, 16928, 17440, 19488)
V_ADAB, V_NPRE, V_NPOST, V_CW, V_CB, V_BR, V_BI, V_LAM, V_MLN, V_QN, V_KN = (
    0, 48, 64, 80, 144, 160, 192, 224, 256, 272, 273)
NV = 274
ARENA_WORDS = 26112


class Buf:
    __slots__ = ("name", "w", "r", "excl")

    def __init__(self, name="", excl=False):
        self.name = name
        self.w = None
        self.r = {}
        self.excl = excl


class Tl:
    __slots__ = ("ap", "buf")

    def __init__(self, ap, buf=None):
        self.ap = ap
        self.buf = buf if buf is not None else Buf()


class Sched:
    COMPUTE = ("pe", "act", "dve", "pool")

    def __init__(self, nc, n_dma_sems=8):
        self.nc = nc
        self.sems = {}
        self.cnt = {}
        self.prog = {e: [] for e in ("pe", "act", "dve", "pool", "sp")}
        for e in self.COMPUTE:
            self.sems[e] = nc.alloc_semaphore("s_" + e)
            self.cnt[e] = 0
        self.dma_rot = {}
        for q in ("sp", "pool"):
            ids = []
            for i in range(n_dma_sems):
                cid = ("dma", q, i)
                self.sems[cid] = nc.alloc_semaphore("d_%s_%d" % (q, i))
                self.cnt[cid] = 0
                ids.append(cid)
            self.dma_rot[q] = [ids, 0]
        self.known = {e: {} for e in self.prog}

    def _deps(self, reads, writes):
        deps = {}
        for b in reads:
            if b.w is not None:
                c, v = b.w
                if deps.get(c, 0) < v:
                    deps[c] = v
        for b in writes:
            if b.w is not None:
                c, v = b.w
                if deps.get(c, 0) < v:
                    deps[c] = v
            for c, v in b.r.items():
                if deps.get(c, 0) < v:
                    deps[c] = v
        return deps

    def _waits(self, eng, deps):
        kn = self.known[eng]
        waits = []
        for c, v in deps.items():
            if c == eng and eng == "pe":
                continue
            if kn.get(c, 0) < v:
                kn[c] = v
                waits.append((self.sems[c], v))
        return waits

    def op(self, eng, fn, reads=(), writes=()):
        ex = [b for b in reads if b.excl]
        if ex:
            writes = list(writes) + ex
        waits = self._waits(eng, self._deps(reads, writes))
        self.cnt[eng] += 1
        val = self.cnt[eng]
        sem = self.sems[eng]

        def run(e, waits=waits, fn=fn, sem=sem):
            for s, v in waits:
                e.wait_ge(s, v)
            fn(e).then_inc(sem, 1)
        self.prog[eng].append(run)
        for b in reads:
            if b.r.get(eng, 0) < val:
                b.r[eng] = val
        for b in writes:
            b.w = (eng, val)
            b.r = {}
        return val

    def dma(self, q, fn, reads=(), writes=()):
        rot = self.dma_rot[q]
        cid = rot[0][rot[1]]
        rot[1] = (rot[1] + 1) % len(rot[0])
        deps = self._deps(reads, writes)
        if self.cnt[cid] > 0:
            deps[cid] = max(deps.get(cid, 0), self.cnt[cid])
        waits = self._waits(q, deps)
        self.cnt[cid] += 16
        val = self.cnt[cid]
        sem = self.sems[cid]

        def run(e, waits=waits, fn=fn, sem=sem):
            for s, v in waits:
                e.wait_ge(s, v)
            fn(e).then_inc(sem, 16)
        self.prog[q].append(run)
        for b in reads:
            b.r[cid] = val
        for b in writes:
            b.w = (cid, val)
            b.r = {}

    def barrier(self):
        deps = {c: v for c, v in self.cnt.items() if v > 0}
        for eng in self.prog:
            waits = self._waits(eng, dict(deps))

            def run(e, waits=waits):
                for s, v in waits:
                    e.wait_ge(s, v)
            self.prog[eng].append(run)

    def finalize(self):
        nc = self.nc
        prog = self.prog
        with nc.Block() as block:
            @block.tensor
            def _(e):
                for f in prog["pe"]:
                    f(e)

            @block.scalar
            def _(e):
                for f in prog["act"]:
                    f(e)

            @block.vector
            def _(e):
                for f in prog["dve"]:
                    f(e)

            @block.gpsimd
            def _(e):
                for f in prog["pool"]:
                    f(e)

            @block.sync
            def _(e):
                for f in prog["sp"]:
                    f(e)


class Rot:
    def __init__(self, items):
        self.items = items
        self.i = 0

    def next(self):
        t = self.items[self.i]
        self.i = (self.i + 1) % len(self.items)
        return t


class Arena:
    def __init__(self, tensor, nwords):
        self.t = tensor
        self.n = nwords
        self.off = 0

    def reset(self):
        self.off = 0

    def _take(self, nw):
        a = self.off
        assert a + nw <= self.n, ("arena overflow", a + nw, self.n)
        self.off += nw
        return self.t[:, a:a + nw]

    def f32(self, *shape):
        n = int(np.prod(shape))
        ap = self._take(n)
        if len(shape) == 2:
            ap = ap.rearrange("p (a b) -> p a b", a=shape[0])
        elif len(shape) == 3:
            ap = ap.rearrange("p (a b c) -> p a b c", a=shape[0], b=shape[1])
        return Tl(ap)

    def bf16(self, *shape):
        n = int(np.prod(shape))
        nw = (n + 1) // 2
        ap = self._take(nw).bitcast(BF16)[:, 0:n]
        if len(shape) == 2:
            ap = ap.rearrange("p (a b) -> p a b", a=shape[0])
        elif len(shape) == 3:
            ap = ap.rearrange("p (a b c) -> p a b c", a=shape[0], b=shape[1])
        return Tl(ap)


def build(n_layers=4, debug=False, phases=("lru", "ml", "att", "merge")):
    nc = bass.Bass("TRN2", target_bir_lowering=False)
    S = Sched(nc)

    def din(name, shape, dt=F32):
        return nc.dram_tensor(name, list(shape), dt, kind="ExternalInput").ap()

    xT0 = din("xT0", [D, T])
    cTd = din("cT", [128, 32])
    ada_w = din("ada_w", [4, D, 6144])
    w_in = din("w_in", [4, D, NIN])
    w_br = din("w_br", [4, 3, D, D])
    w_out = din("w_out", [4, D, D])
    lru_wr = din("lru_wr", [4, 2, 16, 128, 128])
    lru_wi = din("lru_wi", [4, 2, 16, 128, 128])
    vecs_d = din("vecs", [128, 4 * NV])
    gbB_d = din("gbB", [128, 4 * 32])
    cst_d = din("cst", [128, 4 * 128])
    cos_d = din("cosT", [128, TLAT])
    sin_d = din("sinT", [128, TLAT])
    out_d = nc.dram_tensor("outT", [D, TLAT], F32, kind="ExternalOutput").ap()
    ykind = "ExternalOutput" if debug else "Internal"
    xres = nc.dram_tensor("xres", [T // 256, 128, NCH * 256], F32, kind=ykind).ap()
    Yd = nc.dram_tensor("Ybr", [3, D, T], BF16, kind=ykind).ap()
    Gd = nc.dram_tensor("Gbr", [3, T // 256, 128, NCH * 256], BF16, kind=ykind).ap()
    xres_b = [Buf("xres%d" % i) for i in range(9)]
    Y_b = [Buf("Y%d" % i) for i in range(3)]
    G_b = [Buf("G%d" % i) for i in range(3)]
    out_b = Buf("out")

    hxT_t = nc.alloc_sbuf_tensor("hxT", [128, NCH, T], BF16)
    hxT = hxT_t[:]
    hx_b = [Buf("hx%d" % i) for i in range(len(TT))]
    hx_all = hx_b
    WP = Rot([Tl(nc.alloc_sbuf_tensor("wp%d" % i, [128, NCH, 128], BF16)[:]) for i in range(6)])
    arena_t = nc.alloc_sbuf_tensor("arena", [128, ARENA_WORDS], F32)
    AR = Arena(arena_t, ARENA_WORDS)
    vecs = Tl(nc.alloc_sbuf_tensor("vecs_s", [128, 4 * NV], F32)[:])
    gbB = Tl(nc.alloc_sbuf_tensor("gbB_s", [128, 4 * 32], F32)[:])
    cst = Tl(nc.alloc_sbuf_tensor("cst_s", [128, 4 * 128], F32)[:])
    cstb = Tl(nc.alloc_sbuf_tensor("cstb_s", [128, 4 * 128], BF16)[:])
    ones_b = Tl(nc.alloc_sbuf_tensor("ones_b", [128, 128], BF16)[:])
    ones_f = Tl(nc.alloc_sbuf_tensor("ones_f", [128, 128], F32)[:])
    scT = Tl(nc.alloc_sbuf_tensor("scT", [128, 32], F32)[:])
    modT = Tl(nc.alloc_sbuf_tensor("modT", [128, 48, 2], F32)[:])
    Amod = Tl(nc.alloc_sbuf_tensor("Amod", [128, 16, 2], F32)[:])
    Gmod = Tl(nc.alloc_sbuf_tensor("Gmod", [128, 16, 2], F32)[:])
    lamk = Tl(nc.alloc_sbuf_tensor("lamk", [128, 2, 32], F32)[:])
    TRIF = cst.ap[:, 0:128]
    TRIB = cst.ap[:, 128:256]
    RROT = cstb.ap[:, 256:384]

    psum_t = [nc.alloc_psum_tensor("ps%d" % i, [128, 2, 512], F32) for i in range(4)]
    PSB = [Tl(psum_t[i // 2][:, i % 2, :], Buf("psb%d" % i, excl=True)) for i in range(8)]

    def vcol(l, off, n=1):
        return vecs.ap[:, l * NV + off: l * NV + off + n]

    S.dma("sp", lambda e: e.dma_start(out=vecs.ap, in_=vecs_d), writes=[vecs.buf])
    S.dma("sp", lambda e: e.dma_start(out=gbB.ap, in_=gbB_d), writes=[gbB.buf])
    S.dma("sp", lambda e: e.dma_start(out=cst.ap, in_=cst_d), writes=[cst.buf])
    S.dma("pool", lambda e: e.dma_start(out=cstb.ap, in_=cst_d), writes=[cstb.buf])
    S.dma("sp", lambda e: e.dma_start(out=scT.ap, in_=cTd), writes=[scT.buf])
    S.op("dve", lambda e: e.memset(ones_b.ap, 1.0), writes=[ones_b.buf])
    S.op("dve", lambda e: e.memset(ones_f.ap, 1.0), writes=[ones_f.buf])
    S.op("act", lambda e: e.activation(out=scT.ap, in_=scT.ap, func=AF.Silu), reads=[scT.buf], writes=[scT.buf])

    def load_w(src2d, n=128):
        wt = WP.next()
        S.dma("pool", lambda e: e.dma_start(out=wt.ap[:, :, 0:n], in_=src2d.rearrange("(k p) c -> p k c", p=128)),
              writes=[wt.buf])
        return wt

    def proj_fm(wt, n, ti, ps):
        t0, w = TT[ti]

        def f(e):
            for kc in range(NCH):
                ins = e.matmul(ps.ap[0:n, 0:w], lhsT=wt.ap[:, kc, 0:n], rhs=hxT[:, kc, t0:t0 + w],
                               start=(kc == 0), stop=(kc == NCH - 1))
            return ins
        S.op("pe", f, reads=[wt.buf, hx_b[ti]], writes=[ps.buf])

    def proj_fm_rng(wt, n, t0, w, ps):
        def f(e):
            for kc in range(NCH):
                ins = e.matmul(ps.ap[0:n, 0:w], lhsT=wt.ap[:, kc, 0:n], rhs=hxT[:, kc, t0:t0 + w],
                               start=(kc == 0), stop=(kc == NCH - 1))
            return ins
        S.op("pe", f, reads=[wt.buf] + hx_all, writes=[ps.buf])

    evac_flip = [0]

    def evac(out_ap, out_buf, ps, in_ap, scale=None):
        evac_flip[0] ^= 1
        if evac_flip[0]:
            if scale is None:
                S.op("act", lambda e: e.activation(out=out_ap, in_=in_ap, func=AF.Copy), reads=[ps.buf],
                     writes=[out_buf])
            else:
                S.op("act", lambda e: e.activation(out=out_ap, in_=in_ap, func=AF.Copy, scale=scale),
                     reads=[ps.buf], writes=[out_buf])
        else:
            if scale is None:
                S.op("dve", lambda e: e.tensor_copy(out=out_ap, in_=in_ap), reads=[ps.buf], writes=[out_buf])
            else:
                S.op("dve", lambda e: e.tensor_scalar(out=out_ap, in0=in_ap, scalar1=scale, scalar2=None,
                                                      op0=ALU.mult), reads=[ps.buf], writes=[out_buf])

    def phase0(l):
        AR.reset()
        PSG = Rot(PSB)
        wts = [AR.f32(6144), AR.f32(6144)]
        acc = AR.f32(96)
        for kc in range(NCH):
            wt = wts[kc % 2]
            S.dma("sp", lambda e, wt=wt, kc=kc: e.dma_start(out=wt.ap, in_=ada_w[l, kc * 128:(kc + 1) * 128, :]),
                  writes=[wt.buf])
            ps = PSG.next()

            def f(e, wt=wt, kc=kc, ps=ps):
                for j in range(48):
                    ins = e.matmul(ps.ap[:, 2 * j:2 * j + 2], lhsT=wt.ap[:, j * 128:(j + 1) * 128],
                                   rhs=scT.ap[:, 2 * kc:2 * kc + 2], start=True, stop=True)
                return ins
            S.op("pe", f, reads=[wt.buf, scT.buf], writes=[ps.buf])
            if kc == 0:
                S.op("dve", lambda e, ps=ps: e.tensor_copy(out=acc.ap, in_=ps.ap[:, 0:96]), reads=[ps.buf],
                     writes=[acc.buf])
            else:
                S.op("dve", lambda e, ps=ps: e.tensor_tensor(out=acc.ap, in0=ps.ap[:, 0:96], in1=acc.ap, op=ALU.add),
                     reads=[ps.buf, acc.buf], writes=[acc.buf])
        accv = acc.ap.rearrange("p (j i) -> p j i", i=2)
        for i in range(2):
            S.op("dve", lambda e, i=i: e.tensor_tensor(out=modT.ap[:, :, i], in0=accv[:, :, i],
                                                       in1=vcol(l, V_ADAB, 48), op=ALU.add),
                 reads=[acc.buf, vecs.buf], writes=[modT.buf])
        for i in range(2):
            S.op("dve", lambda e, i=i: e.scalar_tensor_tensor(out=Amod.ap[:, :, i], in0=modT.ap[:, 16:32, i],
                                                              scalar=1.0, in1=vcol(l, V_NPRE, 16),
                                                              op0=ALU.add, op1=ALU.mult),
                 reads=[modT.buf, vecs.buf], writes=[Amod.buf])
            S.op("dve", lambda e, i=i: e.tensor_tensor(out=Gmod.ap[:, :, i], in0=modT.ap[:, 32:48, i],
                                                       in1=vcol(l, V_NPOST, 16), op=ALU.mult),
                 reads=[modT.buf, vecs.buf], writes=[Gmod.buf])
        tmp = AR.f32(32)
        S.op("act", lambda e: e.activation(out=tmp.ap, in_=vcol(l, V_LAM, 32), func=AF.Exp, scale=-1.0),
             reads=[vecs.buf], writes=[tmp.buf])
        S.op("act", lambda e: e.activation(out=tmp.ap, in_=tmp.ap, func=AF.Ln, bias=1.0),
             reads=[tmp.buf], writes=[tmp.buf])
        S.op("dve", lambda e: e.tensor_scalar(out=lamk.ap[:, 0, :], in0=tmp.ap, scalar1=-8.0, scalar2=None,
                                              op0=ALU.mult), reads=[tmp.buf], writes=[lamk.buf])
        S.op("dve", lambda e: e.tensor_scalar(out=lamk.ap[:, 1, :], in0=tmp.ap, scalar1=-16.0, scalar2=None,
                                              op0=ALU.mult), reads=[tmp.buf], writes=[lamk.buf])

    def phaseA(l):
        AR.reset()
        PSG = Rot(PSB)
        xt = AR.f32(16, 512)
        rstd = AR.f32(512)
        sqs = Rot([AR.bf16(512) for _ in range(3)])
        tmps = Rot([AR.f32(512) for _ in range(3)])
        for ti, (t0, w) in enumerate(TT):
            i = 1 if ti == 0 else 0
            if l == 0:
                S.dma("sp", lambda e, t0=t0, w=w: e.dma_start(out=xt.ap[:, :, 0:w],
                                                             in_=xT0[:, t0:t0 + w].rearrange("(k p) t -> p k t", p=128)),
                      writes=[xt.buf])
            else:
                for mt in range(t0 // 256, (t0 + w) // 256):
                    o = mt * 256 - t0
                    S.dma("sp", lambda e, mt=mt, o=o: e.dma_start(
                        out=xt.ap[:, :, o:o + 256], in_=xres[mt].rearrange("p (k t) -> p k t", k=NCH)),
                        reads=[xres_b[mt]], writes=[xt.buf])
            ps = PSG.next()
            for kc in range(NCH):
                sq = sqs.next()
                S.op("act", lambda e, sq=sq, kc=kc, w=w: e.activation(out=sq.ap[:, 0:w], in_=xt.ap[:, kc, 0:w],
                                                                     func=AF.Square),
                     reads=[xt.buf], writes=[sq.buf])
                S.op("pe", lambda e, sq=sq, kc=kc, w=w, ps=ps: e.matmul(ps.ap[:, 0:w], lhsT=ones_b.ap, rhs=sq.ap[:, 0:w],
                                                                       start=(kc == 0), stop=(kc == NCH - 1)),
                     reads=[sq.buf, ones_b.buf], writes=[ps.buf])
            S.op("act", lambda e, w=w, ps=ps: e.activation(out=rstd.ap[:, 0:w], in_=ps.ap[:, 0:w], func=AF.Ln,
                                                          scale=1.0 / D, bias=EPS),
                 reads=[ps.buf], writes=[rstd.buf])
            S.op("act", lambda e, w=w: e.activation(out=rstd.ap[:, 0:w], in_=rstd.ap[:, 0:w], func=AF.Exp, scale=-0.5),
                 reads=[rstd.buf], writes=[rstd.buf])
            for kc in range(NCH):
                tp = tmps.next()
                S.op("dve", lambda e, tp=tp, kc=kc, w=w: e.tensor_tensor(out=tp.ap[:, 0:w], in0=xt.ap[:, kc, 0:w],
                                                                        in1=rstd.ap[:, 0:w], op=ALU.mult),
                     reads=[xt.buf, rstd.buf], writes=[tp.buf])
                S.op("act", lambda e, tp=tp, kc=kc, w=w, t0=t0, i=i: e.activation(
                    out=hxT[:, kc, t0:t0 + w], in_=tp.ap[:, 0:w], func=AF.Identity,
                    scale=Amod.ap[:, kc, i:i + 1], bias=modT.ap[:, kc, i:i + 1]),
                    reads=[tp.buf, Amod.buf, modT.buf], writes=[hx_b[ti]])

    def phase_lru(l):
        AR.reset()
        PSG = Rot(PSB)
        XP = 2312
        xsets = Rot([(AR.f32(XP), AR.f32(T), AR.bf16(T)) for _ in range(1)])
        aFs = [AR.f32(T), AR.f32(T)]
        bFs = [AR.f32(T), AR.f32(T)]
        tFs = [AR.f32(T), AR.f32(T)]
        hs = AR.f32(T)
        wgs = Rot([AR.bf16(4, 128) for _ in range(3)])
        szs = Rot([AR.f32(512) for _ in range(5)])
        ybs = Rot([AR.bf16(512) for _ in range(2)])
        for (xp_, _, _) in xsets.items:
            S.op("dve", lambda e, xp_=xp_: e.memset(xp_.ap, 0.0), writes=[xp_.buf])
        segs = [(0, 0, TCX), (259, TCX, TLAT)]

        def stage_x(n):
            xpad, xc, xcb = xsets.next()
            wx = load_w(w_in[l, :, O_LX + n * 128:O_LX + (n + 1) * 128])
            wz = load_w(w_in[l, :, O_LZ + n * 128:O_LZ + (n + 1) * 128])
            wg = wgs.next()
            S.dma("pool", lambda e: e.dma_start(out=wg.ap[:, 0:2, :], in_=lru_wr[l, :, n].rearrange("r c d -> c r d")),
                  writes=[wg.buf])
            S.dma("pool", lambda e: e.dma_start(out=wg.ap[:, 2:4, :], in_=lru_wi[l, :, n].rearrange("r c d -> c r d")),
                  writes=[wg.buf])
            for ti, (t0, w) in enumerate(TT):
                ps = PSG.next()
                proj_fm(wx, 128, ti, ps)
                off = 2 + t0 if ti == 0 else 259 + 2 + (t0 - TCX)
                evac(xpad.ap[:, off:off + w], xpad.buf, ps, ps.ap[:, 0:w])
            for (po, t0, ln) in segs:
                S.op("dve", lambda e, po=po, t0=t0, ln=ln: e.tensor_scalar(
                    out=xc.ap[:, t0:t0 + ln], in0=xpad.ap[:, po:po + ln], scalar1=vcol(l, V_CW + n * 4 + 0),
                    scalar2=vcol(l, V_CB + n), op0=ALU.mult, op1=ALU.add),
                    reads=[xpad.buf, vecs.buf], writes=[xc.buf])
                for k in range(1, 4):
                    S.op("dve", lambda e, po=po, t0=t0, ln=ln, k=k: e.scalar_tensor_tensor(
                        out=xc.ap[:, t0:t0 + ln], in0=xpad.ap[:, po + k:po + k + ln], scalar=vcol(l, V_CW + n * 4 + k),
                        in1=xc.ap[:, t0:t0 + ln], op0=ALU.mult, op1=ALU.add),
                        reads=[xpad.buf, vecs.buf, xc.buf], writes=[xc.buf])
            S.op("act", lambda e: e.activation(out=xcb.ap, in_=xc.ap, func=AF.Copy), reads=[xc.buf], writes=[xcb.buf])
            return (xc, xcb, wz, wg)

        def stage_g(n, ctx):
            xc, xcb, wz, wg = ctx
            szt = []
            for ti, (t0, w) in enumerate(TT):
                ps = PSG.next()
                proj_fm(wz, 128, ti, ps)
                sz = szs.next()
                S.op("act", lambda e, sz=sz, ps=ps, w=w: e.activation(out=sz.ap[:, 0:w], in_=ps.ap[:, 0:w], func=AF.Silu),
                     reads=[ps.buf], writes=[sz.buf])
                szt.append(sz)

            def do_dir(dr):
                aF, bF, tF = aFs[dr], bFs[dr], tFs[dr]
                for ti, (t0, w) in enumerate(TT):
                    psr = PSG.next()
                    S.op("pe", lambda e, psr=psr, dr=dr, t0=t0, w=w: e.matmul(
                        psr.ap[:, 0:w], lhsT=wg.ap[:, dr, :], rhs=xcb.ap[:, t0:t0 + w], start=True, stop=True),
                        reads=[wg.buf, xcb.buf], writes=[psr.buf])
                    S.op("act", lambda e, psr=psr, dr=dr, t0=t0, w=w: e.activation(
                        out=aF.ap[:, t0:t0 + w], in_=psr.ap[:, 0:w], func=AF.Sigmoid,
                        bias=vcol(l, V_BR + dr * 16 + n)), reads=[psr.buf, vecs.buf], writes=[aF.buf])
                    psi = PSG.next()
                    S.op("pe", lambda e, psi=psi, dr=dr, t0=t0, w=w: e.matmul(
                        psi.ap[:, 0:w], lhsT=wg.ap[:, 2 + dr, :], rhs=xcb.ap[:, t0:t0 + w], start=True, stop=True),
                        reads=[wg.buf, xcb.buf], writes=[psi.buf])
                    S.op("act", lambda e, psi=psi, dr=dr, t0=t0, w=w: e.activation(
                        out=bF.ap[:, t0:t0 + w], in_=psi.ap[:, 0:w], func=AF.Sigmoid,
                        bias=vcol(l, V_BI + dr * 16 + n)), reads=[psi.buf, vecs.buf], writes=[bF.buf])
                k1 = lamk.ap[:, 0, dr * 16 + n:dr * 16 + n + 1]
                k2 = lamk.ap[:, 1, dr * 16 + n:dr * 16 + n + 1]
                S.op("act", lambda e, k2=k2: e.activation(out=tF.ap, in_=aF.ap, func=AF.Exp, scale=k2),
                     reads=[aF.buf, lamk.buf], writes=[tF.buf])
                S.op("act", lambda e, k1=k1: e.activation(out=aF.ap, in_=aF.ap, func=AF.Exp, scale=k1),
                     reads=[aF.buf, lamk.buf], writes=[aF.buf])
                S.op("act", lambda e: e.activation(out=tF.ap, in_=tF.ap, func=AF.Sqrt, scale=-1.0, bias=1.0),
                     reads=[tF.buf], writes=[tF.buf])
                S.op("dve", lambda e: e.tensor_tensor(out=bF.ap, in0=bF.ap, in1=tF.ap, op=ALU.mult),
                     reads=[bF.buf, tF.buf], writes=[bF.buf])
                S.op("dve", lambda e: e.tensor_tensor(out=bF.ap, in0=bF.ap, in1=xc.ap, op=ALU.mult),
                     reads=[bF.buf, xc.buf], writes=[bF.buf])
                if dr == 0:
                    S.op("dve", lambda e: e.tensor_tensor_scan(out=hs.ap[:, 0:TCX], data0=aF.ap[:, 0:TCX],
                                                               data1=bF.ap[:, 0:TCX], initial=0.0,
                                                               op0=ALU.mult, op1=ALU.add),
                         reads=[aF.buf, bF.buf], writes=[hs.buf])
                    S.op("dve", lambda e: e.tensor_tensor_scan(out=hs.ap[:, TCX:T], data0=aF.ap[:, TCX:T],
                                                               data1=bF.ap[:, TCX:T], initial=hs.ap[:, TCX - 1:TCX],
                                                               op0=ALU.mult, op1=ALU.add),
                         reads=[aF.buf, bF.buf, hs.buf], writes=[hs.buf])
                else:
                    S.op("dve", lambda e: e.tensor_tensor_scan(out=tF.ap[:, 0:TCX][:, ::-1],
                                                               data0=aF.ap[:, 0:TCX][:, ::-1],
                                                               data1=bF.ap[:, 0:TCX][:, ::-1], initial=0.0,
                                                               op0=ALU.mult, op1=ALU.add),
                         reads=[aF.buf, bF.buf], writes=[tF.buf])
                    S.op("dve", lambda e: e.tensor_tensor_scan(out=tF.ap[:, TCX:T][:, ::-1],
                                                               data0=aF.ap[:, TCX:T][:, ::-1],
                                                               data1=bF.ap[:, TCX:T][:, ::-1],
                                                               initial=tF.ap[:, 0:1],
                                                               op0=ALU.mult, op1=ALU.add),
                         reads=[aF.buf, bF.buf, tF.buf], writes=[tF.buf])
                    S.op("dve", lambda e: e.tensor_tensor(out=hs.ap, in0=hs.ap, in1=tF.ap, op=ALU.add),
                         reads=[hs.buf, tF.buf], writes=[hs.buf])
            for dr in range(2):
                do_dir(dr)
            for ti, (t0, w) in enumerate(TT):
                sz = szt[ti]
                yb = ybs.next()
                S.op("dve", lambda e, sz=sz, yb=yb, t0=t0, w=w: e.tensor_tensor(out=yb.ap[:, 0:w], in0=hs.ap[:, t0:t0 + w],
                                                                               in1=sz.ap[:, 0:w], op=ALU.mult),
                     reads=[hs.buf, sz.buf], writes=[yb.buf])
                S.dma("sp", lambda e, yb=yb, t0=t0, w=w: e.dma_start(out=Yd[0, n * 128:(n + 1) * 128, t0:t0 + w],
                                                                    in_=yb.ap[:, 0:w]),
                      reads=[yb.buf], writes=[Y_b[0]])

        for n in range(NCH):
            stage_g(n, stage_x(n))

    def phase_ml(l):
        AR.reset()
        PSG = Rot([PSB[0], PSB[1], PSB[4], PSB[5]])
        PSE = PSB[0]
        PSA = Rot([PSB[1], PSB[4], PSB[5]])
        PSS = [(PSB[6], PSB[7], psum_t[3]), (PSB[2], PSB[3], psum_t[1])]
        qT = AR.bf16(2, T)
        kT = AR.bf16(2, T)
        ktm = AR.bf16(NTILE, 256)
        vtm = AR.bf16(NTILE, 256)
        hD = [AR.bf16(2, T), AR.bf16(2, T)]
        Gtm = AR.f32(NTILE, 32)
        lftm = AR.f32(NTILE, 32)
        rtm = AR.f32(NTILE, 2, 8)
        sLa = AR.f32(NTILE, 32)
        Ecs = [Rot([AR.f32(128) for _ in range(2)]) for _ in range(2)]
        rhc = Rot([AR.f32(128) for _ in range(2)])
        ST32 = [AR.f32(2, 384), AR.f32(2, 384)]
        STb = [AR.bf16(2, 384), AR.bf16(2, 384)]
        tmpC = Rot([AR.f32(2, 384) for _ in range(2)])
        PTs = Rot([AR.bf16(128) for _ in range(4)])
        kps = Rot([AR.bf16(256) for _ in range(4)])
        t128 = Rot([AR.f32(128) for _ in range(4)])
        qss = [Rot([AR.bf16(2, 128) for _ in range(3)]) for _ in range(2)]
        e256 = Rot([AR.f32(256) for _ in range(6)])
        sq256 = Rot([AR.bf16(256) for _ in range(2)])
        yb256 = Rot([AR.bf16(256) for _ in range(2)])
        hg256 = Rot([AR.f32(2, 256) for _ in range(2)])

        wgt = load_w(w_in[l, :, O_MG:O_MG + 32], n=32)
        for c in range(NTILE):
            ps = PSG.next()

            def f(e, c=c, ps=ps):
                for kc in range(NCH):
                    ins = e.matmul(ps.ap[:, 0:32], lhsT=hxT[:, kc, c * 128:(c + 1) * 128], rhs=wgt.ap[:, kc, 0:32],
                                   start=(kc == 0), stop=(kc == NCH - 1))
                return ins
            S.op("pe", f, reads=[wgt.buf] + hx_all, writes=[ps.buf])
            S.op("dve", lambda e, c=c, ps=ps: e.tensor_tensor(out=Gtm.ap[:, c, :], in0=ps.ap[:, 0:32],
                                                              in1=gbB.ap[:, l * 32:(l + 1) * 32], op=ALU.add),
                 reads=[ps.buf, gbB.buf], writes=[Gtm.buf])
        S.op("act", lambda e: e.activation(out=lftm.ap, in_=Gtm.ap, func=AF.Exp, scale=-1.0),
             reads=[Gtm.buf], writes=[lftm.buf])
        S.op("act", lambda e: e.activation(out=lftm.ap, in_=lftm.ap, func=AF.Ln, bias=1.0),
             reads=[lftm.buf], writes=[lftm.buf])
        lf2 = lftm.ap.rearrange("p c g -> p (c g)")
        for (a, b) in ((0, 288), (288, 576)):
            ps = PSG.next()
            S.op("pe", lambda e, ps=ps, a=a, b=b: e.matmul(ps.ap[:, 0:b - a], lhsT=ones_f.ap, rhs=lf2[:, a:b],
                                                           start=True, stop=True),
                 reads=[ones_f.buf, lftm.buf], writes=[ps.buf])
            S.op("act", lambda e, ps=ps, a=a, b=b: e.activation(
                out=sLa.ap.rearrange("p c g -> p (c g)")[:, a:b], in_=ps.ap[:, 0:b - a], func=AF.Exp, scale=-1.0),
                reads=[ps.buf], writes=[sLa.buf])
        for dr, tri in ((0, TRIF), (1, TRIB)):
            for (a, b) in ((0, 288), (288, 576)):
                ps = PSG.next()
                S.op("pe", lambda e, ps=ps, a=a, b=b, tri=tri: e.matmul(ps.ap[:, 0:b - a], lhsT=tri, rhs=lf2[:, a:b],
                                                                        start=True, stop=True),
                     reads=[cst.buf, lftm.buf], writes=[ps.buf])
                c0 = a // 32
                S.op("dve", lambda e, ps=ps, dr=dr, c0=c0: e.tensor_tensor(
                    out=rtm.ap[:, c0:c0 + 9, dr, :],
                    in0=ps.ap[:, 0:288].rearrange("p (c g) -> p c g", g=32)[:, :, dr * 16 + 8:dr * 16 + 16],
                    in1=Gtm.ap[:, c0:c0 + 9, dr * 16:dr * 16 + 8], op=ALU.add),
                    reads=[ps.buf, Gtm.buf], writes=[rtm.buf])
        S.op("act", lambda e: e.activation(out=rtm.ap, in_=rtm.ap, func=AF.Exp), reads=[rtm.buf], writes=[rtm.buf])

        order_b = [1, 0] + list(range(NTILE - 1, 1, -1))
        masks = (TRIF, TRIB)
        for h in range(8):
            wq = [load_w(w_in[l, :, O_MQ + h * 256 + j * 128:O_MQ + h * 256 + (j + 1) * 128]) for j in range(2)]
            wk = [load_w(w_in[l, :, O_MK + h * 256 + j * 128:O_MK + h * 256 + (j + 1) * 128]) for j in range(2)]
            for j in range(2):
                for ti, (t0, w) in enumerate(TT):
                    ps = PSG.next()
                    proj_fm(wq[j], 128, ti, ps)
                    evac(qT.ap[:, j, t0:t0 + w], qT.buf, ps, ps.ap[:, 0:w])
                    ps = PSG.next()
                    proj_fm(wk[j], 128, ti, ps)
                    evac(kT.ap[:, j, t0:t0 + w], kT.buf, ps, ps.ap[:, 0:w], scale=0.0625)
            for c in range(NTILE):
                ps = PSG.next()

                def f(e, c=c, ps=ps, wk=wk):
                    for j in range(2):
                        for kc in range(NCH):
                            ins = e.matmul(ps.ap[:, j * 128:(j + 1) * 128], lhsT=hxT[:, kc, c * 128:(c + 1) * 128],
                                           rhs=wk[j].ap[:, kc, :], start=(kc == 0), stop=(kc == NCH - 1))
                    return ins
                S.op("pe", f, reads=[wk[0].buf, wk[1].buf] + hx_all, writes=[ps.buf])
                evac(ktm.ap[:, c, :], ktm.buf, ps, ps.ap[:, 0:256], scale=0.0625)
            wv = [load_w(w_in[l, :, O_MV + h * 256 + j * 128:O_MV + h * 256 + (j + 1) * 128]) for j in range(2)]
            for c in range(NTILE):
                ps = PSG.next()

                def f(e, c=c, ps=ps, wv=wv):
                    for j in range(2):
                        for kc in range(NCH):
                            ins = e.matmul(ps.ap[:, j * 128:(j + 1) * 128], lhsT=hxT[:, kc, c * 128:(c + 1) * 128],
                                           rhs=wv[j].ap[:, kc, :], start=(kc == 0), stop=(kc == NCH - 1))
                    return ins
                S.op("pe", f, reads=[wv[0].buf, wv[1].buf] + hx_all, writes=[ps.buf])
                evac(vtm.ap[:, c, :], vtm.buf, ps, ps.ap[:, 0:256])
            def dmm(info, dr):
                kp, c = info["kp"], info["c"]
                pb0, pb1, pst = PSS[dr]

                def fD(e, kp=kp, c=c, pst=pst):
                    for j in range(2):
                        e.matmul(pst[:, j, 0:256], lhsT=kp.ap[:, j * 128:(j + 1) * 128], rhs=vtm.ap[:, c, :],
                                 start=True, stop=True)
                        ins = e.matmul(pst[:, j, 256:384], lhsT=kp.ap[:, j * 128:(j + 1) * 128], rhs=ones_b.ap,
                                       start=True, stop=True)
                    return ins
                S.op("pe", fD, reads=[kp.buf, vtm.buf, ones_b.buf], writes=[pb0.buf, pb1.buf])

            def P1(step, dr):
                c = step if dr == 0 else order_b[step]
                cs = slice(c * 128, (c + 1) * 128)
                fcol = dr * 16 + 8 + h
                rcol = rtm.ap[:, c, dr, h:h + 1]
                rh = rhc.next()
                S.op("act", lambda e: e.activation(out=rh.ap, in_=masks[dr], func=AF.Identity,
                                                   scale=lftm.ap[:, c, fcol:fcol + 1]),
                     reads=[cst.buf, lftm.buf], writes=[rh.buf])
                S.op("pe", lambda e: e.matmul(PSE.ap[:, 0:128], lhsT=ones_f.ap, rhs=rh.ap, start=True, stop=True),
                     reads=[ones_f.buf, rh.buf], writes=[PSE.buf])
                Ec = Ecs[dr].next()
                S.op("act", lambda e: e.activation(out=Ec.ap, in_=PSE.ap[:, 0:128], func=AF.Exp, scale=-1.0),
                     reads=[PSE.buf], writes=[Ec.buf])
                qs = qss[dr].next()
                S.op("dve", lambda e: e.tensor_tensor(out=qs.ap, in0=qT.ap[:, :, cs],
                                                      in1=Ec.ap.unsqueeze(1).broadcast_to([128, 2, 128]), op=ALU.mult),
                     reads=[qT.buf, Ec.buf], writes=[qs.buf])
                pa = PSA.next()

                def fS(e):
                    for j in range(2):
                        ins = e.matmul(pa.ap[:, 0:128], lhsT=kT.ap[:, j, cs], rhs=qs.ap[:, j, :],
                                       start=(j == 0), stop=(j == 1))
                    return ins
                S.op("pe", fS, reads=[kT.buf, qs.buf], writes=[pa.buf])
                PT = PTs.next()
                S.op("dve", lambda e: e.scalar_tensor_tensor(out=PT.ap, in0=pa.ap[:, 0:128], scalar=rcol, in1=masks[dr],
                                                             op0=ALU.mult, op1=ALU.mult),
                     reads=[pa.buf, rtm.buf, cst.buf], writes=[PT.buf])
                info = dict(c=c, cs=cs, fcol=fcol, qs=qs, pa=pa, PT=PT)
                if step < NTILE - 1:
                    kp = kps.next()
                    S.op("act", lambda e: e.activation(out=kp.ap, in_=ktm.ap[:, c, :], func=AF.Identity, scale=rcol),
                         reads=[ktm.buf, rtm.buf], writes=[kp.buf])
                    info["kp"] = kp
                return info

            def P2(step, dr, info, nxt_info):
                c, cs, fcol, qs, pa, PT = (info[k] for k in ("c", "cs", "fcol", "qs", "pa", "PT"))
                first = (step == 0)
                stb = STb[dr]

                def fU(e):
                    for ec in range(2):
                        o = pa.ap[:, 128 + ec * 128:256 + ec * 128]
                        ins = e.matmul(o, lhsT=vtm.ap[:, c, ec * 128:(ec + 1) * 128], rhs=PT.ap, start=True, stop=first)
                        if not first:
                            for j in range(2):
                                ins = e.matmul(o, lhsT=stb.ap[:, j, ec * 128:(ec + 1) * 128], rhs=qs.ap[:, j, :],
                                               start=False, stop=(j == 1))
                    o = pa.ap[:, 384:512]
                    ins = e.matmul(o, lhsT=ones_b.ap, rhs=PT.ap, start=True, stop=first)
                    if not first:
                        for j in range(2):
                            ins = e.matmul(o, lhsT=stb.ap[:, j, 256:384], rhs=qs.ap[:, j, :], start=False, stop=(j == 1))
                    return ins
                S.op("pe", fU, reads=[vtm.buf, PT.buf, stb.buf, qs.buf, ones_b.buf], writes=[pa.buf])
                if step < NTILE - 1:
                    pb0, pb1, pst = PSS[dr]
                    sl = sLa.ap[:, c, fcol:fcol + 1]
                    st32 = ST32[dr]
                    if first:
                        S.op("dve", lambda e: e.tensor_scalar(out=stb.ap, in0=pst[:, :, 0:384], scalar1=sl, scalar2=None,
                                                              op0=ALU.mult),
                             reads=[pb0.buf, pb1.buf, sLa.buf], writes=[stb.buf])
                        S.op("act", lambda e: e.activation(out=st32.ap, in_=pst[:, :, 0:384], func=AF.Identity, scale=sl),
                             reads=[pb0.buf, pb1.buf, sLa.buf], writes=[st32.buf])
                    else:
                        tc_ = tmpC.next()
                        S.op("dve", lambda e: e.tensor_tensor(out=tc_.ap, in0=pst[:, :, 0:384], in1=st32.ap, op=ALU.add),
                             reads=[pb0.buf, pb1.buf, st32.buf], writes=[tc_.buf])
                        S.op("dve", lambda e: e.tensor_scalar(out=stb.ap, in0=tc_.ap, scalar1=sl, scalar2=None,
                                                              op0=ALU.mult),
                             reads=[tc_.buf, sLa.buf], writes=[stb.buf])
                        S.op("act", lambda e: e.activation(out=st32.ap, in_=tc_.ap, func=AF.Identity, scale=sl),
                             reads=[tc_.buf, sLa.buf], writes=[st32.buf])
                    if nxt_info is not None and "kp" in nxt_info:
                        dmm(nxt_info, dr)
                am = t128.next()
                S.op("dve", lambda e: e.tensor_scalar(out=am.ap, in0=pa.ap[:, 384:512], scalar1=-1.0, scalar2=1.0,
                                                      op0=ALU.mult, op1=ALU.max),
                     reads=[pa.buf], writes=[am.buf])
                S.op("dve", lambda e: e.scalar_tensor_tensor(out=am.ap, in0=pa.ap[:, 384:512], scalar=1.0, in1=am.ap,
                                                             op0=ALU.max, op1=ALU.max),
                     reads=[pa.buf, am.buf], writes=[am.buf])
                S.op("act", lambda e: e.activation(out=am.ap, in_=am.ap, func=AF.Ln), reads=[am.buf], writes=[am.buf])
                S.op("act", lambda e: e.activation(out=am.ap, in_=am.ap, func=AF.Exp, scale=-1.0),
                     reads=[am.buf], writes=[am.buf])
                hd = hD[dr]
                S.op("dve", lambda e: e.tensor_tensor(
                    out=hd.ap[:, :, cs], in0=pa.ap[:, 128:384].rearrange("p (a b) -> p a b", a=2),
                    in1=am.ap.unsqueeze(1).broadcast_to([128, 2, 128]), op=ALU.mult),
                    reads=[pa.buf, am.buf], writes=[hd.buf])

            cur = [P1(0, 0), P1(0, 1)]
            for dr in range(2):
                dmm(cur[dr], dr)
            for step in range(NTILE):
                nxt = [P1(step + 1, 0), P1(step + 1, 1)] if step + 1 < NTILE else [None, None]
                for dr in range(2):
                    P2(step, dr, cur[dr], nxt[dr])
                cur = nxt
            wo = [load_w(w_in[l, :, O_MO + h * 256 + j * 128:O_MO + h * 256 + (j + 1) * 128]) for j in range(2)]
            wz = [load_w(w_in[l, :, O_MZ + h * 256 + j * 128:O_MZ + h * 256 + (j + 1) * 128]) for j in range(2)]
            for mt in range(T // 256):
                t0 = mt * 256
                hg = hg256.next()
                pss = PSE
                for ec in range(2):
                    ps = PSA.next()
                    proj_fm_rng(wo[ec], 128, t0, 256, ps)
                    so = e256.next()
                    S.op("act", lambda e, so=so, ps=ps: e.activation(out=so.ap, in_=ps.ap[:, 0:256], func=AF.Sigmoid),
                         reads=[ps.buf], writes=[so.buf])
                    hsum = e256.next()
                    S.op("dve", lambda e, hsum=hsum, ec=ec, t0=t0: e.tensor_tensor(
                        out=hsum.ap, in0=hD[0].ap[:, ec, t0:t0 + 256], in1=hD[1].ap[:, ec, t0:t0 + 256], op=ALU.add),
                        reads=[hD[0].buf, hD[1].buf], writes=[hsum.buf])
                    S.op("dve", lambda e, hsum=hsum, so=so, hg=hg, ec=ec: e.tensor_tensor(
                        out=hg.ap[:, ec, :], in0=hsum.ap, in1=so.ap, op=ALU.mult),
                        reads=[hsum.buf, so.buf], writes=[hg.buf])
                    sq = sq256.next()
                    S.op("act", lambda e, sq=sq, hg=hg, ec=ec: e.activation(out=sq.ap, in_=hg.ap[:, ec, :], func=AF.Square),
                         reads=[hg.buf], writes=[sq.buf])
                    S.op("pe", lambda e, sq=sq, pss=pss, ec=ec: e.matmul(pss.ap[:, 0:256], lhsT=ones_b.ap, rhs=sq.ap,
                                                                        start=(ec == 0), stop=(ec == 1)),
                         reads=[sq.buf, ones_b.buf], writes=[pss.buf])
                rs = e256.next()
                S.op("act", lambda e, rs=rs, pss=pss: e.activation(out=rs.ap, in_=pss.ap[:, 0:256], func=AF.Ln,
                                                                  scale=1.0 / 256, bias=EPS),
                     reads=[pss.buf], writes=[rs.buf])
                S.op("act", lambda e, rs=rs: e.activation(out=rs.ap, in_=rs.ap, func=AF.Exp, scale=-0.5),
                     reads=[rs.buf], writes=[rs.buf])
                for ec in range(2):
                    ps = PSA.next()
                    proj_fm_rng(wz[ec], 128, t0, 256, ps)
                    sz = e256.next()
                    S.op("act", lambda e, sz=sz, ps=ps: e.activation(out=sz.ap, in_=ps.ap[:, 0:256], func=AF.Silu),
                         reads=[ps.buf], writes=[sz.buf])
                    S.op("dve", lambda e, hg=hg, ec=ec, rs=rs: e.tensor_tensor(out=hg.ap[:, ec, :], in0=hg.ap[:, ec, :],
                                                                              in1=rs.ap, op=ALU.mult),
                         reads=[hg.buf, rs.buf], writes=[hg.buf])
                    yb = yb256.next()
                    S.op("dve", lambda e, hg=hg, ec=ec, sz=sz, yb=yb, h=h: e.scalar_tensor_tensor(
                        out=yb.ap, in0=hg.ap[:, ec, :], scalar=vcol(l, V_MLN + h * 2 + ec), in1=sz.ap,
                        op0=ALU.mult, op1=ALU.mult),
                        reads=[hg.buf, sz.buf, vecs.buf], writes=[yb.buf])
                    r0 = h * 256 + ec * 128
                    S.dma("sp", lambda e, yb=yb, r0=r0, t0=t0: e.dma_start(out=Yd[1, r0:r0 + 128, t0:t0 + 256], in_=yb.ap),
                          reads=[yb.buf], writes=[Y_b[1]])

    def phase_att(l):
        AR.reset()
        PSG = Rot(PSB[0:2])
        PSSc = Rot(PSB[2:5])
        PSO = Rot(PSB[5:7])
        PSD = Rot(PSB[7:8])
        KT = AR.bf16(4, T)
        Vtm = AR.bf16(NTILE, 512)
        cosr = Rot([AR.f32(512) for _ in range(2)])
        sinr = Rot([AR.f32(512) for _ in range(2)])
        f512 = Rot([AR.f32(512) for _ in range(8)])
        szs = Rot([AR.f32(512) for _ in range(3)])
        rcs = Rot([AR.f32(512) for _ in range(2)])
        b512 = Rot([AR.bf16(512) for _ in range(5)])
        qTs = Rot([AR.bf16(512) for _ in range(2)])
        PTs = Rot([AR.bf16(512) for _ in range(4)])
        ybs = Rot([AR.bf16(512) for _ in range(2)])
        SC = 128.0 ** -0.5

        def load_cs(ti):
            t0, w = TT[ti]
            ct = cosr.next()
            st = sinr.next()
            S.dma("sp", lambda e: e.dma_start(out=ct.ap, in_=cos_d[:, t0 - TCX:t0 - TCX + w]), writes=[ct.buf])
            S.dma("sp", lambda e: e.dma_start(out=st.ap, in_=sin_d[:, t0 - TCX:t0 - TCX + w]), writes=[st.buf])
            return ct, st

        def norm_rope_gen(ps, w, gcol, rope, dst_ap, dst_buf, cs):
            q32 = f512.next()
            S.op("dve", lambda e: e.tensor_copy(out=q32.ap[:, 0:w], in_=ps.ap[:, 0:w]),
                 reads=[ps.buf], writes=[q32.buf])
            sq = b512.next()
            S.op("dve", lambda e: e.tensor_tensor(out=sq.ap[:, 0:w], in0=q32.ap[:, 0:w], in1=q32.ap[:, 0:w], op=ALU.mult),
                 reads=[q32.buf], writes=[sq.buf])
            yield
            ps2 = PSG.next()
            S.op("pe", lambda e: e.matmul(ps2.ap[:, 0:w], lhsT=ones_b.ap, rhs=sq.ap[:, 0:w], start=True, stop=True),
                 reads=[ones_b.buf, sq.buf], writes=[ps2.buf])
            rs = f512.next()
            S.op("act", lambda e: e.activation(out=rs.ap[:, 0:w], in_=ps2.ap[:, 0:w], func=AF.Ln, scale=1.0 / 128,
                                               bias=EPS), reads=[ps2.buf], writes=[rs.buf])
            S.op("act", lambda e: e.activation(out=rs.ap[:, 0:w], in_=rs.ap[:, 0:w], func=AF.Exp, scale=-0.5),
                 reads=[rs.buf], writes=[rs.buf])
            if not rope:
                S.op("dve", lambda e: e.scalar_tensor_tensor(out=dst_ap, in0=q32.ap[:, 0:w], scalar=gcol, in1=rs.ap[:, 0:w],
                                                             op0=ALU.mult, op1=ALU.mult),
                     reads=[q32.buf, rs.buf, vecs.buf], writes=[dst_buf])
                return
            qn = b512.next()
            S.op("dve", lambda e: e.scalar_tensor_tensor(out=qn.ap[:, 0:w], in0=q32.ap[:, 0:w], scalar=gcol,
                                                         in1=rs.ap[:, 0:w], op0=ALU.mult, op1=ALU.mult),
                 reads=[q32.buf, rs.buf, vecs.buf], writes=[qn.buf])
            ct, st = cs
            t2 = f512.next()
            S.op("dve", lambda e: e.tensor_tensor(out=t2.ap[:, 0:w], in0=qn.ap[:, 0:w], in1=ct.ap[:, 0:w], op=ALU.mult),
                 reads=[qn.buf, ct.buf], writes=[t2.buf])
            yield
            ps3 = PSG.next()
            S.op("pe", lambda e: e.matmul(ps3.ap[:, 0:w], lhsT=RROT, rhs=qn.ap[:, 0:w], start=True, stop=True),
                 reads=[cstb.buf, qn.buf], writes=[ps3.buf])
            t1 = f512.next()
            S.op("dve", lambda e: e.tensor_tensor(out=t1.ap[:, 0:w], in0=ps3.ap[:, 0:w], in1=st.ap[:, 0:w], op=ALU.mult),
                 reads=[ps3.buf, st.buf], writes=[t1.buf])
            S.op("dve", lambda e: e.tensor_tensor(out=dst_ap, in0=t1.ap[:, 0:w], in1=t2.ap[:, 0:w], op=ALU.add),
                 reads=[t1.buf, t2.buf], writes=[dst_buf])

        def norm_rope(*a):
            for _ in norm_rope_gen(*a):
                pass

        for g in range(4):
            wk = load_w(w_in[l, :, O_AK + g * 128:O_AK + (g + 1) * 128])
            for ti, (t0, w) in enumerate(TT):
                ps = PSG.next()
                proj_fm(wk, 128, ti, ps)
                cs = load_cs(ti) if ti > 0 else None
                norm_rope(ps, w, vcol(l, V_KN), ti > 0, KT.ap[:, g, t0:t0 + w], KT.buf, cs)
        wv = [load_w(w_in[l, :, O_AV + g * 128:O_AV + (g + 1) * 128]) for g in range(4)]
        for c in range(NTILE):
            ps = PSG.next()

            def f(e, c=c, ps=ps):
                for g in range(4):
                    for kc in range(NCH):
                        ins = e.matmul(ps.ap[:, g * 128:(g + 1) * 128], lhsT=hxT[:, kc, c * 128:(c + 1) * 128],
                                       rhs=wv[g].ap[:, kc, :], start=(kc == 0), stop=(kc == NCH - 1))
                return ins
            S.op("pe", f, reads=[w.buf for w in wv] + hx_all, writes=[ps.buf])
            evac(Vtm.ap[:, c, :], Vtm.buf, ps, ps.ap[:, 0:512])
        tiles = [(h, ti) for h in range(16) for ti in range(len(TT))]
        ready = {}
        wcur = {}

        def prologue(h, ti):
            t0, w = TT[ti]
            if ti == 0:
                wcur["q"] = load_w(w_in[l, :, O_AQ + h * 128:O_AQ + (h + 1) * 128])
                wcur["z"] = load_w(w_in[l, :, O_AZ + h * 128:O_AZ + (h + 1) * 128])
            wq, wz = wcur["q"], wcur["z"]
            ps = PSG.next()
            proj_fm(wq, 128, ti, ps)
            qt = qTs.next()
            cs = load_cs(ti) if ti > 0 else None
            psz = PSG.next()
            proj_fm(wz, 128, ti, psz)
            yield
            gen = norm_rope_gen(ps, w, vcol(l, V_QN), ti > 0, qt.ap[:, 0:w], qt.buf, cs)
            next(gen, None)
            sz = szs.next()
            S.op("act", lambda e: e.activation(out=sz.ap[:, 0:w], in_=psz.ap[:, 0:w], func=AF.Exp, scale=-1.0),
                 reads=[psz.buf], writes=[sz.buf])
            S.op("act", lambda e: e.activation(out=sz.ap[:, 0:w], in_=sz.ap[:, 0:w], func=AF.Ln, bias=1.0),
                 reads=[sz.buf], writes=[sz.buf])
            S.op("act", lambda e: e.activation(out=sz.ap[:, 0:w], in_=sz.ap[:, 0:w], func=AF.Exp, scale=-1.0),
                 reads=[sz.buf], writes=[sz.buf])
            S.op("dve", lambda e: e.tensor_tensor(out=sz.ap[:, 0:w], in0=psz.ap[:, 0:w], in1=sz.ap[:, 0:w], op=ALU.mult),
                 reads=[psz.buf, sz.buf], writes=[sz.buf])
            ready[(h, ti)] = (qt, sz)
            yield
            for _ in gen:
                yield

        for _ in prologue(*tiles[0]):
            pass
        for idx, (h, ti) in enumerate(tiles):
            g = h // 4
            t0, w = TT[ti]
            qt, sz = ready.pop((h, ti))
            nxt = prologue(*tiles[idx + 1]) if idx + 1 < len(tiles) else None
            keys = list(range(NTILE)) if ti > 0 else [0, 1]
            nk = len(keys)
            pO = PSO.next()
            pD = PSD.next()
            pSs = {}

            def emit_S(ki):
                pS = PSSc.next()
                kc_ = keys[ki]
                S.op("pe", lambda e, pS=pS, kc_=kc_, qt=qt, w=w, g=g: e.matmul(
                    pS.ap[:, 0:w], lhsT=KT.ap[:, g, kc_ * 128:(kc_ + 1) * 128], rhs=qt.ap[:, 0:w],
                    start=True, stop=True), reads=[KT.buf, qt.buf], writes=[pS.buf])
                pSs[ki] = pS
            emit_S(0)
            if nk > 1:
                emit_S(1)
            for ki, kc_ in enumerate(keys):
                pS = pSs.pop(ki)
                PT = PTs.next()
                S.op("act", lambda e, pS=pS, PT=PT, w=w: e.activation(out=PT.ap[:, 0:w], in_=pS.ap[:, 0:w],
                                                                     func=AF.Exp, scale=SC),
                     reads=[pS.buf], writes=[PT.buf])
                if ki + 2 < nk:
                    emit_S(ki + 2)
                fst = (ki == 0)
                lst = (ki == nk - 1)

                def fO(e, PT=PT, kc_=kc_, w=w, fst=fst, lst=lst, pO=pO, pD=pD, g=g):
                    e.matmul(pO.ap[:, 0:w], lhsT=Vtm.ap[:, kc_, g * 128:(g + 1) * 128], rhs=PT.ap[:, 0:w],
                             start=fst, stop=lst)
                    return e.matmul(pD.ap[:, 0:w], lhsT=ones_b.ap, rhs=PT.ap[:, 0:w], start=fst, stop=lst)
                S.op("pe", fO, reads=[Vtm.buf, PT.buf, ones_b.buf], writes=[pO.buf, pD.buf])
                if nxt is not None and ki in (1, 4, 8, 12):
                    next(nxt, None)
            if nxt is not None:
                for _ in nxt:
                    pass
            rc = rcs.next()
            S.op("act", lambda e, rc=rc, pD=pD, w=w: e.activation(out=rc.ap[:, 0:w], in_=pD.ap[:, 0:w], func=AF.Ln),
                 reads=[pD.buf], writes=[rc.buf])
            S.op("act", lambda e, rc=rc, w=w: e.activation(out=rc.ap[:, 0:w], in_=rc.ap[:, 0:w], func=AF.Exp, scale=-1.0),
                 reads=[rc.buf], writes=[rc.buf])
            S.op("dve", lambda e, rc=rc, sz=sz, w=w: e.tensor_tensor(out=rc.ap[:, 0:w], in0=rc.ap[:, 0:w],
                                                                    in1=sz.ap[:, 0:w], op=ALU.mult),
                 reads=[rc.buf, sz.buf], writes=[rc.buf])
            yb = ybs.next()
            S.op("dve", lambda e, yb=yb, pO=pO, rc=rc, w=w: e.tensor_tensor(out=yb.ap[:, 0:w], in0=pO.ap[:, 0:w],
                                                                           in1=rc.ap[:, 0:w], op=ALU.mult),
                 reads=[pO.buf, rc.buf], writes=[yb.buf])
            S.dma("sp", lambda e, yb=yb, h=h, t0=t0, w=w: e.dma_start(out=Yd[2, h * 128:(h + 1) * 128, t0:t0 + w],
                                                                     in_=yb.ap[:, 0:w]),
                  reads=[yb.buf], writes=[Y_b[2]])

    def phase_merge(l, last):
        PSG = Rot(PSB)
        for n in range(3 if "nom1" not in phases else 0):
            S.barrier()
            AR.reset()
            Yt = AR.bf16(NCH, T)
            sgs = Rot([AR.f32(512) for _ in range(3)])
            pbs = Rot([AR.bf16(512) for _ in range(3)])
            for kc in range(NCH):
                S.dma("sp", lambda e, n=n, Yt=Yt, kc=kc: e.dma_start(out=Yt.ap[:, kc, :],
                                                                    in_=Yd[n, kc * 128:(kc + 1) * 128, :]),
                      reads=[Y_b[n]], writes=[Yt.buf])
            for dj in range(NCH):
                wb = load_w(w_br[l, n, :, dj * 128:(dj + 1) * 128])
                wg = load_w(w_in[l, :, O_G + n * D + dj * 128:O_G + n * D + (dj + 1) * 128])
                for ti, (t0, w) in enumerate(TT):
                    psP = PSG.next()

                    def f(e, psP=psP, wb=wb, t0=t0, w=w, Yt=Yt):
                        for kc in range(NCH):
                            ins = e.matmul(psP.ap[:, 0:w], lhsT=wb.ap[:, kc, :], rhs=Yt.ap[:, kc, t0:t0 + w],
                                           start=(kc == 0), stop=(kc == NCH - 1))
                        return ins
                    S.op("pe", f, reads=[wb.buf, Yt.buf], writes=[psP.buf])
                    psG = PSG.next()
                    proj_fm(wg, 128, ti, psG)
                    sg = sgs.next()
                    S.op("act", lambda e, sg=sg, psG=psG, w=w: e.activation(out=sg.ap[:, 0:w], in_=psG.ap[:, 0:w],
                                                                           func=AF.Sigmoid),
                         reads=[psG.buf], writes=[sg.buf])
                    pb = pbs.next()
                    S.op("dve", lambda e, pb=pb, psP=psP, sg=sg, w=w: e.tensor_tensor(out=pb.ap[:, 0:w], in0=psP.ap[:, 0:w],
                                                                                     in1=sg.ap[:, 0:w], op=ALU.mult),
                         reads=[psP.buf, sg.buf], writes=[pb.buf])
                    for mt in range(t0 // 256, (t0 + w) // 256):
                        o = mt * 256 - t0
                        S.dma("sp", lambda e, pb=pb, n=n, dj=dj, mt=mt, o=o: e.dma_start(
                            out=Gd[n, mt, :, dj * 256:(dj + 1) * 256], in_=pb.ap[:, o:o + 256]),
                            reads=[pb.buf], writes=[G_b[n]])
        if "nom2" in phases:
            return
        S.barrier()
        AR.reset()
        wout = hxT[:, :, 0:D]
        wo_b = Buf("wout")
        for dj in range(NCH):
            S.dma("pool", lambda e, dj=dj: e.dma_start(
                out=wout[:, :, dj * 128:(dj + 1) * 128],
                in_=w_out[l, :, dj * 128:(dj + 1) * 128].rearrange("(k p) c -> p k c", p=128)), writes=[wo_b])
        Gts = [Rot([AR.bf16(NCH, 256) for _ in range(k_)]) for k_ in (2, 1, 1)]
        yos = Rot([AR.f32(NCH, 256) for _ in range(2)])
        xts = Rot([AR.f32(NCH, 256) for _ in range(2)])
        sqs = Rot([AR.bf16(256) for _ in range(4)])
        rsds = Rot([AR.f32(256) for _ in range(2)])
        tmps = Rot([AR.f32(256) for _ in range(2)])
        PSG7 = Rot(PSB[0:7])
        mts = range(T // 256) if not last else range(1, T // 256)
        for mt in mts:
            t0 = mt * 256
            i = 1 if mt == 0 else 0
            gt = []
            for n in range(3):
                g_ = Gts[n].next()
                S.dma("sp", lambda e, g_=g_, n=n, mt=mt: e.dma_start(
                    out=g_.ap, in_=Gd[n, mt].rearrange("p (k t) -> p k t", k=NCH)),
                    reads=[G_b[n]], writes=[g_.buf])
                gt.append(g_)
            xt = xts.next()
            if l == 0:
                S.dma("sp", lambda e, xt=xt, t0=t0: e.dma_start(
                    out=xt.ap, in_=xT0[:, t0:t0 + 256].rearrange("(k p) t -> p k t", p=128)), writes=[xt.buf])
            else:
                S.dma("sp", lambda e, xt=xt, mt=mt: e.dma_start(
                    out=xt.ap, in_=xres[mt].rearrange("p (k t) -> p k t", k=NCH)), reads=[xres_b[mt]], writes=[xt.buf])
            S.op("dve", lambda e, gt=gt: e.tensor_tensor(out=gt[0].ap, in0=gt[0].ap, in1=gt[1].ap, op=ALU.add),
                 reads=[gt[0].buf, gt[1].buf], writes=[gt[0].buf])
            S.op("dve", lambda e, gt=gt: e.tensor_tensor(out=gt[0].ap, in0=gt[0].ap, in1=gt[2].ap, op=ALU.add),
                 reads=[gt[0].buf, gt[2].buf], writes=[gt[0].buf])
            mg = gt[0]
            yo = yos.next()
            rsd = rsds.next()
            pss = PSB[7]
            pend = []

            def emit_ones(sq, dj, pss=pss):
                S.op("pe", lambda e: e.matmul(pss.ap[:, 0:256], lhsT=ones_b.ap, rhs=sq.ap,
                                              start=(dj == 0), stop=(dj == NCH - 1)),
                     reads=[sq.buf, ones_b.buf], writes=[pss.buf])
            PSG = PSG7
            for dj in range(NCH):
                ps = PSG.next()

                def f(e, ps=ps, dj=dj, mg=mg):
                    for kc in range(NCH):
                        ins = e.matmul(ps.ap[:, 0:256], lhsT=wout[:, kc, dj * 128:(dj + 1) * 128], rhs=mg.ap[:, kc, :],
                                       start=(kc == 0), stop=(kc == NCH - 1))
                    return ins
                S.op("pe", f, reads=[wo_b, mg.buf], writes=[ps.buf])
                S.op("dve", lambda e, ps=ps, dj=dj, yo=yo: e.tensor_copy(out=yo.ap[:, dj, :], in_=ps.ap[:, 0:256]),
                     reads=[ps.buf], writes=[yo.buf])
                sq = sqs.next()
                S.op("act", lambda e, ps=ps, sq=sq: e.activation(out=sq.ap, in_=ps.ap[:, 0:256], func=AF.Square),
                     reads=[ps.buf], writes=[sq.buf])
                pend.append((sq, dj))
                if len(pend) > 2:
                    emit_ones(*pend.pop(0))
            while pend:
                emit_ones(*pend.pop(0))
            S.op("act", lambda e, pss=pss, rsd=rsd: e.activation(out=rsd.ap, in_=pss.ap[:, 0:256], func=AF.Ln,
                                                                scale=1.0 / D, bias=EPS),
                 reads=[pss.buf], writes=[rsd.buf])
            S.op("act", lambda e, rsd=rsd: e.activation(out=rsd.ap, in_=rsd.ap, func=AF.Exp, scale=-0.5),
                 reads=[rsd.buf], writes=[rsd.buf])
            for dj in range(NCH):
                tp = tmps.next()
                S.op("dve", lambda e, tp=tp, dj=dj, yo=yo, rsd=rsd: e.tensor_tensor(out=tp.ap, in0=yo.ap[:, dj, :],
                                                                                  in1=rsd.ap, op=ALU.mult),
                     reads=[yo.buf, rsd.buf], writes=[tp.buf])
                S.op("dve", lambda e, tp=tp, dj=dj, xt=xt, i=i: e.scalar_tensor_tensor(
                    out=xt.ap[:, dj, :], in0=tp.ap, scalar=Gmod.ap[:, dj, i:i + 1], in1=xt.ap[:, dj, :],
                    op0=ALU.mult, op1=ALU.add), reads=[tp.buf, Gmod.buf, xt.buf], writes=[xt.buf])
            if last:
                S.dma("sp", lambda e, xt=xt, t0=t0: e.dma_start(
                    out=out_d[:, t0 - TCX:t0 - TCX + 256].rearrange("(k p) t -> p k t", p=128), in_=xt.ap),
                    reads=[xt.buf], writes=[out_b])
            else:
                S.dma("sp", lambda e, xt=xt, mt=mt: e.dma_start(
                    out=xres[mt].rearrange("p (k t) -> p k t", k=NCH), in_=xt.ap),
                    reads=[xt.buf], writes=[xres_b[mt]])

    for l in range(n_layers):
        last = (l == n_layers - 1) and not debug
        S.barrier()
        phase0(l)
        S.barrier()
        phaseA(l)
        if "lru" in phases:
            S.barrier()
            phase_lru(l)
        if "ml" in phases:
            S.barrier()
            phase_ml(l)
        if "att" in phases:
            S.barrier()
            phase_att(l)
        if "merge" in phases:
            phase_merge(l, last)
    S.barrier()
    S.finalize()
    return nc


def _fm(v):
    v = np.asarray(v, np.float32)
    lead = v.shape[:-1]
    r = v.reshape(lead + (16, 128))
    return np.moveaxis(r, -1, 0)


def _host_consts():
    s = np.arange(128)
    trif = (s[:, None] <= s[None, :]).astype(np.float32)
    trib = (s[:, None] >= s[None, :]).astype(np.float32)
    R = np.zeros((128, 128), np.float32)
    for m in range(64):
        R[m + 64, m] = -1.0
        R[m, m + 64] = 1.0
    ident = np.eye(128, dtype=np.float32)
    cst = np.concatenate([trif, trib, R, ident], axis=1)
    n_freq = 32
    inv = (1.0 / (np.float32(10000.0) ** (np.arange(n_freq, dtype=np.float32) / np.float32(n_freq)))).astype(np.float32)
    row = np.repeat(np.arange(TLAT // 64), 64).astype(np.float32)
    col = np.tile(np.arange(64), TLAT // 64).astype(np.float32)
    ang = np.concatenate([row[:, None] * inv, col[:, None] * inv], axis=-1).astype(np.float32)
    cos = np.cos(ang).astype(np.float32)
    sin = np.sin(ang).astype(np.float32)
    cosT = np.ascontiguousarray(np.concatenate([cos, cos], axis=1).T)
    sinT = np.ascontiguousarray(np.concatenate([sin, sin], axis=1).T)
    return cst, cosT, sinT


def make_in_maps(inputs, cores):
    f32 = np.float32
    L = 4
    vecs = np.zeros((128, L, NV), f32)
    vecs[:, :, V_ADAB:V_ADAB + 48] = np.moveaxis(np.asarray(inputs["ada_b"], f32).reshape(L, 48, 128), -1, 0)
    vecs[:, :, V_NPRE:V_NPRE + 16] = _fm(inputs["norm_pre"])
    vecs[:, :, V_NPOST:V_NPOST + 16] = _fm(inputs["norm_post"])
    cw = _fm(inputs["lru_conv_w"])
    vecs[:, :, V_CW:V_CW + 64] = np.swapaxes(cw, 2, 3).reshape(128, L, 64)
    vecs[:, :, V_CB:V_CB + 16] = _fm(inputs["lru_conv_b"])
    vecs[:, :, V_BR:V_BR + 32] = _fm(inputs["lru_br"]).reshape(128, L, 32)
    vecs[:, :, V_BI:V_BI + 32] = _fm(inputs["lru_bi"]).reshape(128, L, 32)
    vecs[:, :, V_LAM:V_LAM + 32] = _fm(inputs["lru_lam"]).reshape(128, L, 32)
    vecs[:, :, V_MLN:V_MLN + 16] = _fm(inputs["ml_norm"])
    vecs[:, :, V_QN] = np.asarray(inputs["q_norm"], f32).T
    vecs[:, :, V_KN] = np.asarray(inputs["k_norm"], f32).T
    vecs = np.ascontiguousarray(vecs.reshape(128, L * NV))
    gb = np.asarray(inputs["ml_gate_b"], f32).reshape(L, 32)
    gbB = np.ascontiguousarray(np.broadcast_to(gb.reshape(1, L * 32), (128, L * 32)))
    cst, cosT, sinT = _host_consts()
    shared = {
        "ada_w": np.ascontiguousarray(inputs["ada_w"], dtype=f32),
        "w_in": np.ascontiguousarray(inputs["w_in"], dtype=f32),
        "w_br": np.ascontiguousarray(inputs["w_br"], dtype=f32),
        "w_out": np.ascontiguousarray(inputs["w_out"], dtype=f32),
        "lru_wr": np.ascontiguousarray(inputs["lru_wr"], dtype=f32),
        "lru_wi": np.ascontiguousarray(inputs["lru_wi"], dtype=f32),
        "vecs": vecs, "gbB": gbB, "cst": cst, "cosT": cosT, "sinT": sinT,
    }
    maps = []
    cc = np.asarray(inputs["c_ctx"], f32)
    for b in cores:
        xT = np.ascontiguousarray(np.concatenate([np.asarray(inputs["ctx"][b], f32), np.asarray(inputs["x"][b], f32)],
                                                 axis=0).T)
        cT = np.stack([np.asarray(inputs["c"][b], f32), cc], axis=-1)
        cT = np.ascontiguousarray(np.moveaxis(cT.reshape(16, 128, 2), 1, 0).reshape(128, 32))
        m = dict(shared)
        m["xT0"] = xT
        m["cT"] = cT
        maps.append(m)
    return maps


_NC_CACHE = {}


def kernel(**inputs):
    if "full" not in _NC_CACHE:
        _NC_CACHE["full"] = build(4, False)
    nc = _NC_CACHE["full"]
    B = 4
    maps = make_in_maps(inputs, list(range(B)))
    res = run_bass_kernel_spmd(nc, maps, core_ids=list(range(B)))
    out = np.stack([np.ascontiguousarray(res.results[b]["outT"].T) for b in range(B)], axis=0)
    return out.astype(np.float32)
```

```python
import numpy as np
import ml_dtypes
import concourse.bass as bass
import concourse.mybir as mybir
from concourse.bass_utils import run_bass_kernel_spmd

F32 = mybir.dt.float32
BF16 = mybir.dt.bfloat16
AF = mybir.ActivationFunctionType
ALU = mybir.AluOpType

D = 2048
NCH = 16
TCX = 256
TLAT = 2048
T = TCX + TLAT
NTILE = T // 128
NIN = 25632
EPS = 1e-6
TT = [(0, 256), (256, 512), (768, 512), (1280, 512), (1792, 512)]
O_LX, O_LZ, O_MQ, O_MK, O_MV, O_MO, O_MZ, O_MG, O_AQ, O_AK, O_AV, O_AZ, O_G = (
    0, 2048, 4096, 6144, 8192, 10240, 12288, 14336, 14368, 16416, 16928, 17440, 19488)
V_ADAB, V_NPRE, V_NPOST, V_CW, V_CB, V_BR, V_BI, V_LAM, V_MLN, V_QN, V_KN = (
    0, 48, 64, 80, 144, 160, 192, 224, 256, 272, 273)
NV = 274
ARENA_WORDS = 26112


class Buf:
    __slots__ = ("name", "w", "r", "excl")

    def __init__(self, name="", excl=False):
        self.name = name
        self.w = None
        self.r = {}
        self.excl = excl


class Tl:
    __slots__ = ("ap", "buf")

    def __init__(self, ap, buf=None):
        self.ap = ap
        self.buf = buf if buf is not None else Buf()


class Sched:
    COMPUTE = ("pe", "act", "dve", "pool")

    def __init__(self, nc, n_dma_sems=8):
        self.nc = nc
        self.sems = {}
        self.cnt = {}
        self.prog = {e: [] for e in ("pe", "act", "dve", "pool", "sp")}
        for e in self.COMPUTE:
            self.sems[e] = nc.alloc_semaphore("s_" + e)
            self.cnt[e] = 0
        self.dma_rot = {}
        for q in ("sp", "pool"):
            ids = []
            for i in range(n_dma_sems):
                cid = ("dma", q, i)
                self.sems[cid] = nc.alloc_semaphore("d_%s_%d" % (q, i))
                self.cnt[cid] = 0
                ids.append(cid)
            self.dma_rot[q] = [ids, 0]
        self.known = {e: {} for e in self.prog}

    def _deps(self, reads, writes):
        deps = {}
        for b in reads:
            if b.w is not None:
                c, v = b.w
                if deps.get(c, 0) < v:
                    deps[c] = v
        for b in writes:
            if b.w is not None:
                c, v = b.w
                if deps.get(c, 0) < v:
                    deps[c] = v
            for c, v in b.r.items():
                if deps.get(c, 0) < v:
                    deps[c] = v
        return deps

    def _waits(self, eng, deps):
        kn = self.known[eng]
        waits = []
        for c, v in deps.items():
            if c == eng and eng == "pe":
                continue
            if kn.get(c, 0) < v:
                kn[c] = v
                waits.append((self.sems[c], v))
        return waits

    def op(self, eng, fn, reads=(), writes=()):
        ex = [b for b in reads if b.excl]
        if ex:
            writes = list(writes) + ex
        waits = self._waits(eng, self._deps(reads, writes))
        self.cnt[eng] += 1
        val = self.cnt[eng]
        sem = self.sems[eng]

        def run(e, waits=waits, fn=fn, sem=sem):
            for s, v in waits:
                e.wait_ge(s, v)
            fn(e).then_inc(sem, 1)
        self.prog[eng].append(run)
        for b in reads:
            if b.r.get(eng, 0) < val:
                b.r[eng] = val
        for b in writes:
            b.w = (eng, val)
            b.r = {}
        return val

    def dma(self, q, fn, reads=(), writes=()):
        rot = self.dma_rot[q]
        cid = rot[0][rot[1]]
        rot[1] = (rot[1] + 1) % len(rot[0])
        deps = self._deps(reads, writes)
        if self.cnt[cid] > 0:
            deps[cid] = max(deps.get(cid, 0), self.cnt[cid])
        waits = self._waits(q, deps)
        self.cnt[cid] += 16
        val = self.cnt[cid]
        sem = self.sems[cid]

        def run(e, waits=waits, fn=fn, sem=sem):
            for s, v in waits:
                e.wait_ge(s, v)
            fn(e).then_inc(sem, 16)
        self.prog[q].append(run)
        for b in reads:
            b.r[cid] = val
        for b in writes:
            b.w = (cid, val)
            b.r = {}

    def barrier(self):
        deps = {c: v for c, v in self.cnt.items() if v > 0}
        for eng in self.prog:
            waits = self._waits(eng, dict(deps))

            def run(e, waits=waits):
                for s, v in waits:
                    e.wait_ge(s, v)
            self.prog[eng].append(run)

    def finalize(self):
        nc = self.nc
        prog = self.prog
        with nc.Block() as block:
            @block.tensor
            def _(e):
                for f in prog["pe"]:
                    f(e)

            @block.scalar
            def _(e):
                for f in prog["act"]:
                    f(e)

            @block.vector
            def _(e):
                for f in prog["dve"]:
                    f(e)

            @block.gpsimd
            def _(e):
                for f in prog["pool"]:
                    f(e)

            @block.sync
            def _(e):
                for f in prog["sp"]:
                    f(e)


class Rot:
    def __init__(self, items):
        self.items = items
        self.i = 0

    def next(self):
        t = self.items[self.i]
        self.i = (self.i + 1) % len(self.items)
        return t


class Arena:
    def __init__(self, tensor, nwords):
        self.t = tensor
        self.n = nwords
        self.off = 0

    def reset(self):
        self.off = 0

    def _take(self, nw):
        a = self.off
        assert a + nw <= self.n, ("arena overflow", a + nw, self.n)
        self.off += nw
        return self.t[:, a:a + nw]

    def f32(self, *shape):
        n = int(np.prod(shape))
        ap = self._take(n)
        if len(shape) == 2:
            ap = ap.rearrange("p (a b) -> p a b", a=shape[0])
        elif len(shape) == 3:
            ap = ap.rearrange("p (a b c) -> p a b c", a=shape[0], b=shape[1])
        return Tl(ap)

    def bf16(self, *shape):
        n = int(np.prod(shape))
        nw = (n + 1) // 2
        ap = self._take(nw).bitcast(BF16)[:, 0:n]
        if len(shape) == 2:
            ap = ap.rearrange("p (a b) -> p a b", a=shape[0])
        elif len(shape) == 3:
            ap = ap.rearrange("p (a b c) -> p a b c", a=shape[0], b=shape[1])
        return Tl(ap)


def build(n_layers=4, debug=False, phases=("lru", "ml", "att", "merge")):
    nc = bass.Bass("TRN2", target_bir_lowering=False)
    S = Sched(nc)

    def din(name, shape, dt=F32):
        return nc.dram_tensor(name, list(shape), dt, kind="ExternalInput").ap()

    xT0 = din("xT0", [D, T])
    cTd = din("cT", [128, 32])
    ada_w = din("ada_w", [4, D, 6144])
    w_in = din("w_in", [4, D, NIN])
    w_br = din("w_br", [4, 3, D, D])
    w_out = din("w_out", [4, D, D])
    lru_wr = din("lru_wr", [4, 2, 16, 128, 128])
    lru_wi = din("lru_wi", [4, 2, 16, 128, 128])
    vecs_d = din("vecs", [128, 4 * NV])
    gbB_d = din("gbB", [128, 4 * 32])
    cst_d = din("cst", [128, 4 * 128])
    cos_d = din("cosT", [128, TLAT])
    sin_d = din("sinT", [128, TLAT])
    out_d = nc.dram_tensor("outT", [D, TLAT], F32, kind="ExternalOutput").ap()
    ykind = "ExternalOutput" if debug else "Internal"
    xres = nc.dram_tensor("xres", [T // 256, 128, NCH * 256], F32, kind=ykind).ap()
    Yd = nc.dram_tensor("Ybr", [3, D, T], BF16, kind=ykind).ap()
    Gd = nc.dram_tensor("Gbr", [3, T // 256, 128, NCH * 256], BF16, kind=ykind).ap()
    xres_b = [Buf("xres%d" % i) for i in range(9)]
    Y_b = [Buf("Y%d" % i) for i in range(3)]
    G_b = [Buf("G%d" % i) for i in range(3)]
    out_b = Buf("out")

    hxT_t = nc.alloc_sbuf_tensor("hxT", [128, NCH, T], BF16)
    hxT = hxT_t[:]
    hx_b = [Buf("hx%d" % i) for i in range(len(TT))]
    hx_all = hx_b
    WP = Rot([Tl(nc.alloc_sbuf_tensor("wp%d" % i, [128, NCH, 128], BF16)[:]) for i in range(6)])
    arena_t = nc.alloc_sbuf_tensor("arena", [128, ARENA_WORDS], F32)
    AR = Arena(arena_t, ARENA_WORDS)
    vecs = Tl(nc.alloc_sbuf_tensor("vecs_s", [128, 4 * NV], F32)[:])
    gbB = Tl(nc.alloc_sbuf_tensor("gbB_s", [128, 4 * 32], F32)[:])
    cst = Tl(nc.alloc_sbuf_tensor("cst_s", [128, 4 * 128], F32)[:])
    cstb = Tl(nc.alloc_sbuf_tensor("cstb_s", [128, 4 * 128], BF16)[:])
    ones_b = Tl(nc.alloc_sbuf_tensor("ones_b", [128, 128], BF16)[:])
    ones_f = Tl(nc.alloc_sbuf_tensor("ones_f", [128, 128], F32)[:])
    scT = Tl(nc.alloc_sbuf_tensor("scT", [128, 32], F32)[:])
    modT = Tl(nc.alloc_sbuf_tensor("modT", [128, 48, 2], F32)[:])
    Amod = Tl(nc.alloc_sbuf_tensor("Amod", [128, 16, 2], F32)[:])
    Gmod = Tl(nc.alloc_sbuf_tensor("Gmod", [128, 16, 2], F32)[:])
    lamk = Tl(nc.alloc_sbuf_tensor("lamk", [128, 2, 32], F32)[:])
    TRIF = cst.ap[:, 0:128]
    TRIB = cst.ap[:, 128:256]
    RROT = cstb.ap[:, 256:384]

    psum_t = [nc.alloc_psum_tensor("ps%d" % i, [128, 2, 512], F32) for i in range(4)]
    PSB = [Tl(psum_t[i // 2][:, i % 2, :], Buf("psb%d" % i, excl=True)) for i in range(8)]

    def vcol(l, off, n=1):
        return vecs.ap[:, l * NV + off: l * NV + off + n]

    S.dma("sp", lambda e: e.dma_start(out=vecs.ap, in_=vecs_d), writes=[vecs.buf])
    S.dma("sp", lambda e: e.dma_start(out=gbB.ap, in_=gbB_d), writes=[gbB.buf])
    S.dma("sp", lambda e: e.dma_start(out=cst.ap, in_=cst_d), writes=[cst.buf])
    S.dma("pool", lambda e: e.dma_start(out=cstb.ap, in_=cst_d), writes=[cstb.buf])
    S.dma("sp", lambda e: e.dma_start(out=scT.ap, in_=cTd), writes=[scT.buf])
    S.op("dve", lambda e: e.memset(ones_b.ap, 1.0), writes=[ones_b.buf])
    S.op("dve", lambda e: e.memset(ones_f.ap, 1.0), writes=[ones_f.buf])
    S.op("act", lambda e: e.activation(out=scT.ap, in_=scT.ap, func=AF.Silu), reads=[scT.buf], writes=[scT.buf])

    def load_w(src2d, n=128):
        wt = WP.next()
        S.dma("pool", lambda e: e.dma_start(out=wt.ap[:, :, 0:n], in_=src2d.rearrange("(k p) c -> p k c", p=128)),
              writes=[wt.buf])
        return wt

    def proj_fm(wt, n, ti, ps):
        t0, w = TT[ti]

        def f(e):
            for kc in range(NCH):
                ins = e.matmul(ps.ap[0:n, 0:w], lhsT=wt.ap[:, kc, 0:n], rhs=hxT[:, kc, t0:t0 + w],
                               start=(kc == 0), stop=(kc == NCH - 1))
            return ins
        S.op("pe", f, reads=[wt.buf, hx_b[ti]], writes=[ps.buf])

    def proj_fm_rng(wt, n, t0, w, ps):
        def f(e):
            for kc in range(NCH):
                ins = e.matmul(ps.ap[0:n, 0:w], lhsT=wt.ap[:, kc, 0:n], rhs=hxT[:, kc, t0:t0 + w],
                               start=(kc == 0), stop=(kc == NCH - 1))
            return ins
        S.op("pe", f, reads=[wt.buf] + hx_all, writes=[ps.buf])

    evac_flip = [0]

    def evac(out_ap, out_buf, ps, in_ap, scale=None):
        evac_flip[0] ^= 1
        if evac_flip[0]:
            if scale is None:
                S.op("act", lambda e: e.activation(out=out_ap, in_=in_ap, func=AF.Copy), reads=[ps.buf],
                     writes=[out_buf])
            else:
                S.op("act", lambda e: e.activation(out=out_ap, in_=in_ap, func=AF.Copy, scale=scale),
                     reads=[ps.buf], writes=[out_buf])
        else:
            if scale is None:
                S.op("dve", lambda e: e.tensor_copy(out=out_ap, in_=in_ap), reads=[ps.buf], writes=[out_buf])
            else:
                S.op("dve", lambda e: e.tensor_scalar(out=out_ap, in0=in_ap, scalar1=scale, scalar2=None,
                                                      op0=ALU.mult), reads=[ps.buf], writes=[out_buf])

    def phase0(l):
        AR.reset()
        PSG = Rot(PSB)
        wts = [AR.f32(6144) for _ in range(4)]
        acc = AR.f32(96)
        for kc in range(NCH):
            wt = wts[kc % 4]
            S.dma("sp", lambda e, wt=wt, kc=kc: e.dma_start(out=wt.ap, in_=ada_w[l, kc * 128:(kc + 1) * 128, :]),
                  writes=[wt.buf])
            ps = PSG.next()

            def f(e, wt=wt, kc=kc, ps=ps):
                for j in range(48):
                    ins = e.matmul(ps.ap[:, 2 * j:2 * j + 2], lhsT=wt.ap[:, j * 128:(j + 1) * 128],
                                   rhs=scT.ap[:, 2 * kc:2 * kc + 2], start=True, stop=True)
                return ins
            S.op("pe", f, reads=[wt.buf, scT.buf], writes=[ps.buf])
            if kc == 0:
                S.op("dve", lambda e, ps=ps: e.tensor_copy(out=acc.ap, in_=ps.ap[:, 0:96]), reads=[ps.buf],
                     writes=[acc.buf])
            else:
                S.op("dve", lambda e, ps=ps: e.tensor_tensor(out=acc.ap, in0=ps.ap[:, 0:96], in1=acc.ap, op=ALU.add),
                     reads=[ps.buf, acc.buf], writes=[acc.buf])
        accv = acc.ap.rearrange("p (j i) -> p j i", i=2)
        for i in range(2):
            S.op("dve", lambda e, i=i: e.tensor_tensor(out=modT.ap[:, :, i], in0=accv[:, :, i],
                                                       in1=vcol(l, V_ADAB, 48), op=ALU.add),
                 reads=[acc.buf, vecs.buf], writes=[modT.buf])
        for i in range(2):
            S.op("dve", lambda e, i=i: e.scalar_tensor_tensor(out=Amod.ap[:, :, i], in0=modT.ap[:, 16:32, i],
                                                              scalar=1.0, in1=vcol(l, V_NPRE, 16),
                                                              op0=ALU.add, op1=ALU.mult),
                 reads=[modT.buf, vecs.buf], writes=[Amod.buf])
            S.op("dve", lambda e, i=i: e.tensor_tensor(out=Gmod.ap[:, :, i], in0=modT.ap[:, 32:48, i],
                                                       in1=vcol(l, V_NPOST, 16), op=ALU.mult),
                 reads=[modT.buf, vecs.buf], writes=[Gmod.buf])
        tmp = AR.f32(32)
        S.op("act", lambda e: e.activation(out=tmp.ap, in_=vcol(l, V_LAM, 32), func=AF.Exp, scale=-1.0),
             reads=[vecs.buf], writes=[tmp.buf])
        S.op("act", lambda e: e.activation(out=tmp.ap, in_=tmp.ap, func=AF.Ln, bias=1.0),
             reads=[tmp.buf], writes=[tmp.buf])
        S.op("dve", lambda e: e.tensor_scalar(out=lamk.ap[:, 0, :], in0=tmp.ap, scalar1=-8.0, scalar2=None,
                                              op0=ALU.mult), reads=[tmp.buf], writes=[lamk.buf])
        S.op("dve", lambda e: e.tensor_scalar(out=lamk.ap[:, 1, :], in0=tmp.ap, scalar1=-16.0, scalar2=None,
                                              op0=ALU.mult), reads=[tmp.buf], writes=[lamk.buf])

    def phaseA(l):
        AR.reset()
        PSG = Rot(PSB)
        xts_ = Rot([AR.f32(16, 512) for _ in range(2)])
        rstds_ = Rot([AR.f32(512) for _ in range(2)])
        sqs = Rot([AR.bf16(512) for _ in range(3)])
        tmps = Rot([AR.f32(512) for _ in range(3)])
        for ti, (t0, w) in enumerate(TT):
            i = 1 if ti == 0 else 0
            xt = xts_.next()
            rstd = rstds_.next()
            if l == 0:
                S.dma("sp", lambda e, t0=t0, w=w, xt=xt: e.dma_start(out=xt.ap[:, :, 0:w],
                                                             in_=xT0[:, t0:t0 + w].rearrange("(k p) t -> p k t", p=128)),
                      writes=[xt.buf])
            else:
                for mt in range(t0 // 256, (t0 + w) // 256):
                    o = mt * 256 - t0
                    S.dma("sp", lambda e, mt=mt, o=o, xt=xt: e.dma_start(
                        out=xt.ap[:, :, o:o + 256], in_=xres[mt].rearrange("p (k t) -> p k t", k=NCH)),
                        reads=[xres_b[mt]], writes=[xt.buf])
            ps = PSG.next()
            for kc in range(NCH):
                sq = sqs.next()
                S.op("act", lambda e, sq=sq, kc=kc, w=w, xt=xt: e.activation(out=sq.ap[:, 0:w], in_=xt.ap[:, kc, 0:w],
                                                                            func=AF.Square),
                     reads=[xt.buf], writes=[sq.buf])
                S.op("pe", lambda e, sq=sq, kc=kc, w=w, ps=ps: e.matmul(ps.ap[:, 0:w], lhsT=ones_b.ap, rhs=sq.ap[:, 0:w],
                                                                       start=(kc == 0), stop=(kc == NCH - 1)),
                     reads=[sq.buf, ones_b.buf], writes=[ps.buf])
            S.op("act", lambda e, w=w, ps=ps, rstd=rstd: e.activation(out=rstd.ap[:, 0:w], in_=ps.ap[:, 0:w], func=AF.Ln,
                                                                     scale=1.0 / D, bias=EPS),
                 reads=[ps.buf], writes=[rstd.buf])
            S.op("act", lambda e, w=w, rstd=rstd: e.activation(out=rstd.ap[:, 0:w], in_=rstd.ap[:, 0:w], func=AF.Exp,
                                                              scale=-0.5),
                 reads=[rstd.buf], writes=[rstd.buf])
            for kc in range(NCH):
                tp = tmps.next()
                S.op("dve", lambda e, tp=tp, kc=kc, w=w, xt=xt, rstd=rstd: e.tensor_tensor(
                    out=tp.ap[:, 0:w], in0=xt.ap[:, kc, 0:w], in1=rstd.ap[:, 0:w], op=ALU.mult),
                     reads=[xt.buf, rstd.buf], writes=[tp.buf])
                S.op("act", lambda e, tp=tp, kc=kc, w=w, t0=t0, i=i: e.activation(
                    out=hxT[:, kc, t0:t0 + w], in_=tp.ap[:, 0:w], func=AF.Identity,
                    scale=Amod.ap[:, kc, i:i + 1], bias=modT.ap[:, kc, i:i + 1]),
                    reads=[tp.buf, Amod.buf, modT.buf], writes=[hx_b[ti]])

    def phase_lru(l):
        AR.reset()
        PSG = Rot(PSB)
        XP = 2312
        xsets = Rot([(AR.f32(XP), AR.f32(T), AR.bf16(T)) for _ in range(1)])
        aFs = [AR.f32(T), AR.f32(T)]
        bFs = [AR.f32(T), AR.f32(T)]
        tFs = [AR.f32(T), AR.f32(T)]
        hs = AR.f32(T)
        wgs = Rot([AR.bf16(4, 128) for _ in range(3)])
        szs = Rot([AR.f32(512) for _ in range(5)])
        ybs = Rot([AR.bf16(512) for _ in range(2)])
        for (xp_, _, _) in xsets.items:
            S.op("dve", lambda e, xp_=xp_: e.memset(xp_.ap, 0.0), writes=[xp_.buf])
        segs = [(0, 0, TCX), (259, TCX, TLAT)]

        def stage_x(n):
            xpad, xc, xcb = xsets.next()
            wx = load_w(w_in[l, :, O_LX + n * 128:O_LX + (n + 1) * 128])
            wz = load_w(w_in[l, :, O_LZ + n * 128:O_LZ + (n + 1) * 128])
            wg = wgs.next()
            S.dma("pool", lambda e: e.dma_start(out=wg.ap[:, 0:2, :], in_=lru_wr[l, :, n].rearrange("r c d -> c r d")),
                  writes=[wg.buf])
            S.dma("pool", lambda e: e.dma_start(out=wg.ap[:, 2:4, :], in_=lru_wi[l, :, n].rearrange("r c d -> c r d")),
                  writes=[wg.buf])
            for ti, (t0, w) in enumerate(TT):
                ps = PSG.next()
                proj_fm(wx, 128, ti, ps)
                off = 2 + t0 if ti == 0 else 259 + 2 + (t0 - TCX)
                evac(xpad.ap[:, off:off + w], xpad.buf, ps, ps.ap[:, 0:w])
            for (po, t0, ln) in segs:
                S.op("dve", lambda e, po=po, t0=t0, ln=ln: e.tensor_scalar(
                    out=xc.ap[:, t0:t0 + ln], in0=xpad.ap[:, po:po + ln], scalar1=vcol(l, V_CW + n * 4 + 0),
                    scalar2=vcol(l, V_CB + n), op0=ALU.mult, op1=ALU.add),
                    reads=[xpad.buf, vecs.buf], writes=[xc.buf])
                for k in range(1, 4):
                    S.op("dve", lambda e, po=po, t0=t0, ln=ln, k=k: e.scalar_tensor_tensor(
                        out=xc.ap[:, t0:t0 + ln], in0=xpad.ap[:, po + k:po + k + ln], scalar=vcol(l, V_CW + n * 4 + k),
                        in1=xc.ap[:, t0:t0 + ln], op0=ALU.mult, op1=ALU.add),
                        reads=[xpad.buf, vecs.buf, xc.buf], writes=[xc.buf])
            S.op("act", lambda e: e.activation(out=xcb.ap, in_=xc.ap, func=AF.Copy), reads=[xc.buf], writes=[xcb.buf])
            return (xc, xcb, wz, wg)

        def stage_g(n, ctx):
            xc, xcb, wz, wg = ctx
            szt = []
            for ti, (t0, w) in enumerate(TT):
                ps = PSG.next()
                proj_fm(wz, 128, ti, ps)
                sz = szs.next()
                S.op("act", lambda e, sz=sz, ps=ps, w=w: e.activation(out=sz.ap[:, 0:w], in_=ps.ap[:, 0:w], func=AF.Silu),
                     reads=[ps.buf], writes=[sz.buf])
                szt.append(sz)

            def do_dir(dr):
                aF, bF, tF = aFs[dr], bFs[dr], tFs[dr]
                for ti, (t0, w) in enumerate(TT):
                    psr = PSG.next()
                    S.op("pe", lambda e, psr=psr, dr=dr, t0=t0, w=w: e.matmul(
                        psr.ap[:, 0:w], lhsT=wg.ap[:, dr, :], rhs=xcb.ap[:, t0:t0 + w], start=True, stop=True),
                        reads=[wg.buf, xcb.buf], writes=[psr.buf])
                    S.op("act", lambda e, psr=psr, dr=dr, t0=t0, w=w: e.activation(
                        out=aF.ap[:, t0:t0 + w], in_=psr.ap[:, 0:w], func=AF.Sigmoid,
                        bias=vcol(l, V_BR + dr * 16 + n)), reads=[psr.buf, vecs.buf], writes=[aF.buf])
                    psi = PSG.next()
                    S.op("pe", lambda e, psi=psi, dr=dr, t0=t0, w=w: e.matmul(
                        psi.ap[:, 0:w], lhsT=wg.ap[:, 2 + dr, :], rhs=xcb.ap[:, t0:t0 + w], start=True, stop=True),
                        reads=[wg.buf, xcb.buf], writes=[psi.buf])
                    S.op("act", lambda e, psi=psi, dr=dr, t0=t0, w=w: e.activation(
                        out=bF.ap[:, t0:t0 + w], in_=psi.ap[:, 0:w], func=AF.Sigmoid,
                        bias=vcol(l, V_BI + dr * 16 + n)), reads=[psi.buf, vecs.buf], writes=[bF.buf])
                k1 = lamk.ap[:, 0, dr * 16 + n:dr * 16 + n + 1]
                k2 = lamk.ap[:, 1, dr * 16 + n:dr * 16 + n + 1]
                S.op("act", lambda e, k2=k2: e.activation(out=tF.ap, in_=aF.ap, func=AF.Exp, scale=k2),
                     reads=[aF.buf, lamk.buf], writes=[tF.buf])
                S.op("act", lambda e, k1=k1: e.activation(out=aF.ap, in_=aF.ap, func=AF.Exp, scale=k1),
                     reads=[aF.buf, lamk.buf], writes=[aF.buf])
                S.op("act", lambda e: e.activation(out=tF.ap, in_=tF.ap, func=AF.Sqrt, scale=-1.0, bias=1.0),
                     reads=[tF.buf], writes=[tF.buf])
                S.op("dve", lambda e: e.tensor_tensor(out=bF.ap, in0=bF.ap, in1=tF.ap, op=ALU.mult),
                     reads=[bF.buf, tF.buf], writes=[bF.buf])
                S.op("dve", lambda e: e.tensor_tensor(out=bF.ap, in0=bF.ap, in1=xc.ap, op=ALU.mult),
                     reads=[bF.buf, xc.buf], writes=[bF.buf])
                if dr == 0:
                    S.op("dve", lambda e: e.tensor_tensor_scan(out=hs.ap[:, 0:TCX], data0=aF.ap[:, 0:TCX],
                                                               data1=bF.ap[:, 0:TCX], initial=0.0,
                                                               op0=ALU.mult, op1=ALU.add),
                         reads=[aF.buf, bF.buf], writes=[hs.buf])
                    S.op("dve", lambda e: e.tensor_tensor_scan(out=hs.ap[:, TCX:T], data0=aF.ap[:, TCX:T],
                                                               data1=bF.ap[:, TCX:T], initial=hs.ap[:, TCX - 1:TCX],
                                                               op0=ALU.mult, op1=ALU.add),
                         reads=[aF.buf, bF.buf, hs.buf], writes=[hs.buf])
                else:
                    S.op("dve", lambda e: e.tensor_tensor_scan(out=tF.ap[:, 0:TCX][:, ::-1],
                                                               data0=aF.ap[:, 0:TCX][:, ::-1],
                                                               data1=bF.ap[:, 0:TCX][:, ::-1], initial=0.0,
                                                               op0=ALU.mult, op1=ALU.add),
                         reads=[aF.buf, bF.buf], writes=[tF.buf])
                    S.op("dve", lambda e: e.tensor_tensor_scan(out=tF.ap[:, TCX:T][:, ::-1],
                                                               data0=aF.ap[:, TCX:T][:, ::-1],
                                                               data1=bF.ap[:, TCX:T][:, ::-1],
                                                               initial=tF.ap[:, 0:1],
                                                               op0=ALU.mult, op1=ALU.add),
                         reads=[aF.buf, bF.buf, tF.buf], writes=[tF.buf])
                    S.op("dve", lambda e: e.tensor_tensor(out=hs.ap, in0=hs.ap, in1=tF.ap, op=ALU.add),
                         reads=[hs.buf, tF.buf], writes=[hs.buf])
            for dr in range(2):
                do_dir(dr)
            for ti, (t0, w) in enumerate(TT):
                sz = szt[ti]
                yb = ybs.next()
                S.op("dve", lambda e, sz=sz, yb=yb, t0=t0, w=w: e.tensor_tensor(out=yb.ap[:, 0:w], in0=hs.ap[:, t0:t0 + w],
                                                                               in1=sz.ap[:, 0:w], op=ALU.mult),
                     reads=[hs.buf, sz.buf], writes=[yb.buf])
                S.dma("sp", lambda e, yb=yb, t0=t0, w=w: e.dma_start(out=Yd[0, n * 128:(n + 1) * 128, t0:t0 + w],
                                                                    in_=yb.ap[:, 0:w]),
                      reads=[yb.buf], writes=[Y_b[0]])

        for n in range(NCH):
            stage_g(n, stage_x(n))

    def phase_ml(l):
        AR.reset()
        PSG = Rot([PSB[0], PSB[1], PSB[4], PSB[5]])
        PSE = PSB[0]
        PSA = Rot([PSB[1], PSB[4], PSB[5]])
        PSS = [(PSB[6], PSB[7], psum_t[3]), (PSB[2], PSB[3], psum_t[1])]
        qT = AR.bf16(2, T)
        kT = AR.bf16(2, T)
        ktm = AR.bf16(NTILE, 256)
        vtm = AR.bf16(NTILE, 256)
        hD = [AR.bf16(2, T), AR.bf16(2, T)]
        Gtm = AR.f32(NTILE, 32)
        lftm = AR.f32(NTILE, 32)
        rtm = AR.f32(NTILE, 2, 8)
        sLa = AR.f32(NTILE, 32)
        Ecs = [Rot([AR.f32(128) for _ in range(2)]) for _ in range(2)]
        rhc = Rot([AR.f32(128) for _ in range(2)])
        ST32 = [AR.f32(2, 384), AR.f32(2, 384)]
        STb = [AR.bf16(2, 384), AR.bf16(2, 384)]
        tmpC = Rot([AR.f32(2, 384) for _ in range(2)])
        PTs = Rot([AR.bf16(128) for _ in range(4)])
        kps = Rot([AR.bf16(256) for _ in range(4)])
        t128 = Rot([AR.f32(128) for _ in range(4)])
        qss = [Rot([AR.bf16(2, 128) for _ in range(3)]) for _ in range(2)]
        e256 = Rot([AR.f32(256) for _ in range(6)])
        sq256 = Rot([AR.bf16(256) for _ in range(2)])
        yb256 = Rot([AR.bf16(256) for _ in range(2)])
        hg256 = Rot([AR.f32(2, 256) for _ in range(2)])

        wgt = load_w(w_in[l, :, O_MG:O_MG + 32], n=32)
        for c in range(NTILE):
            ps = PSG.next()

            def f(e, c=c, ps=ps):
                for kc in range(NCH):
                    ins = e.matmul(ps.ap[:, 0:32], lhsT=hxT[:, kc, c * 128:(c + 1) * 128], rhs=wgt.ap[:, kc, 0:32],
                                   start=(kc == 0), stop=(kc == NCH - 1))
                return ins
            S.op("pe", f, reads=[wgt.buf] + hx_all, writes=[ps.buf])
            S.op("dve", lambda e, c=c, ps=ps: e.tensor_tensor(out=Gtm.ap[:, c, :], in0=ps.ap[:, 0:32],
                                                              in1=gbB.ap[:, l * 32:(l + 1) * 32], op=ALU.add),
                 reads=[ps.buf, gbB.buf], writes=[Gtm.buf])
        S.op("act", lambda e: e.activation(out=lftm.ap, in_=Gtm.ap, func=AF.Exp, scale=-1.0),
             reads=[Gtm.buf], writes=[lftm.buf])
        S.op("act", lambda e: e.activation(out=lftm.ap, in_=lftm.ap, func=AF.Ln, bias=1.0),
             reads=[lftm.buf], writes=[lftm.buf])
        lf2 = lftm.ap.rearrange("p c g -> p (c g)")
        for (a, b) in ((0, 288), (288, 576)):
            ps = PSG.next()
            S.op("pe", lambda e, ps=ps, a=a, b=b: e.matmul(ps.ap[:, 0:b - a], lhsT=ones_f.ap, rhs=lf2[:, a:b],
                                                           start=True, stop=True),
                 reads=[ones_f.buf, lftm.buf], writes=[ps.buf])
            S.op("act", lambda e, ps=ps, a=a, b=b: e.activation(
                out=sLa.ap.rearrange("p c g -> p (c g)")[:, a:b], in_=ps.ap[:, 0:b - a], func=AF.Exp, scale=-1.0),
                reads=[ps.buf], writes=[sLa.buf])
        for dr, tri in ((0, TRIF), (1, TRIB)):
            for (a, b) in ((0, 288), (288, 576)):
                ps = PSG.next()
                S.op("pe", lambda e, ps=ps, a=a, b=b, tri=tri: e.matmul(ps.ap[:, 0:b - a], lhsT=tri, rhs=lf2[:, a:b],
                                                                        start=True, stop=True),
                     reads=[cst.buf, lftm.buf], writes=[ps.buf])
                c0 = a // 32
                S.op("dve", lambda e, ps=ps, dr=dr, c0=c0: e.tensor_tensor(
                    out=rtm.ap[:, c0:c0 + 9, dr, :],
                    in0=ps.ap[:, 0:288].rearrange("p (c g) -> p c g", g=32)[:, :, dr * 16 + 8:dr * 16 + 16],
                    in1=Gtm.ap[:, c0:c0 + 9, dr * 16:dr * 16 + 8], op=ALU.add),
                    reads=[ps.buf, Gtm.buf], writes=[rtm.buf])
        S.op("act", lambda e: e.activation(out=rtm.ap, in_=rtm.ap, func=AF.Exp), reads=[rtm.buf], writes=[rtm.buf])

        order_b = [1, 0] + list(range(NTILE - 1, 1, -1))
        masks = (TRIF, TRIB)
        for h in range(8):
            wq = [load_w(w_in[l, :, O_MQ + h * 256 + j * 128:O_MQ + h * 256 + (j + 1) * 128]) for j in range(2)]
            wk = [load_w(w_in[l, :, O_MK + h * 256 + j * 128:O_MK + h * 256 + (j + 1) * 128]) for j in range(2)]
            for j in range(2):
                for ti, (t0, w) in enumerate(TT):
                    ps = PSG.next()
                    proj_fm(wq[j], 128, ti, ps)
                    evac(qT.ap[:, j, t0:t0 + w], qT.buf, ps, ps.ap[:, 0:w])
                    ps = PSG.next()
                    proj_fm(wk[j], 128, ti, ps)
                    evac(kT.ap[:, j, t0:t0 + w], kT.buf, ps, ps.ap[:, 0:w], scale=0.0625)
            for c in range(NTILE):
                ps = PSG.next()

                def f(e, c=c, ps=ps, wk=wk):
                    for j in range(2):
                        for kc in range(NCH):
                            ins = e.matmul(ps.ap[:, j * 128:(j + 1) * 128], lhsT=hxT[:, kc, c * 128:(c + 1) * 128],
                                           rhs=wk[j].ap[:, kc, :], start=(kc == 0), stop=(kc == NCH - 1))
                    return ins
                S.op("pe", f, reads=[wk[0].buf, wk[1].buf] + hx_all, writes=[ps.buf])
                evac(ktm.ap[:, c, :], ktm.buf, ps, ps.ap[:, 0:256], scale=0.0625)
            wv = [load_w(w_in[l, :, O_MV + h * 256 + j * 128:O_MV + h * 256 + (j + 1) * 128]) for j in range(2)]
            for c in range(NTILE):
                ps = PSG.next()

                def f(e, c=c, ps=ps, wv=wv):
                    for j in range(2):
                        for kc in range(NCH):
                            ins = e.matmul(ps.ap[:, j * 128:(j + 1) * 128], lhsT=hxT[:, kc, c * 128:(c + 1) * 128],
                                           rhs=wv[j].ap[:, kc, :], start=(kc == 0), stop=(kc == NCH - 1))
                    return ins
                S.op("pe", f, reads=[wv[0].buf, wv[1].buf] + hx_all, writes=[ps.buf])
                evac(vtm.ap[:, c, :], vtm.buf, ps, ps.ap[:, 0:256])
            def dmm(info, dr):
                kp, c = info["kp"], info["c"]
                pb0, pb1, pst = PSS[dr]

                def fD(e, kp=kp, c=c, pst=pst):
                    for j in range(2):
                        e.matmul(pst[:, j, 0:256], lhsT=kp.ap[:, j * 128:(j + 1) * 128], rhs=vtm.ap[:, c, :],
                                 start=True, stop=True)
                        ins = e.matmul(pst[:, j, 256:384], lhsT=kp.ap[:, j * 128:(j + 1) * 128], rhs=ones_b.ap,
                                       start=True, stop=True)
                    return ins
                S.op("pe", fD, reads=[kp.buf, vtm.buf, ones_b.buf], writes=[pb0.buf, pb1.buf])

            def P1(step, dr):
                c = step if dr == 0 else order_b[step]
                cs = slice(c * 128, (c + 1) * 128)
                fcol = dr * 16 + 8 + h
                rcol = rtm.ap[:, c, dr, h:h + 1]
                rh = rhc.next()
                S.op("act", lambda e: e.activation(out=rh.ap, in_=masks[dr], func=AF.Identity,
                                                   scale=lftm.ap[:, c, fcol:fcol + 1]),
                     reads=[cst.buf, lftm.buf], writes=[rh.buf])
                S.op("pe", lambda e: e.matmul(PSE.ap[:, 0:128], lhsT=ones_f.ap, rhs=rh.ap, start=True, stop=True),
                     reads=[ones_f.buf, rh.buf], writes=[PSE.buf])
                Ec = Ecs[dr].next()
                S.op("act", lambda e: e.activation(out=Ec.ap, in_=PSE.ap[:, 0:128], func=AF.Exp, scale=-1.0),
                     reads=[PSE.buf], writes=[Ec.buf])
                qs = qss[dr].next()
                S.op("dve", lambda e: e.tensor_tensor(out=qs.ap, in0=qT.ap[:, :, cs],
                                                      in1=Ec.ap.unsqueeze(1).broadcast_to([128, 2, 128]), op=ALU.mult),
                     reads=[qT.buf, Ec.buf], writes=[qs.buf])
                pa = PSA.next()

                def fS(e):
                    for j in range(2):
                        ins = e.matmul(pa.ap[:, 0:128], lhsT=kT.ap[:, j, cs], rhs=qs.ap[:, j, :],
                                       start=(j == 0), stop=(j == 1))
                    return ins
                S.op("pe", fS, reads=[kT.buf, qs.buf], writes=[pa.buf])
                PT = PTs.next()
                S.op("dve", lambda e: e.scalar_tensor_tensor(out=PT.ap, in0=pa.ap[:, 0:128], scalar=rcol, in1=masks[dr],
                                                             op0=ALU.mult, op1=ALU.mult),
                     reads=[pa.buf, rtm.buf, cst.buf], writes=[PT.buf])
                info = dict(c=c, cs=cs, fcol=fcol, qs=qs, pa=pa, PT=PT)
                if step < NTILE - 1:
                    kp = kps.next()
                    S.op("act", lambda e: e.activation(out=kp.ap, in_=ktm.ap[:, c, :], func=AF.Identity, scale=rcol),
                         reads=[ktm.buf, rtm.buf], writes=[kp.buf])
                    info["kp"] = kp
                return info

            def P2(step, dr, info, nxt_info):
                c, cs, fcol, qs, pa, PT = (info[k] for k in ("c", "cs", "fcol", "qs", "pa", "PT"))
                first = (step == 0)
                stb = STb[dr]

                def fU(e):
                    for ec in range(2):
                        o = pa.ap[:, 128 + ec * 128:256 + ec * 128]
                        ins = e.matmul(o, lhsT=vtm.ap[:, c, ec * 128:(ec + 1) * 128], rhs=PT.ap, start=True, stop=first)
                        if not first:
                            for j in range(2):
                                ins = e.matmul(o, lhsT=stb.ap[:, j, ec * 128:(ec + 1) * 128], rhs=qs.ap[:, j, :],
                                               start=False, stop=(j == 1))
                    o = pa.ap[:, 384:512]
                    ins = e.matmul(o, lhsT=ones_b.ap, rhs=PT.ap, start=True, stop=first)
                    if not first:
                        for j in range(2):
                            ins = e.matmul(o, lhsT=stb.ap[:, j, 256:384], rhs=qs.ap[:, j, :], start=False, stop=(j == 1))
                    return ins
                S.op("pe", fU, reads=[vtm.buf, PT.buf, stb.buf, qs.buf, ones_b.buf], writes=[pa.buf])
                if step < NTILE - 1:
                    pb0, pb1, pst = PSS[dr]
                    sl = sLa.ap[:, c, fcol:fcol + 1]
                    st32 = ST32[dr]
                    if first:
                        S.op("dve", lambda e: e.tensor_scalar(out=stb.ap, in0=pst[:, :, 0:384], scalar1=sl, scalar2=None,
                                                              op0=ALU.mult),
                             reads=[pb0.buf, pb1.buf, sLa.buf], writes=[stb.buf])
                        S.op("act", lambda e: e.activation(out=st32.ap, in_=pst[:, :, 0:384], func=AF.Identity, scale=sl),
                             reads=[pb0.buf, pb1.buf, sLa.buf], writes=[st32.buf])
                    else:
                        tc_ = tmpC.next()
                        S.op("dve", lambda e: e.tensor_tensor(out=tc_.ap, in0=pst[:, :, 0:384], in1=st32.ap, op=ALU.add),
                             reads=[pb0.buf, pb1.buf, st32.buf], writes=[tc_.buf])
                        S.op("dve", lambda e: e.tensor_scalar(out=stb.ap, in0=tc_.ap, scalar1=sl, scalar2=None,
                                                              op0=ALU.mult),
                             reads=[tc_.buf, sLa.buf], writes=[stb.buf])
                        S.op("act", lambda e: e.activation(out=st32.ap, in_=tc_.ap, func=AF.Identity, scale=sl),
                             reads=[tc_.buf, sLa.buf], writes=[st32.buf])
                    if nxt_info is not None and "kp" in nxt_info:
                        dmm(nxt_info, dr)
                am = t128.next()
                S.op("dve", lambda e: e.tensor_scalar(out=am.ap, in0=pa.ap[:, 384:512], scalar1=-1.0, scalar2=1.0,
                                                      op0=ALU.mult, op1=ALU.max),
                     reads=[pa.buf], writes=[am.buf])
                S.op("dve", lambda e: e.scalar_tensor_tensor(out=am.ap, in0=pa.ap[:, 384:512], scalar=1.0, in1=am.ap,
                                                             op0=ALU.max, op1=ALU.max),
                     reads=[pa.buf, am.buf], writes=[am.buf])
                S.op("act", lambda e: e.activation(out=am.ap, in_=am.ap, func=AF.Ln), reads=[am.buf], writes=[am.buf])
                S.op("act", lambda e: e.activation(out=am.ap, in_=am.ap, func=AF.Exp, scale=-1.0),
                     reads=[am.buf], writes=[am.buf])
                hd = hD[dr]
                S.op("dve", lambda e: e.tensor_tensor(
                    out=hd.ap[:, :, cs], in0=pa.ap[:, 128:384].rearrange("p (a b) -> p a b", a=2),
                    in1=am.ap.unsqueeze(1).broadcast_to([128, 2, 128]), op=ALU.mult),
                    reads=[pa.buf, am.buf], writes=[hd.buf])

            cur = [P1(0, 0), P1(0, 1)]
            for dr in range(2):
                dmm(cur[dr], dr)
            for step in range(NTILE):
                nxt = [P1(step + 1, 0), P1(step + 1, 1)] if step + 1 < NTILE else [None, None]
                for dr in range(2):
                    P2(step, dr, cur[dr], nxt[dr])
                cur = nxt
            wo = [load_w(w_in[l, :, O_MO + h * 256 + j * 128:O_MO + h * 256 + (j + 1) * 128]) for j in range(2)]
            wz = [load_w(w_in[l, :, O_MZ + h * 256 + j * 128:O_MZ + h * 256 + (j + 1) * 128]) for j in range(2)]
            for mt in range(T // 256):
                t0 = mt * 256
                hg = hg256.next()
                pss = PSE
                for ec in range(2):
                    ps = PSA.next()
                    proj_fm_rng(wo[ec], 128, t0, 256, ps)
                    so = e256.next()
                    S.op("act", lambda e, so=so, ps=ps: e.activation(out=so.ap, in_=ps.ap[:, 0:256], func=AF.Sigmoid),
                         reads=[ps.buf], writes=[so.buf])
                    hsum = e256.next()
                    S.op("dve", lambda e, hsum=hsum, ec=ec, t0=t0: e.tensor_tensor(
                        out=hsum.ap, in0=hD[0].ap[:, ec, t0:t0 + 256], in1=hD[1].ap[:, ec, t0:t0 + 256], op=ALU.add),
                        reads=[hD[0].buf, hD[1].buf], writes=[hsum.buf])
                    S.op("dve", lambda e, hsum=hsum, so=so, hg=hg, ec=ec: e.tensor_tensor(
                        out=hg.ap[:, ec, :], in0=hsum.ap, in1=so.ap, op=ALU.mult),
                        reads=[hsum.buf, so.buf], writes=[hg.buf])
                    sq = sq256.next()
                    S.op("act", lambda e, sq=sq, hg=hg, ec=ec: e.activation(out=sq.ap, in_=hg.ap[:, ec, :], func=AF.Square),
                         reads=[hg.buf], writes=[sq.buf])
                    S.op("pe", lambda e, sq=sq, pss=pss, ec=ec: e.matmul(pss.ap[:, 0:256], lhsT=ones_b.ap, rhs=sq.ap,
                                                                        start=(ec == 0), stop=(ec == 1)),
                         reads=[sq.buf, ones_b.buf], writes=[pss.buf])
                rs = e256.next()
                S.op("act", lambda e, rs=rs, pss=pss: e.activation(out=rs.ap, in_=pss.ap[:, 0:256], func=AF.Ln,
                                                                  scale=1.0 / 256, bias=EPS),
                     reads=[pss.buf], writes=[rs.buf])
                S.op("act", lambda e, rs=rs: e.activation(out=rs.ap, in_=rs.ap, func=AF.Exp, scale=-0.5),
                     reads=[rs.buf], writes=[rs.buf])
                for ec in range(2):
                    ps = PSA.next()
                    proj_fm_rng(wz[ec], 128, t0, 256, ps)
                    sz = e256.next()
                    S.op("act", lambda e, sz=sz, ps=ps: e.activation(out=sz.ap, in_=ps.ap[:, 0:256], func=AF.Silu),
                         reads=[ps.buf], writes=[sz.buf])
                    S.op("dve", lambda e, hg=hg, ec=ec, rs=rs: e.tensor_tensor(out=hg.ap[:, ec, :], in0=hg.ap[:, ec, :],
                                                                              in1=rs.ap, op=ALU.mult),
                         reads=[hg.buf, rs.buf], writes=[hg.buf])
                    yb = yb256.next()
                    S.op("dve", lambda e, hg=hg, ec=ec, sz=sz, yb=yb, h=h: e.scalar_tensor_tensor(
                        out=yb.ap, in0=hg.ap[:, ec, :], scalar=vcol(l, V_MLN + h * 2 + ec), in1=sz.ap,
                        op0=ALU.mult, op1=ALU.mult),
                        reads=[hg.buf, sz.buf, vecs.buf], writes=[yb.buf])
                    r0 = h * 256 + ec * 128
                    S.dma("sp", lambda e, yb=yb, r0=r0, t0=t0: e.dma_start(out=Yd[1, r0:r0 + 128, t0:t0 + 256], in_=yb.ap),
                          reads=[yb.buf], writes=[Y_b[1]])

    def phase_att(l):
        AR.reset()
        PSG = Rot(PSB[0:2])
        PSSc = Rot(PSB[2:5])
        PSO = Rot(PSB[5:7])
        PSD = Rot(PSB[7:8])
        KT = AR.bf16(4, T)
        Vtm = AR.bf16(NTILE, 512)
        cosr = Rot([AR.f32(512) for _ in range(2)])
        sinr = Rot([AR.f32(512) for _ in range(2)])
        f512 = Rot([AR.f32(512) for _ in range(8)])
        szs = Rot([AR.f32(512) for _ in range(3)])
        rcs = Rot([AR.f32(512) for _ in range(2)])
        b512 = Rot([AR.bf16(512) for _ in range(5)])
        qTs = Rot([AR.bf16(512) for _ in range(2)])
        PTs = Rot([AR.bf16(512) for _ in range(4)])
        ybs = Rot([AR.bf16(512) for _ in range(2)])
        SC = 128.0 ** -0.5

        def load_cs(ti):
            t0, w = TT[ti]
            ct = cosr.next()
            st = sinr.next()
            S.dma("sp", lambda e: e.dma_start(out=ct.ap, in_=cos_d[:, t0 - TCX:t0 - TCX + w]), writes=[ct.buf])
            S.dma("sp", lambda e: e.dma_start(out=st.ap, in_=sin_d[:, t0 - TCX:t0 - TCX + w]), writes=[st.buf])
            return ct, st

        def norm_rope_gen(ps, w, gcol, rope, dst_ap, dst_buf, cs):
            q32 = f512.next()
            S.op("dve", lambda e: e.tensor_copy(out=q32.ap[:, 0:w], in_=ps.ap[:, 0:w]),
                 reads=[ps.buf], writes=[q32.buf])
            sq = b512.next()
            S.op("dve", lambda e: e.tensor_tensor(out=sq.ap[:, 0:w], in0=q32.ap[:, 0:w], in1=q32.ap[:, 0:w], op=ALU.mult),
                 reads=[q32.buf], writes=[sq.buf])
            yield
            ps2 = PSG.next()
            S.op("pe", lambda e: e.matmul(ps2.ap[:, 0:w], lhsT=ones_b.ap, rhs=sq.ap[:, 0:w], start=True, stop=True),
                 reads=[ones_b.buf, sq.buf], writes=[ps2.buf])
            rs = f512.next()
            S.op("act", lambda e: e.activation(out=rs.ap[:, 0:w], in_=ps2.ap[:, 0:w], func=AF.Ln, scale=1.0 / 128,
                                               bias=EPS), reads=[ps2.buf], writes=[rs.buf])
            S.op("act", lambda e: e.activation(out=rs.ap[:, 0:w], in_=rs.ap[:, 0:w], func=AF.Exp, scale=-0.5),
                 reads=[rs.buf], writes=[rs.buf])
            if not rope:
                S.op("dve", lambda e: e.scalar_tensor_tensor(out=dst_ap, in0=q32.ap[:, 0:w], scalar=gcol, in1=rs.ap[:, 0:w],
                                                             op0=ALU.mult, op1=ALU.mult),
                     reads=[q32.buf, rs.buf, vecs.buf], writes=[dst_buf])
                return
            qn = b512.next()
            S.op("dve", lambda e: e.scalar_tensor_tensor(out=qn.ap[:, 0:w], in0=q32.ap[:, 0:w], scalar=gcol,
                                                         in1=rs.ap[:, 0:w], op0=ALU.mult, op1=ALU.mult),
                 reads=[q32.buf, rs.buf, vecs.buf], writes=[qn.buf])
            ct, st = cs
            t2 = f512.next()
            S.op("dve", lambda e: e.tensor_tensor(out=t2.ap[:, 0:w], in0=qn.ap[:, 0:w], in1=ct.ap[:, 0:w], op=ALU.mult),
                 reads=[qn.buf, ct.buf], writes=[t2.buf])
            yield
            ps3 = PSG.next()
            S.op("pe", lambda e: e.matmul(ps3.ap[:, 0:w], lhsT=RROT, rhs=qn.ap[:, 0:w], start=True, stop=True),
                 reads=[cstb.buf, qn.buf], writes=[ps3.buf])
            t1 = f512.next()
            S.op("dve", lambda e: e.tensor_tensor(out=t1.ap[:, 0:w], in0=ps3.ap[:, 0:w], in1=st.ap[:, 0:w], op=ALU.mult),
                 reads=[ps3.buf, st.buf], writes=[t1.buf])
            S.op("dve", lambda e: e.tensor_tensor(out=dst_ap, in0=t1.ap[:, 0:w], in1=t2.ap[:, 0:w], op=ALU.add),
                 reads=[t1.buf, t2.buf], writes=[dst_buf])

        def norm_rope(*a):
            for _ in norm_rope_gen(*a):
                pass

        for g in range(4):
            wk = load_w(w_in[l, :, O_AK + g * 128:O_AK + (g + 1) * 128])
            for ti, (t0, w) in enumerate(TT):
                ps = PSG.next()
                proj_fm(wk, 128, ti, ps)
                cs = load_cs(ti) if ti > 0 else None
                norm_rope(ps, w, vcol(l, V_KN), ti > 0, KT.ap[:, g, t0:t0 + w], KT.buf, cs)
        wv = [load_w(w_in[l, :, O_AV + g * 128:O_AV + (g + 1) * 128]) for g in range(4)]
        for c in range(NTILE):
            ps = PSG.next()

            def f(e, c=c, ps=ps):
                for g in range(4):
                    for kc in range(NCH):
                        ins = e.matmul(ps.ap[:, g * 128:(g + 1) * 128], lhsT=hxT[:, kc, c * 128:(c + 1) * 128],
                                       rhs=wv[g].ap[:, kc, :], start=(kc == 0), stop=(kc == NCH - 1))
                return ins
            S.op("pe", f, reads=[w.buf for w in wv] + hx_all, writes=[ps.buf])
            evac(Vtm.ap[:, c, :], Vtm.buf, ps, ps.ap[:, 0:512])
        tiles = [(h, ti) for h in range(16) for ti in range(len(TT))]
        ready = {}
        wcur = {}

        def prologue(h, ti):
            t0, w = TT[ti]
            if ti == 0:
                wcur["q"] = load_w(w_in[l, :, O_AQ + h * 128:O_AQ + (h + 1) * 128])
                wcur["z"] = load_w(w_in[l, :, O_AZ + h * 128:O_AZ + (h + 1) * 128])
            wq, wz = wcur["q"], wcur["z"]
            ps = PSG.next()
            proj_fm(wq, 128, ti, ps)
            qt = qTs.next()
            cs = load_cs(ti) if ti > 0 else None
            psz = PSG.next()
            proj_fm(wz, 128, ti, psz)
            yield
            gen = norm_rope_gen(ps, w, vcol(l, V_QN), ti > 0, qt.ap[:, 0:w], qt.buf, cs)
            next(gen, None)
            sz = szs.next()
            S.op("act", lambda e: e.activation(out=sz.ap[:, 0:w], in_=psz.ap[:, 0:w], func=AF.Exp, scale=-1.0),
                 reads=[psz.buf], writes=[sz.buf])
            S.op("act", lambda e: e.activation(out=sz.ap[:, 0:w], in_=sz.ap[:, 0:w], func=AF.Ln, bias=1.0),
                 reads=[sz.buf], writes=[sz.buf])
            S.op("act", lambda e: e.activation(out=sz.ap[:, 0:w], in_=sz.ap[:, 0:w], func=AF.Exp, scale=-1.0),
                 reads=[sz.buf], writes=[sz.buf])
            S.op("dve", lambda e: e.tensor_tensor(out=sz.ap[:, 0:w], in0=psz.ap[:, 0:w], in1=sz.ap[:, 0:w], op=ALU.mult),
                 reads=[psz.buf, sz.buf], writes=[sz.buf])
            ready[(h, ti)] = (qt, sz)
            yield
            for _ in gen:
                yield

        for _ in prologue(*tiles[0]):
            pass
        for idx, (h, ti) in enumerate(tiles):
            g = h // 4
            t0, w = TT[ti]
            qt, sz = ready.pop((h, ti))
            nxt = prologue(*tiles[idx + 1]) if idx + 1 < len(tiles) else None
            keys = list(range(NTILE)) if ti > 0 else [0, 1]
            nk = len(keys)
            pO = PSO.next()
            pD = PSD.next()
            pSs = {}

            def emit_S(ki):
                pS = PSSc.next()
                kc_ = keys[ki]
                S.op("pe", lambda e, pS=pS, kc_=kc_, qt=qt, w=w, g=g: e.matmul(
                    pS.ap[:, 0:w], lhsT=KT.ap[:, g, kc_ * 128:(kc_ + 1) * 128], rhs=qt.ap[:, 0:w],
                    start=True, stop=True), reads=[KT.buf, qt.buf], writes=[pS.buf])
                pSs[ki] = pS
            emit_S(0)
            if nk > 1:
                emit_S(1)
            for ki, kc_ in enumerate(keys):
                pS = pSs.pop(ki)
                PT = PTs.next()
                S.op("act", lambda e, pS=pS, PT=PT, w=w: e.activation(out=PT.ap[:, 0:w], in_=pS.ap[:, 0:w],
                                                                     func=AF.Exp, scale=SC),
                     reads=[pS.buf], writes=[PT.buf])
                if ki + 2 < nk:
                    emit_S(ki + 2)
                fst = (ki == 0)
                lst = (ki == nk - 1)

                def fO(e, PT=PT, kc_=kc_, w=w, fst=fst, lst=lst, pO=pO, pD=pD, g=g):
                    e.matmul(pO.ap[:, 0:w], lhsT=Vtm.ap[:, kc_, g * 128:(g + 1) * 128], rhs=PT.ap[:, 0:w],
                             start=fst, stop=lst)
                    return e.matmul(pD.ap[:, 0:w], lhsT=ones_b.ap, rhs=PT.ap[:, 0:w], start=fst, stop=lst)
                S.op("pe", fO, reads=[Vtm.buf, PT.buf, ones_b.buf], writes=[pO.buf, pD.buf])
                if nxt is not None and ki in (1, 4, 8, 12):
                    next(nxt, None)
            if nxt is not None:
                for _ in nxt:
                    pass
            rc = rcs.next()
            S.op("act", lambda e, rc=rc, pD=pD, w=w: e.activation(out=rc.ap[:, 0:w], in_=pD.ap[:, 0:w], func=AF.Ln),
                 reads=[pD.buf], writes=[rc.buf])
            S.op("act", lambda e, rc=rc, w=w: e.activation(out=rc.ap[:, 0:w], in_=rc.ap[:, 0:w], func=AF.Exp, scale=-1.0),
                 reads=[rc.buf], writes=[rc.buf])
            S.op("dve", lambda e, rc=rc, sz=sz, w=w: e.tensor_tensor(out=rc.ap[:, 0:w], in0=rc.ap[:, 0:w],
                                                                    in1=sz.ap[:, 0:w], op=ALU.mult),
                 reads=[rc.buf, sz.buf], writes=[rc.buf])
            yb = ybs.next()
            S.op("dve", lambda e, yb=yb, pO=pO, rc=rc, w=w: e.tensor_tensor(out=yb.ap[:, 0:w], in0=pO.ap[:, 0:w],
                                                                           in1=rc.ap[:, 0:w], op=ALU.mult),
                 reads=[pO.buf, rc.buf], writes=[yb.buf])
            S.dma("sp", lambda e, yb=yb, h=h, t0=t0, w=w: e.dma_start(out=Yd[2, h * 128:(h + 1) * 128, t0:t0 + w],
                                                                     in_=yb.ap[:, 0:w]),
                  reads=[yb.buf], writes=[Y_b[2]])

    def phase_merge(l, last):
        PSG = Rot(PSB)
        for n in range(3 if "nom1" not in phases else 0):
            S.barrier()
            AR.reset()
            Yt = AR.bf16(NCH, T)
            sgs = Rot([AR.f32(512) for _ in range(3)])
            pbs = Rot([AR.bf16(512) for _ in range(3)])
            for kc in range(NCH):
                S.dma("sp", lambda e, n=n, Yt=Yt, kc=kc: e.dma_start(out=Yt.ap[:, kc, :],
                                                                    in_=Yd[n, kc * 128:(kc + 1) * 128, :]),
                      reads=[Y_b[n]], writes=[Yt.buf])
            for dj in range(NCH):
                wb = load_w(w_br[l, n, :, dj * 128:(dj + 1) * 128])
                wg = load_w(w_in[l, :, O_G + n * D + dj * 128:O_G + n * D + (dj + 1) * 128])
                for ti, (t0, w) in enumerate(TT):
                    psP = PSG.next()

                    def f(e, psP=psP, wb=wb, t0=t0, w=w, Yt=Yt):
                        for kc in range(NCH):
                            ins = e.matmul(psP.ap[:, 0:w], lhsT=wb.ap[:, kc, :], rhs=Yt.ap[:, kc, t0:t0 + w],
                                           start=(kc == 0), stop=(kc == NCH - 1))
                        return ins
                    S.op("pe", f, reads=[wb.buf, Yt.buf], writes=[psP.buf])
                    psG = PSG.next()
                    proj_fm(wg, 128, ti, psG)
                    sg = sgs.next()
                    S.op("act", lambda e, sg=sg, psG=psG, w=w: e.activation(out=sg.ap[:, 0:w], in_=psG.ap[:, 0:w],
                                                                           func=AF.Sigmoid),
                         reads=[psG.buf], writes=[sg.buf])
                    pb = pbs.next()
                    S.op("dve", lambda e, pb=pb, psP=psP, sg=sg, w=w: e.tensor_tensor(out=pb.ap[:, 0:w], in0=psP.ap[:, 0:w],
                                                                                     in1=sg.ap[:, 0:w], op=ALU.mult),
                         reads=[psP.buf, sg.buf], writes=[pb.buf])
                    for mt in range(t0 // 256, (t0 + w) // 256):
                        o = mt * 256 - t0
                        S.dma("sp", lambda e, pb=pb, n=n, dj=dj, mt=mt, o=o: e.dma_start(
                            out=Gd[n, mt, :, dj * 256:(dj + 1) * 256], in_=pb.ap[:, o:o + 256]),
                            reads=[pb.buf], writes=[G_b[n]])
        if "nom2" in phases:
            return
        S.barrier()
        AR.reset()
        wout = hxT[:, :, 0:D]
        wo_b = Buf("wout")
        for dj in range(NCH):
            S.dma("pool", lambda e, dj=dj: e.dma_start(
                out=wout[:, :, dj * 128:(dj + 1) * 128],
                in_=w_out[l, :, dj * 128:(dj + 1) * 128].rearrange("(k p) c -> p k c", p=128)), writes=[wo_b])
        Gts = [Rot([AR.bf16(NCH, 256) for _ in range(k_)]) for k_ in (2, 1, 1)]
        yos = Rot([AR.f32(NCH, 256) for _ in range(2)])
        xts = Rot([AR.f32(NCH, 256) for _ in range(2)])
        sqs = Rot([AR.bf16(256) for _ in range(4)])
        rsds = Rot([AR.f32(256) for _ in range(2)])
        tmps = Rot([AR.f32(256) for _ in range(2)])
        PSG7 = Rot(PSB[0:7])
        mts = range(T // 256) if not last else range(1, T // 256)
        for mt in mts:
            t0 = mt * 256
            i = 1 if mt == 0 else 0
            gt = []
            for n in range(3):
                g_ = Gts[n].next()
                S.dma("sp", lambda e, g_=g_, n=n, mt=mt: e.dma_start(
                    out=g_.ap, in_=Gd[n, mt].rearrange("p (k t) -> p k t", k=NCH)),
                    reads=[G_b[n]], writes=[g_.buf])
                gt.append(g_)
            xt = xts.next()
            if l == 0:
                S.dma("sp", lambda e, xt=xt, t0=t0: e.dma_start(
                    out=xt.ap, in_=xT0[:, t0:t0 + 256].rearrange("(k p) t -> p k t", p=128)), writes=[xt.buf])
            else:
                S.dma("sp", lambda e, xt=xt, mt=mt: e.dma_start(
                    out=xt.ap, in_=xres[mt].rearrange("p (k t) -> p k t", k=NCH)), reads=[xres_b[mt]], writes=[xt.buf])
            S.op("dve", lambda e, gt=gt: e.tensor_tensor(out=gt[0].ap, in0=gt[0].ap, in1=gt[1].ap, op=ALU.add),
                 reads=[gt[0].buf, gt[1].buf], writes=[gt[0].buf])
            S.op("dve", lambda e, gt=gt: e.tensor_tensor(out=gt[0].ap, in0=gt[0].ap, in1=gt[2].ap, op=ALU.add),
                 reads=[gt[0].buf, gt[2].buf], writes=[gt[0].buf])
            mg = gt[0]
            yo = yos.next()
            rsd = rsds.next()
            pss = PSB[7]
            pend = []

            def emit_ones(sq, dj, pss=pss):
                S.op("pe", lambda e: e.matmul(pss.ap[:, 0:256], lhsT=ones_b.ap, rhs=sq.ap,
                                              start=(dj == 0), stop=(dj == NCH - 1)),
                     reads=[sq.buf, ones_b.buf], writes=[pss.buf])
            PSG = PSG7
            for dj in range(NCH):
                ps = PSG.next()

                def f(e, ps=ps, dj=dj, mg=mg):
                    for kc in range(NCH):
                        ins = e.matmul(ps.ap[:, 0:256], lhsT=wout[:, kc, dj * 128:(dj + 1) * 128], rhs=mg.ap[:, kc, :],
                                       start=(kc == 0), stop=(kc == NCH - 1))
                    return ins
                S.op("pe", f, reads=[wo_b, mg.buf], writes=[ps.buf])
                S.op("dve", lambda e, ps=ps, dj=dj, yo=yo: e.tensor_copy(out=yo.ap[:, dj, :], in_=ps.ap[:, 0:256]),
                     reads=[ps.buf], writes=[yo.buf])
                sq = sqs.next()
                S.op("act", lambda e, ps=ps, sq=sq: e.activation(out=sq.ap, in_=ps.ap[:, 0:256], func=AF.Square),
                     reads=[ps.buf], writes=[sq.buf])
                pend.append((sq, dj))
                if len(pend) > 2:
                    emit_ones(*pend.pop(0))
            while pend:
                emit_ones(*pend.pop(0))
            S.op("act", lambda e, pss=pss, rsd=rsd: e.activation(out=rsd.ap, in_=pss.ap[:, 0:256], func=AF.Ln,
                                                                scale=1.0 / D, bias=EPS),
                 reads=[pss.buf], writes=[rsd.buf])
            S.op("act", lambda e, rsd=rsd: e.activation(out=rsd.ap, in_=rsd.ap, func=AF.Exp, scale=-0.5),
                 reads=[rsd.buf], writes=[rsd.buf])
            for dj in range(NCH):
                tp = tmps.next()
                S.op("dve", lambda e, tp=tp, dj=dj, yo=yo, rsd=rsd: e.tensor_tensor(out=tp.ap, in0=yo.ap[:, dj, :],
                                                                                  in1=rsd.ap, op=ALU.mult),
                     reads=[yo.buf, rsd.buf], writes=[tp.buf])
                S.op("dve", lambda e, tp=tp, dj=dj, xt=xt, i=i: e.scalar_tensor_tensor(
                    out=xt.ap[:, dj, :], in0=tp.ap, scalar=Gmod.ap[:, dj, i:i + 1], in1=xt.ap[:, dj, :],
                    op0=ALU.mult, op1=ALU.add), reads=[tp.buf, Gmod.buf, xt.buf], writes=[xt.buf])
            if last:
                S.dma("sp", lambda e, xt=xt, t0=t0: e.dma_start(
                    out=out_d[:, t0 - TCX:t0 - TCX + 256].rearrange("(k p) t -> p k t", p=128), in_=xt.ap),
                    reads=[xt.buf], writes=[out_b])
            else:
                S.dma("sp", lambda e, xt=xt, mt=mt: e.dma_start(
                    out=xres[mt].rearrange("p (k t) -> p k t", k=NCH), in_=xt.ap),
                    reads=[xt.buf], writes=[xres_b[mt]])

    for l in range(n_layers):
        last = (l == n_layers - 1) and not debug
        S.barrier()
        phase0(l)
        S.barrier()
        phaseA(l)
        if "lru" in phases:
            S.barrier()
            phase_lru(l)
        if "ml" in phases:
            S.barrier()
            phase_ml(l)
        if "att" in phases:
            S.barrier()
            phase_att(l)
        if "merge" in phases:
            phase_merge(l, last)
    S.barrier()
    S.finalize()
    return nc


def _fm(v):
    v = np.asarray(v, np.float32)
    lead = v.shape[:-1]
    r = v.reshape(lead + (16, 128))
    return np.moveaxis(r, -1, 0)


def _host_consts():
    s = np.arange(128)
    trif = (s[:, None] <= s[None, :]).astype(np.float32)
    trib = (s[:, None] >= s[None, :]).astype(np.float32)
    R = np.zeros((128, 128), np.float32)
    for m in range(64):
        R[m + 64, m] = -1.0
        R[m, m + 64] = 1.0
    ident = np.eye(128, dtype=np.float32)
    cst = np.concatenate([trif, trib, R, ident], axis=1)
    n_freq = 32
    inv = (1.0 / (np.float32(10000.0) ** (np.arange(n_freq, dtype=np.float32) / np.float32(n_freq)))).astype(np.float32)
    row = np.repeat(np.arange(TLAT // 64), 64).astype(np.float32)
    col = np.tile(np.arange(64), TLAT // 64).astype(np.float32)
    ang = np.concatenate([row[:, None] * inv, col[:, None] * inv], axis=-1).astype(np.float32)
    cos = np.cos(ang).astype(np.float32)
    sin = np.sin(ang).astype(np.float32)
    cosT = np.ascontiguousarray(np.concatenate([cos, cos], axis=1).T)
    sinT = np.ascontiguousarray(np.concatenate([sin, sin], axis=1).T)
    return cst, cosT, sinT


def make_in_maps(inputs, cores):
    f32 = np.float32
    L = 4
    vecs = np.zeros((128, L, NV), f32)
    vecs[:, :, V_ADAB:V_ADAB + 48] = np.moveaxis(np.asarray(inputs["ada_b"], f32).reshape(L, 48, 128), -1, 0)
    vecs[:, :, V_NPRE:V_NPRE + 16] = _fm(inputs["norm_pre"])
    vecs[:, :, V_NPOST:V_NPOST + 16] = _fm(inputs["norm_post"])
    cw = _fm(inputs["lru_conv_w"])
    vecs[:, :, V_CW:V_CW + 64] = np.swapaxes(cw, 2, 3).reshape(128, L, 64)
    vecs[:, :, V_CB:V_CB + 16] = _fm(inputs["lru_conv_b"])
    vecs[:, :, V_BR:V_BR + 32] = _fm(inputs["lru_br"]).reshape(128, L, 32)
    vecs[:, :, V_BI:V_BI + 32] = _fm(inputs["lru_bi"]).reshape(128, L, 32)
    vecs[:, :, V_LAM:V_LAM + 32] = _fm(inputs["lru_lam"]).reshape(128, L, 32)
    vecs[:, :, V_MLN:V_MLN + 16] = _fm(inputs["ml_norm"])
    vecs[:, :, V_QN] = np.asarray(inputs["q_norm"], f32).T
    vecs[:, :, V_KN] = np.asarray(inputs["k_norm"], f32).T
    vecs = np.ascontiguousarray(vecs.reshape(128, L * NV))
    gb = np.asarray(inputs["ml_gate_b"], f32).reshape(L, 32)
    gbB = np.ascontiguousarray(np.broadcast_to(gb.reshape(1, L * 32), (128, L * 32)))
    cst, cosT, sinT = _host_consts()
    shared = {
        "ada_w": np.ascontiguousarray(inputs["ada_w"], dtype=f32),
        "w_in": np.ascontiguousarray(inputs["w_in"], dtype=f32),
        "w_br": np.ascontiguousarray(inputs["w_br"], dtype=f32),
        "w_out": np.ascontiguousarray(inputs["w_out"], dtype=f32),
        "lru_wr": np.ascontiguousarray(inputs["lru_wr"], dtype=f32),
        "lru_wi": np.ascontiguousarray(inputs["lru_wi"], dtype=f32),
        "vecs": vecs, "gbB": gbB, "cst": cst, "cosT": cosT, "sinT": sinT,
    }
    maps = []
    cc = np.asarray(inputs["c_ctx"], f32)
    for b in cores:
        xT = np.ascontiguousarray(np.concatenate([np.asarray(inputs["ctx"][b], f32), np.asarray(inputs["x"][b], f32)],
                                                 axis=0).T)
        cT = np.stack([np.asarray(inputs["c"][b], f32), cc], axis=-1)
        cT = np.ascontiguousarray(np.moveaxis(cT.reshape(16, 128, 2), 1, 0).reshape(128, 32))
        m = dict(shared)
        m["xT0"] = xT
        m["cT"] = cT
        maps.append(m)
    return maps


_NC_CACHE = {}


def kernel(**inputs):
    if "full" not in _NC_CACHE:
        _NC_CACHE["full"] = build(4, False)
    nc = _NC_CACHE["full"]
    B = 4
    maps = make_in_maps(inputs, list(range(B)))
    res = run_bass_kernel_spmd(nc, maps, core_ids=list(range(B)))
    out = np.stack([np.ascontiguousarray(res.results[b]["outT"].T) for b in range(B)], axis=0)
    return out.astype(np.float32)
```

```python
import numpy as np
import ml_dtypes
import concourse.bass as bass
import concourse.mybir as mybir
from concourse.bass_utils import run_bass_kernel_spmd

F32 = mybir.dt.float32
BF16 = mybir.dt.bfloat16
AF = mybir.ActivationFunctionType
ALU = mybir.AluOpType

D = 2048
NCH = 16
TCX = 256
TLAT = 2048
T = TCX + TLAT
NTILE = T // 128
NIN = 25632
EPS = 1e-6
TT = [(0, 256), (256, 512), (768, 512), (1280, 512), (1792, 512)]
O_LX, O_LZ, O_MQ, O_MK, O_MV, O_MO, O_MZ, O_MG, O_AQ, O_AK, O_AV, O_AZ, O_G = (
    0, 2048, 4096, 6144, 8192, 10240, 12288, 14336, 14368, 16416, 16928, 17440, 19488)
V_ADAB, V_NPRE, V_NPOST, V_CW, V_CB, V_BR, V_BI, V_LAM, V_MLN, V_QN, V_KN = (
    0, 48, 64, 80, 144, 160, 192, 224, 256, 272, 273)
NV = 274
ARENA_WORDS = 26112


class Buf:
    __slots__ = ("name", "w", "r", "excl")

    def __init__(self, name="", excl=False):
        self.name = name
        self.w = None
        self.r = {}
        self.excl = excl


class Tl:
    __slots__ = ("ap", "buf")

    def __init__(self, ap, buf=None):
        self.ap = ap
        self.buf = buf if buf is not None else Buf()


class Sched:
    COMPUTE = ("pe", "act", "dve", "pool")

    def __init__(self, nc, n_dma_sems=8):
        self.nc = nc
        self.sems = {}
        self.cnt = {}
        self.prog = {e: [] for e in ("pe", "act", "dve", "pool", "sp")}
        for e in self.COMPUTE:
            self.sems[e] = nc.alloc_semaphore("s_" + e)
            self.cnt[e] = 0
        self.dma_rot = {}
        for q in ("sp", "pool"):
            ids = []
            for i in range(n_dma_sems):
                cid = ("dma", q, i)
                self.sems[cid] = nc.alloc_semaphore("d_%s_%d" % (q, i))
                self.cnt[cid] = 0
                ids.append(cid)
            self.dma_rot[q] = [ids, 0]
        self.known = {e: {} for e in self.prog}

    def _deps(self, reads, writes):
        deps = {}
        for b in reads:
            if b.w is not None:
                c, v = b.w
                if deps.get(c, 0) < v:
                    deps[c] = v
        for b in writes:
            if b.w is not None:
                c, v = b.w
                if deps.get(c, 0) < v:
                    deps[c] = v
            for c, v in b.r.items():
                if deps.get(c, 0) < v:
                    deps[c] = v
        return deps

    def _waits(self, eng, deps):
        kn = self.known[eng]
        waits = []
        for c, v in deps.items():
            if c == eng and eng == "pe":
                continue
            if kn.get(c, 0) < v:
                kn[c] = v
                waits.append((self.sems[c], v))
        return waits

    def op(self, eng, fn, reads=(), writes=()):
        ex = [b for b in reads if b.excl]
        if ex:
            writes = list(writes) + ex
        waits = self._waits(eng, self._deps(reads, writes))
        self.cnt[eng] += 1
        val = self.cnt[eng]
        sem = self.sems[eng]

        def run(e, waits=waits, fn=fn, sem=sem):
            for s, v in waits:
                e.wait_ge(s, v)
            fn(e).then_inc(sem, 1)
        self.prog[eng].append(run)
        for b in reads:
            if b.r.get(eng, 0) < val:
                b.r[eng] = val
        for b in writes:
            b.w = (eng, val)
            b.r = {}
        return val

    def dma(self, q, fn, reads=(), writes=()):
        rot = self.dma_rot[q]
        cid = rot[0][rot[1]]
        rot[1] = (rot[1] + 1) % len(rot[0])
        deps = self._deps(reads, writes)
        if self.cnt[cid] > 0:
            deps[cid] = max(deps.get(cid, 0), self.cnt[cid])
        waits = self._waits(q, deps)
        self.cnt[cid] += 16
        val = self.cnt[cid]
        sem = self.sems[cid]

        def run(e, waits=waits, fn=fn, sem=sem):
            for s, v in waits:
                e.wait_ge(s, v)
            fn(e).then_inc(sem, 16)
        self.prog[q].append(run)
        for b in reads:
            b.r[cid] = val
        for b in writes:
            b.w = (cid, val)
            b.r = {}

    def barrier(self):
        deps = {c: v for c, v in self.cnt.items() if v > 0}
        for eng in self.prog:
            waits = self._waits(eng, dict(deps))

            def run(e, waits=waits):
                for s, v in waits:
                    e.wait_ge(s, v)
            self.prog[eng].append(run)

    def finalize(self):
        nc = self.nc
        prog = self.prog
        with nc.Block() as block:
            @block.tensor
            def _(e):
                for f in prog["pe"]:
                    f(e)

            @block.scalar
            def _(e):
                for f in prog["act"]:
                    f(e)

            @block.vector
            def _(e):
                for f in prog["dve"]:
                    f(e)

            @block.gpsimd
            def _(e):
                for f in prog["pool"]:
                    f(e)

            @block.sync
            def _(e):
                for f in prog["sp"]:
                    f(e)


class Rot:
    def __init__(self, items):
        self.items = items
        self.i = 0

    def next(self):
        t = self.items[self.i]
        self.i = (self.i + 1) % len(self.items)
        return t


class Arena:
    def __init__(self, tensor, nwords):
        self.t = tensor
        self.n = nwords
        self.off = 0

    def reset(self):
        self.off = 0

    def _take(self, nw):
        a = self.off
        assert a + nw <= self.n, ("arena overflow", a + nw, self.n)
        self.off += nw
        return self.t[:, a:a + nw]

    def f32(self, *shape):
        n = int(np.prod(shape))
        ap = self._take(n)
        if len(shape) == 2:
            ap = ap.rearrange("p (a b) -> p a b", a=shape[0])
        elif len(shape) == 3:
            ap = ap.rearrange("p (a b c) -> p a b c", a=shape[0], b=shape[1])
        return Tl(ap)

    def bf16(self, *shape):
        n = int(np.prod(shape))
        nw = (n + 1) // 2
        ap = self._take(nw).bitcast(BF16)[:, 0:n]
        if len(shape) == 2:
            ap = ap.rearrange("p (a b) -> p a b", a=shape[0])
        elif len(shape) == 3:
            ap = ap.rearrange("p (a b c) -> p a b c", a=shape[0], b=shape[1])
        return Tl(ap)


def build(n_layers=4, debug=False, phases=("lru", "ml", "att", "merge")):
    nc = bass.Bass("TRN2", target_bir_lowering=False)
    S = Sched(nc)

    def din(name, shape, dt=F32):
        return nc.dram_tensor(name, list(shape), dt, kind="ExternalInput").ap()

    xT0 = din("xT0", [D, T])
    cTd = din("cT", [128, 32])
    ada_w = din("ada_w", [4, D, 6144])
    w_in = din("w_in", [4, D, NIN])
    w_br = din("w_br", [4, 3, D, D])
    w_out = din("w_out", [4, D, D])
    lru_wr = din("lru_wr", [4, 2, 16, 128, 128])
    lru_wi = din("lru_wi", [4, 2, 16, 128, 128])
    vecs_d = din("vecs", [128, 4 * NV])
    gbB_d = din("gbB", [128, 4 * 32])
    cst_d = din("cst", [128, 4 * 128])
    cos_d = din("cosT", [128, TLAT])
    sin_d = din("sinT", [128, TLAT])
    out_d = nc.dram_tensor("outT", [D, TLAT], F32, kind="ExternalOutput").ap()
    ykind = "ExternalOutput" if debug else "Internal"
    xres = nc.dram_tensor("xres", [T // 256, 128, NCH * 256], F32, kind=ykind).ap()
    Yd = nc.dram_tensor("Ybr", [3, D, T], BF16, kind=ykind).ap()
    Gd = nc.dram_tensor("Gbr", [3, T // 256, 128, NCH * 256], BF16, kind=ykind).ap()
    xres_b = [Buf("xres%d" % i) for i in range(9)]
    Y_b = [Buf("Y%d" % i) for i in range(3)]
    G_b = [Buf("G%d" % i) for i in range(3)]
    out_b = Buf("out")

    hxT_t = nc.alloc_sbuf_tensor("hxT", [128, NCH, T], BF16)
    hxT = hxT_t[:]
    hx_b = [Buf("hx%d" % i) for i in range(len(TT))]
    hx_all = hx_b
    WP = Rot([Tl(nc.alloc_sbuf_tensor("wp%d" % i, [128, NCH, 128], BF16)[:]) for i in range(6)])
    arena_t = nc.alloc_sbuf_tensor("arena", [128, ARENA_WORDS], F32)
    AR = Arena(arena_t, ARENA_WORDS)
    vecs = Tl(nc.alloc_sbuf_tensor("vecs_s", [128, 4 * NV], F32)[:])
    gbB = Tl(nc.alloc_sbuf_tensor("gbB_s", [128, 4 * 32], F32)[:])
    cst = Tl(nc.alloc_sbuf_tensor("cst_s", [128, 4 * 128], F32)[:])
    cstb = Tl(nc.alloc_sbuf_tensor("cstb_s", [128, 4 * 128], BF16)[:])
    ones_b = Tl(nc.alloc_sbuf_tensor("ones_b", [128, 128], BF16)[:])
    ones_f = Tl(nc.alloc_sbuf_tensor("ones_f", [128, 128], F32)[:])
    scT = Tl(nc.alloc_sbuf_tensor("scT", [128, 32], F32)[:])
    modT = Tl(nc.alloc_sbuf_tensor("modT", [128, 48, 2], F32)[:])
    Amod = Tl(nc.alloc_sbuf_tensor("Amod", [128, 16, 2], F32)[:])
    Gmod = Tl(nc.alloc_sbuf_tensor("Gmod", [128, 16, 2], F32)[:])
    lamk = Tl(nc.alloc_sbuf_tensor("lamk", [128, 2, 32], F32)[:])
    TRIF = cst.ap[:, 0:128]
    TRIB = cst.ap[:, 128:256]
    RROT = cstb.ap[:, 256:384]

    psum_t = [nc.alloc_psum_tensor("ps%d" % i, [128, 2, 512], F32) for i in range(4)]
    PSB = [Tl(psum_t[i // 2][:, i % 2, :], Buf("psb%d" % i, excl=True)) for i in range(8)]

    def vcol(l, off, n=1):
        return vecs.ap[:, l * NV + off: l * NV + off + n]

    S.dma("sp", lambda e: e.dma_start(out=vecs.ap, in_=vecs_d), writes=[vecs.buf])
    S.dma("sp", lambda e: e.dma_start(out=gbB.ap, in_=gbB_d), writes=[gbB.buf])
    S.dma("sp", lambda e: e.dma_start(out=cst.ap, in_=cst_d), writes=[cst.buf])
    S.dma("pool", lambda e: e.dma_start(out=cstb.ap, in_=cst_d), writes=[cstb.buf])
    S.dma("sp", lambda e: e.dma_start(out=scT.ap, in_=cTd), writes=[scT.buf])
    S.op("dve", lambda e: e.memset(ones_b.ap, 1.0), writes=[ones_b.buf])
    S.op("dve", lambda e: e.memset(ones_f.ap, 1.0), writes=[ones_f.buf])
    S.op("act", lambda e: e.activation(out=scT.ap, in_=scT.ap, func=AF.Silu), reads=[scT.buf], writes=[scT.buf])

    def load_w(src2d, n=128):
        wt = WP.next()
        S.dma("pool", lambda e: e.dma_start(out=wt.ap[:, :, 0:n], in_=src2d.rearrange("(k p) c -> p k c", p=128)),
              writes=[wt.buf])
        return wt

    def proj_fm(wt, n, ti, ps):
        t0, w = TT[ti]

        def f(e):
            for kc in range(NCH):
                ins = e.matmul(ps.ap[0:n, 0:w], lhsT=wt.ap[:, kc, 0:n], rhs=hxT[:, kc, t0:t0 + w],
                               start=(kc == 0), stop=(kc == NCH - 1))
            return ins
        S.op("pe", f, reads=[wt.buf, hx_b[ti]], writes=[ps.buf])

    def proj_fm_rng(wt, n, t0, w, ps):
        def f(e):
            for kc in range(NCH):
                ins = e.matmul(ps.ap[0:n, 0:w], lhsT=wt.ap[:, kc, 0:n], rhs=hxT[:, kc, t0:t0 + w],
                               start=(kc == 0), stop=(kc == NCH - 1))
            return ins
        S.op("pe", f, reads=[wt.buf] + hx_all, writes=[ps.buf])

    evac_flip = [0]

    def evac(out_ap, out_buf, ps, in_ap, scale=None):
        evac_flip[0] ^= 1
        if evac_flip[0]:
            if scale is None:
                S.op("act", lambda e: e.activation(out=out_ap, in_=in_ap, func=AF.Copy), reads=[ps.buf],
                     writes=[out_buf])
            else:
                S.op("act", lambda e: e.activation(out=out_ap, in_=in_ap, func=AF.Copy, scale=scale),
                     reads=[ps.buf], writes=[out_buf])
        else:
            if scale is None:
                S.op("dve", lambda e: e.tensor_copy(out=out_ap, in_=in_ap), reads=[ps.buf], writes=[out_buf])
            else:
                S.op("dve", lambda e: e.tensor_scalar(out=out_ap, in0=in_ap, scalar1=scale, scalar2=None,
                                                      op0=ALU.mult), reads=[ps.buf], writes=[out_buf])

    def phase0(l):
        AR.reset()
        PSG = Rot(PSB)
        wts = [AR.f32(6144) for _ in range(4)]
        acc = AR.f32(96)
        for kc in range(NCH):
            wt = wts[kc % 4]
            S.dma("sp", lambda e, wt=wt, kc=kc: e.dma_start(out=wt.ap, in_=ada_w[l, kc * 128:(kc + 1) * 128, :]),
                  writes=[wt.buf])
            ps = PSG.next()

            def f(e, wt=wt, kc=kc, ps=ps):
                for j in range(48):
                    ins = e.matmul(ps.ap[:, 2 * j:2 * j + 2], lhsT=wt.ap[:, j * 128:(j + 1) * 128],
                                   rhs=scT.ap[:, 2 * kc:2 * kc + 2], start=True, stop=True)
                return ins
            S.op("pe", f, reads=[wt.buf, scT.buf], writes=[ps.buf])
            if kc == 0:
                S.op("dve", lambda e, ps=ps: e.tensor_copy(out=acc.ap, in_=ps.ap[:, 0:96]), reads=[ps.buf],
                     writes=[acc.buf])
            else:
                S.op("dve", lambda e, ps=ps: e.tensor_tensor(out=acc.ap, in0=ps.ap[:, 0:96], in1=acc.ap, op=ALU.add),
                     reads=[ps.buf, acc.buf], writes=[acc.buf])
        accv = acc.ap.rearrange("p (j i) -> p j i", i=2)
        for i in range(2):
            S.op("dve", lambda e, i=i: e.tensor_tensor(out=modT.ap[:, :, i], in0=accv[:, :, i],
                                                       in1=vcol(l, V_ADAB, 48), op=ALU.add),
                 reads=[acc.buf, vecs.buf], writes=[modT.buf])
        for i in range(2):
            S.op("dve", lambda e, i=i: e.scalar_tensor_tensor(out=Amod.ap[:, :, i], in0=modT.ap[:, 16:32, i],
                                                              scalar=1.0, in1=vcol(l, V_NPRE, 16),
                                                              op0=ALU.add, op1=ALU.mult),
                 reads=[modT.buf, vecs.buf], writes=[Amod.buf])
            S.op("dve", lambda e, i=i: e.tensor_tensor(out=Gmod.ap[:, :, i], in0=modT.ap[:, 32:48, i],
                                                       in1=vcol(l, V_NPOST, 16), op=ALU.mult),
                 reads=[modT.buf, vecs.buf], writes=[Gmod.buf])
        tmp = AR.f32(32)
        S.op("act", lambda e: e.activation(out=tmp.ap, in_=vcol(l, V_LAM, 32), func=AF.Exp, scale=-1.0),
             reads=[vecs.buf], writes=[tmp.buf])
        S.op("act", lambda e: e.activation(out=tmp.ap, in_=tmp.ap, func=AF.Ln, bias=1.0),
             reads=[tmp.buf], writes=[tmp.buf])
        S.op("dve", lambda e: e.tensor_scalar(out=lamk.ap[:, 0, :], in0=tmp.ap, scalar1=-8.0, scalar2=None,
                                              op0=ALU.mult), reads=[tmp.buf], writes=[lamk.buf])
        S.op("dve", lambda e: e.tensor_scalar(out=lamk.ap[:, 1, :], in0=tmp.ap, scalar1=-16.0, scalar2=None,
                                              op0=ALU.mult), reads=[tmp.buf], writes=[lamk.buf])

    def phaseA(l):
        AR.reset()
        PSG = Rot(PSB)
        xts_ = Rot([AR.f32(16, 512) for _ in range(2)])
        rstds_ = Rot([AR.f32(512) for _ in range(2)])
        sqs = Rot([AR.bf16(512) for _ in range(3)])
        tmps = Rot([AR.f32(512) for _ in range(3)])
        for ti, (t0, w) in enumerate(TT):
            i = 1 if ti == 0 else 0
            xt = xts_.next()
            rstd = rstds_.next()
            if l == 0:
                S.dma("sp", lambda e, t0=t0, w=w, xt=xt: e.dma_start(out=xt.ap[:, :, 0:w],
                                                             in_=xT0[:, t0:t0 + w].rearrange("(k p) t -> p k t", p=128)),
                      writes=[xt.buf])
            else:
                for mt in range(t0 // 256, (t0 + w) // 256):
                    o = mt * 256 - t0
                    S.dma("sp", lambda e, mt=mt, o=o, xt=xt: e.dma_start(
                        out=xt.ap[:, :, o:o + 256], in_=xres[mt].rearrange("p (k t) -> p k t", k=NCH)),
                        reads=[xres_b[mt]], writes=[xt.buf])
            ps = PSG.next()
            for kc in range(NCH):
                sq = sqs.next()
                S.op("act", lambda e, sq=sq, kc=kc, w=w, xt=xt: e.activation(out=sq.ap[:, 0:w], in_=xt.ap[:, kc, 0:w],
                                                                            func=AF.Square),
                     reads=[xt.buf], writes=[sq.buf])
                S.op("pe", lambda e, sq=sq, kc=kc, w=w, ps=ps: e.matmul(ps.ap[:, 0:w], lhsT=ones_b.ap, rhs=sq.ap[:, 0:w],
                                                                       start=(kc == 0), stop=(kc == NCH - 1)),
                     reads=[sq.buf, ones_b.buf], writes=[ps.buf])
            S.op("act", lambda e, w=w, ps=ps, rstd=rstd: e.activation(out=rstd.ap[:, 0:w], in_=ps.ap[:, 0:w], func=AF.Ln,
                                                                     scale=1.0 / D, bias=EPS),
                 reads=[ps.buf], writes=[rstd.buf])
            S.op("act", lambda e, w=w, rstd=rstd: e.activation(out=rstd.ap[:, 0:w], in_=rstd.ap[:, 0:w], func=AF.Exp,
                                                              scale=-0.5),
                 reads=[rstd.buf], writes=[rstd.buf])
            for kc in range(NCH):
                tp = tmps.next()
                S.op("dve", lambda e, tp=tp, kc=kc, w=w, xt=xt, rstd=rstd: e.tensor_tensor(
                    out=tp.ap[:, 0:w], in0=xt.ap[:, kc, 0:w], in1=rstd.ap[:, 0:w], op=ALU.mult),
                     reads=[xt.buf, rstd.buf], writes=[tp.buf])
                S.op("act", lambda e, tp=tp, kc=kc, w=w, t0=t0, i=i: e.activation(
                    out=hxT[:, kc, t0:t0 + w], in_=tp.ap[:, 0:w], func=AF.Identity,
                    scale=Amod.ap[:, kc, i:i + 1], bias=modT.ap[:, kc, i:i + 1]),
                    reads=[tp.buf, Amod.buf, modT.buf], writes=[hx_b[ti]])

    def phase_lru(l):
        AR.reset()
        PSG = Rot(PSB)
        XP = 2312
        xsets = Rot([(AR.f32(XP), AR.f32(T), AR.bf16(T)) for _ in range(1)])
        aFs = [AR.f32(T), AR.f32(T)]
        bFs = [AR.f32(T), AR.f32(T)]
        tFs = [AR.f32(T), AR.f32(T)]
        hs = AR.f32(T)
        wgs = Rot([AR.bf16(4, 128) for _ in range(3)])
        szs = Rot([AR.f32(512) for _ in range(5)])
        ybs = Rot([AR.bf16(512) for _ in range(2)])
        for (xp_, _, _) in xsets.items:
            S.op("dve", lambda e, xp_=xp_: e.memset(xp_.ap, 0.0), writes=[xp_.buf])
        segs = [(0, 0, TCX), (259, TCX, TLAT)]

        def stage_x(n):
            xpad, xc, xcb = xsets.next()
            wx = load_w(w_in[l, :, O_LX + n * 128:O_LX + (n + 1) * 128])
            wz = load_w(w_in[l, :, O_LZ + n * 128:O_LZ + (n + 1) * 128])
            wg = wgs.next()
            S.dma("pool", lambda e: e.dma_start(out=wg.ap[:, 0:2, :], in_=lru_wr[l, :, n].rearrange("r c d -> c r d")),
                  writes=[wg.buf])
            S.dma("pool", lambda e: e.dma_start(out=wg.ap[:, 2:4, :], in_=lru_wi[l, :, n].rearrange("r c d -> c r d")),
                  writes=[wg.buf])
            for ti, (t0, w) in enumerate(TT):
                ps = PSG.next()
                proj_fm(wx, 128, ti, ps)
                off = 2 + t0 if ti == 0 else 259 + 2 + (t0 - TCX)
                evac(xpad.ap[:, off:off + w], xpad.buf, ps, ps.ap[:, 0:w])
            for (po, t0, ln) in segs:
                S.op("dve", lambda e, po=po, t0=t0, ln=ln: e.tensor_scalar(
                    out=xc.ap[:, t0:t0 + ln], in0=xpad.ap[:, po:po + ln], scalar1=vcol(l, V_CW + n * 4 + 0),
                    scalar2=vcol(l, V_CB + n), op0=ALU.mult, op1=ALU.add),
                    reads=[xpad.buf, vecs.buf], writes=[xc.buf])
                for k in range(1, 4):
                    S.op("dve", lambda e, po=po, t0=t0, ln=ln, k=k: e.scalar_tensor_tensor(
                        out=xc.ap[:, t0:t0 + ln], in0=xpad.ap[:, po + k:po + k + ln], scalar=vcol(l, V_CW + n * 4 + k),
                        in1=xc.ap[:, t0:t0 + ln], op0=ALU.mult, op1=ALU.add),
                        reads=[xpad.buf, vecs.buf, xc.buf], writes=[xc.buf])
            S.op("act", lambda e: e.activation(out=xcb.ap, in_=xc.ap, func=AF.Copy), reads=[xc.buf], writes=[xcb.buf])
            return (xc, xcb, wz, wg)

        def stage_g(n, ctx):
            xc, xcb, wz, wg = ctx
            szt = []
            for ti, (t0, w) in enumerate(TT):
                ps = PSG.next()
                proj_fm(wz, 128, ti, ps)
                sz = szs.next()
                S.op("act", lambda e, sz=sz, ps=ps, w=w: e.activation(out=sz.ap[:, 0:w], in_=ps.ap[:, 0:w], func=AF.Silu),
                     reads=[ps.buf], writes=[sz.buf])
                szt.append(sz)

            def do_dir(dr):
                aF, bF, tF = aFs[dr], bFs[dr], tFs[dr]
                for ti, (t0, w) in enumerate(TT):
                    psr = PSG.next()
                    S.op("pe", lambda e, psr=psr, dr=dr, t0=t0, w=w: e.matmul(
                        psr.ap[:, 0:w], lhsT=wg.ap[:, dr, :], rhs=xcb.ap[:, t0:t0 + w], start=True, stop=True),
                        reads=[wg.buf, xcb.buf], writes=[psr.buf])
                    S.op("act", lambda e, psr=psr, dr=dr, t0=t0, w=w: e.activation(
                        out=aF.ap[:, t0:t0 + w], in_=psr.ap[:, 0:w], func=AF.Sigmoid,
                        bias=vcol(l, V_BR + dr * 16 + n)), reads=[psr.buf, vecs.buf], writes=[aF.buf])
                    psi = PSG.next()
                    S.op("pe", lambda e, psi=psi, dr=dr, t0=t0, w=w: e.matmul(
                        psi.ap[:, 0:w], lhsT=wg.ap[:, 2 + dr, :], rhs=xcb.ap[:, t0:t0 + w], start=True, stop=True),
                        reads=[wg.buf, xcb.buf], writes=[psi.buf])
                    S.op("act", lambda e, psi=psi, dr=dr, t0=t0, w=w: e.activation(
                        out=bF.ap[:, t0:t0 + w], in_=psi.ap[:, 0:w], func=AF.Sigmoid,
                        bias=vcol(l, V_BI + dr * 16 + n)), reads=[psi.buf, vecs.buf], writes=[bF.buf])
                k1 = lamk.ap[:, 0, dr * 16 + n:dr * 16 + n + 1]
                k2 = lamk.ap[:, 1, dr * 16 + n:dr * 16 + n + 1]
                S.op("act", lambda e, k2=k2: e.activation(out=tF.ap, in_=aF.ap, func=AF.Exp, scale=k2),
                     reads=[aF.buf, lamk.buf], writes=[tF.buf])
                S.op("act", lambda e, k1=k1: e.activation(out=aF.ap, in_=aF.ap, func=AF.Exp, scale=k1),
                     reads=[aF.buf, lamk.buf], writes=[aF.buf])
                S.op("act", lambda e: e.activation(out=tF.ap, in_=tF.ap, func=AF.Sqrt, scale=-1.0, bias=1.0),
                     reads=[tF.buf], writes=[tF.buf])
                S.op("dve", lambda e: e.tensor_tensor(out=bF.ap, in0=bF.ap, in1=tF.ap, op=ALU.mult),
                     reads=[bF.buf, tF.buf], writes=[bF.buf])
                S.op("dve", lambda e: e.tensor_tensor(out=bF.ap, in0=bF.ap, in1=xc.ap, op=ALU.mult),
                     reads=[bF.buf, xc.buf], writes=[bF.buf])
                if dr == 0:
                    S.op("dve", lambda e: e.tensor_tensor_scan(out=hs.ap[:, 0:TCX], data0=aF.ap[:, 0:TCX],
                                                               data1=bF.ap[:, 0:TCX], initial=0.0,
                                                               op0=ALU.mult, op1=ALU.add),
                         reads=[aF.buf, bF.buf], writes=[hs.buf])
                    S.op("dve", lambda e: e.tensor_tensor_scan(out=hs.ap[:, TCX:T], data0=aF.ap[:, TCX:T],
                                                               data1=bF.ap[:, TCX:T], initial=hs.ap[:, TCX - 1:TCX],
                                                               op0=ALU.mult, op1=ALU.add),
                         reads=[aF.buf, bF.buf, hs.buf], writes=[hs.buf])
                else:
                    S.op("dve", lambda e: e.tensor_tensor_scan(out=tF.ap[:, 0:TCX][:, ::-1],
                                                               data0=aF.ap[:, 0:TCX][:, ::-1],
                                                               data1=bF.ap[:, 0:TCX][:, ::-1], initial=0.0,
                                                               op0=ALU.mult, op1=ALU.add),
                         reads=[aF.buf, bF.buf], writes=[tF.buf])
                    S.op("dve", lambda e: e.tensor_tensor_scan(out=tF.ap[:, TCX:T][:, ::-1],
                                                               data0=aF.ap[:, TCX:T][:, ::-1],
                                                               data1=bF.ap[:, TCX:T][:, ::-1],
                                                               initial=tF.ap[:, 0:1],
                                                               op0=ALU.mult, op1=ALU.add),
                         reads=[aF.buf, bF.buf, tF.buf], writes=[tF.buf])
                    S.op("dve", lambda e: e.tensor_tensor(out=hs.ap, in0=hs.ap, in1=tF.ap, op=ALU.add),
                         reads=[hs.buf, tF.buf], writes=[hs.buf])
            for dr in range(2):
                do_dir(dr)
            for ti, (t0, w) in enumerate(TT):
                sz = szt[ti]
                yb = ybs.next()
                S.op("dve", lambda e, sz=sz, yb=yb, t0=t0, w=w: e.tensor_tensor(out=yb.ap[:, 0:w], in0=hs.ap[:, t0:t0 + w],
                                                                               in1=sz.ap[:, 0:w], op=ALU.mult),
                     reads=[hs.buf, sz.buf], writes=[yb.buf])
                S.dma("sp", lambda e, yb=yb, t0=t0, w=w: e.dma_start(out=Yd[0, n * 128:(n + 1) * 128, t0:t0 + w],
                                                                    in_=yb.ap[:, 0:w]),
                      reads=[yb.buf], writes=[Y_b[0]])

        for n in range(NCH):
            stage_g(n, stage_x(n))

    def phase_ml(l):
        AR.reset()
        PSG = Rot([PSB[0], PSB[1], PSB[4], PSB[5]])
        PSE = PSB[0]
        PSA = Rot([PSB[1], PSB[4], PSB[5]])
        PSS = [(PSB[6], PSB[7], psum_t[3]), (PSB[2], PSB[3], psum_t[1])]
        qT = AR.bf16(2, T)
        kT = AR.bf16(2, T)
        ktm = AR.bf16(NTILE, 256)
        vtm = AR.bf16(NTILE, 256)
        hD = [AR.bf16(2, T), AR.bf16(2, T)]
        Gtm = AR.f32(NTILE, 32)
        lftm = AR.f32(NTILE, 32)
        rtm = AR.f32(NTILE, 2, 8)
        sLa = AR.f32(NTILE, 32)
        Ecs = [Rot([AR.f32(128) for _ in range(2)]) for _ in range(2)]
        rhc = Rot([AR.f32(128) for _ in range(2)])
        ST32 = [AR.f32(2, 384), AR.f32(2, 384)]
        STb = [AR.bf16(2, 384), AR.bf16(2, 384)]
        tmpC = Rot([AR.f32(2, 384) for _ in range(2)])
        PTs = Rot([AR.bf16(128) for _ in range(4)])
        kps = Rot([AR.bf16(256) for _ in range(3)])
        t128 = Rot([AR.f32(128) for _ in range(4)])
        qss = [Rot([AR.bf16(2, 128) for _ in range(2)]) for _ in range(2)]
        e256 = Rot([AR.f32(512) for _ in range(4)])
        sq256 = Rot([AR.bf16(512) for _ in range(1)])
        yb256 = Rot([AR.bf16(512) for _ in range(2)])
        hg256 = Rot([AR.f32(2, 512) for _ in range(1)])

        wgt = load_w(w_in[l, :, O_MG:O_MG + 32], n=32)
        for c in range(NTILE):
            ps = PSG.next()

            def f(e, c=c, ps=ps):
                for kc in range(NCH):
                    ins = e.matmul(ps.ap[:, 0:32], lhsT=hxT[:, kc, c * 128:(c + 1) * 128], rhs=wgt.ap[:, kc, 0:32],
                                   start=(kc == 0), stop=(kc == NCH - 1))
                return ins
            S.op("pe", f, reads=[wgt.buf] + hx_all, writes=[ps.buf])
            S.op("dve", lambda e, c=c, ps=ps: e.tensor_tensor(out=Gtm.ap[:, c, :], in0=ps.ap[:, 0:32],
                                                              in1=gbB.ap[:, l * 32:(l + 1) * 32], op=ALU.add),
                 reads=[ps.buf, gbB.buf], writes=[Gtm.buf])
        S.op("act", lambda e: e.activation(out=lftm.ap, in_=Gtm.ap, func=AF.Exp, scale=-1.0),
             reads=[Gtm.buf], writes=[lftm.buf])
        S.op("act", lambda e: e.activation(out=lftm.ap, in_=lftm.ap, func=AF.Ln, bias=1.0),
             reads=[lftm.buf], writes=[lftm.buf])
        lf2 = lftm.ap.rearrange("p c g -> p (c g)")
        for (a, b) in ((0, 288), (288, 576)):
            ps = PSG.next()
            S.op("pe", lambda e, ps=ps, a=a, b=b: e.matmul(ps.ap[:, 0:b - a], lhsT=ones_f.ap, rhs=lf2[:, a:b],
                                                           start=True, stop=True),
                 reads=[ones_f.buf, lftm.buf], writes=[ps.buf])
            S.op("act", lambda e, ps=ps, a=a, b=b: e.activation(
                out=sLa.ap.rearrange("p c g -> p (c g)")[:, a:b], in_=ps.ap[:, 0:b - a], func=AF.Exp, scale=-1.0),
                reads=[ps.buf], writes=[sLa.buf])
        for dr, tri in ((0, TRIF), (1, TRIB)):
            for (a, b) in ((0, 288), (288, 576)):
                ps = PSG.next()
                S.op("pe", lambda e, ps=ps, a=a, b=b, tri=tri: e.matmul(ps.ap[:, 0:b - a], lhsT=tri, rhs=lf2[:, a:b],
                                                                        start=True, stop=True),
                     reads=[cst.buf, lftm.buf], writes=[ps.buf])
                c0 = a // 32
                S.op("dve", lambda e, ps=ps, dr=dr, c0=c0: e.tensor_tensor(
                    out=rtm.ap[:, c0:c0 + 9, dr, :],
                    in0=ps.ap[:, 0:288].rearrange("p (c g) -> p c g", g=32)[:, :, dr * 16 + 8:dr * 16 + 16],
                    in1=Gtm.ap[:, c0:c0 + 9, dr * 16:dr * 16 + 8], op=ALU.add),
                    reads=[ps.buf, Gtm.buf], writes=[rtm.buf])
        S.op("act", lambda e: e.activation(out=rtm.ap, in_=rtm.ap, func=AF.Exp), reads=[rtm.buf], writes=[rtm.buf])

        order_b = [1, 0] + list(range(NTILE - 1, 1, -1))
        masks = (TRIF, TRIB)
        for h in range(8):
            wq = [load_w(w_in[l, :, O_MQ + h * 256 + j * 128:O_MQ + h * 256 + (j + 1) * 128]) for j in range(2)]
            wk = [load_w(w_in[l, :, O_MK + h * 256 + j * 128:O_MK + h * 256 + (j + 1) * 128]) for j in range(2)]
            for j in range(2):
                for ti, (t0, w) in enumerate(TT):
                    ps = PSG.next()
                    proj_fm(wq[j], 128, ti, ps)
                    evac(qT.ap[:, j, t0:t0 + w], qT.buf, ps, ps.ap[:, 0:w])
                    ps = PSG.next()
                    proj_fm(wk[j], 128, ti, ps)
                    evac(kT.ap[:, j, t0:t0 + w], kT.buf, ps, ps.ap[:, 0:w], scale=0.0625)
            for c in range(NTILE):
                ps = PSG.next()

                def f(e, c=c, ps=ps, wk=wk):
                    for j in range(2):
                        for kc in range(NCH):
                            ins = e.matmul(ps.ap[:, j * 128:(j + 1) * 128], lhsT=hxT[:, kc, c * 128:(c + 1) * 128],
                                           rhs=wk[j].ap[:, kc, :], start=(kc == 0), stop=(kc == NCH - 1))
                    return ins
                S.op("pe", f, reads=[wk[0].buf, wk[1].buf] + hx_all, writes=[ps.buf])
                evac(ktm.ap[:, c, :], ktm.buf, ps, ps.ap[:, 0:256], scale=0.0625)
            wv = [load_w(w_in[l, :, O_MV + h * 256 + j * 128:O_MV + h * 256 + (j + 1) * 128]) for j in range(2)]
            for c in range(NTILE):
                ps = PSG.next()

                def f(e, c=c, ps=ps, wv=wv):
                    for j in range(2):
                        for kc in range(NCH):
                            ins = e.matmul(ps.ap[:, j * 128:(j + 1) * 128], lhsT=hxT[:, kc, c * 128:(c + 1) * 128],
                                           rhs=wv[j].ap[:, kc, :], start=(kc == 0), stop=(kc == NCH - 1))
                    return ins
                S.op("pe", f, reads=[wv[0].buf, wv[1].buf] + hx_all, writes=[ps.buf])
                evac(vtm.ap[:, c, :], vtm.buf, ps, ps.ap[:, 0:256])
            def dmm(info, dr):
                kp, c = info["kp"], info["c"]
                pb0, pb1, pst = PSS[dr]

                def fD(e, kp=kp, c=c, pst=pst):
                    for j in range(2):
                        e.matmul(pst[:, j, 0:256], lhsT=kp.ap[:, j * 128:(j + 1) * 128], rhs=vtm.ap[:, c, :],
                                 start=True, stop=True)
                        ins = e.matmul(pst[:, j, 256:384], lhsT=kp.ap[:, j * 128:(j + 1) * 128], rhs=ones_b.ap,
                                       start=True, stop=True)
                    return ins
                S.op("pe", fD, reads=[kp.buf, vtm.buf, ones_b.buf], writes=[pb0.buf, pb1.buf])

            def P1(step, dr):
                c = step if dr == 0 else order_b[step]
                cs = slice(c * 128, (c + 1) * 128)
                fcol = dr * 16 + 8 + h
                rcol = rtm.ap[:, c, dr, h:h + 1]
                rh = rhc.next()
                S.op("act", lambda e: e.activation(out=rh.ap, in_=masks[dr], func=AF.Identity,
                                                   scale=lftm.ap[:, c, fcol:fcol + 1]),
                     reads=[cst.buf, lftm.buf], writes=[rh.buf])
                S.op("pe", lambda e: e.matmul(PSE.ap[:, 0:128], lhsT=ones_f.ap, rhs=rh.ap, start=True, stop=True),
                     reads=[ones_f.buf, rh.buf], writes=[PSE.buf])
                Ec = Ecs[dr].next()
                S.op("act", lambda e: e.activation(out=Ec.ap, in_=PSE.ap[:, 0:128], func=AF.Exp, scale=-1.0),
                     reads=[PSE.buf], writes=[Ec.buf])
                qs = qss[dr].next()
                S.op("dve", lambda e: e.tensor_tensor(out=qs.ap, in0=qT.ap[:, :, cs],
                                                      in1=Ec.ap.unsqueeze(1).broadcast_to([128, 2, 128]), op=ALU.mult),
                     reads=[qT.buf, Ec.buf], writes=[qs.buf])
                pa = PSA.next()

                def fS(e):
                    for j in range(2):
                        ins = e.matmul(pa.ap[:, 0:128], lhsT=kT.ap[:, j, cs], rhs=qs.ap[:, j, :],
                                       start=(j == 0), stop=(j == 1))
                    return ins
                S.op("pe", fS, reads=[kT.buf, qs.buf], writes=[pa.buf])
                PT = PTs.next()
                S.op("dve", lambda e: e.scalar_tensor_tensor(out=PT.ap, in0=pa.ap[:, 0:128], scalar=rcol, in1=masks[dr],
                                                             op0=ALU.mult, op1=ALU.mult),
                     reads=[pa.buf, rtm.buf, cst.buf], writes=[PT.buf])
                info = dict(c=c, cs=cs, fcol=fcol, qs=qs, pa=pa, PT=PT)
                if step < NTILE - 1:
                    kp = kps.next()
                    S.op("act", lambda e: e.activation(out=kp.ap, in_=ktm.ap[:, c, :], func=AF.Identity, scale=rcol),
                         reads=[ktm.buf, rtm.buf], writes=[kp.buf])
                    info["kp"] = kp
                return info

            def P2(step, dr, info, nxt_info):
                c, cs, fcol, qs, pa, PT = (info[k] for k in ("c", "cs", "fcol", "qs", "pa", "PT"))
                first = (step == 0)
                stb = STb[dr]

                def fU(e):
                    for ec in range(2):
                        o = pa.ap[:, 128 + ec * 128:256 + ec * 128]
                        ins = e.matmul(o, lhsT=vtm.ap[:, c, ec * 128:(ec + 1) * 128], rhs=PT.ap, start=True, stop=first)
                        if not first:
                            for j in range(2):
                                ins = e.matmul(o, lhsT=stb.ap[:, j, ec * 128:(ec + 1) * 128], rhs=qs.ap[:, j, :],
                                               start=False, stop=(j == 1))
                    o = pa.ap[:, 384:512]
                    ins = e.matmul(o, lhsT=ones_b.ap, rhs=PT.ap, start=True, stop=first)
                    if not first:
                        for j in range(2):
                            ins = e.matmul(o, lhsT=stb.ap[:, j, 256:384], rhs=qs.ap[:, j, :], start=False, stop=(j == 1))
                    return ins
                S.op("pe", fU, reads=[vtm.buf, PT.buf, stb.buf, qs.buf, ones_b.buf], writes=[pa.buf])
                if step < NTILE - 1:
                    pb0, pb1, pst = PSS[dr]
                    sl = sLa.ap[:, c, fcol:fcol + 1]
                    st32 = ST32[dr]
                    if first:
                        S.op("dve", lambda e: e.tensor_scalar(out=stb.ap, in0=pst[:, :, 0:384], scalar1=sl, scalar2=None,
                                                              op0=ALU.mult),
                             reads=[pb0.buf, pb1.buf, sLa.buf], writes=[stb.buf])
                        S.op("act", lambda e: e.activation(out=st32.ap, in_=pst[:, :, 0:384], func=AF.Identity, scale=sl),
                             reads=[pb0.buf, pb1.buf, sLa.buf], writes=[st32.buf])
                    else:
                        tc_ = tmpC.next()
                        S.op("dve", lambda e: e.tensor_tensor(out=tc_.ap, in0=pst[:, :, 0:384], in1=st32.ap, op=ALU.add),
                             reads=[pb0.buf, pb1.buf, st32.buf], writes=[tc_.buf])
                        S.op("dve", lambda e: e.tensor_scalar(out=stb.ap, in0=tc_.ap, scalar1=sl, scalar2=None,
                                                              op0=ALU.mult),
                             reads=[tc_.buf, sLa.buf], writes=[stb.buf])
                        S.op("act", lambda e: e.activation(out=st32.ap, in_=tc_.ap, func=AF.Identity, scale=sl),
                             reads=[tc_.buf, sLa.buf], writes=[st32.buf])
                    if nxt_info is not None and "kp" in nxt_info:
                        dmm(nxt_info, dr)
                am = t128.next()
                S.op("dve", lambda e: e.tensor_scalar(out=am.ap, in0=pa.ap[:, 384:512], scalar1=-1.0, scalar2=1.0,
                                                      op0=ALU.mult, op1=ALU.max),
                     reads=[pa.buf], writes=[am.buf])
                S.op("dve", lambda e: e.scalar_tensor_tensor(out=am.ap, in0=pa.ap[:, 384:512], scalar=1.0, in1=am.ap,
                                                             op0=ALU.max, op1=ALU.max),
                     reads=[pa.buf, am.buf], writes=[am.buf])
                S.op("act", lambda e: e.activation(out=am.ap, in_=am.ap, func=AF.Ln), reads=[am.buf], writes=[am.buf])
                S.op("act", lambda e: e.activation(out=am.ap, in_=am.ap, func=AF.Exp, scale=-1.0),
                     reads=[am.buf], writes=[am.buf])
                hd = hD[dr]
                S.op("dve", lambda e: e.tensor_tensor(
                    out=hd.ap[:, :, cs], in0=pa.ap[:, 128:384].rearrange("p (a b) -> p a b", a=2),
                    in1=am.ap.unsqueeze(1).broadcast_to([128, 2, 128]), op=ALU.mult),
                    reads=[pa.buf, am.buf], writes=[hd.buf])

            cur = [P1(0, 0), P1(0, 1)]
            for dr in range(2):
                dmm(cur[dr], dr)
            for step in range(NTILE):
                nxt = [P1(step + 1, 0), P1(step + 1, 1)] if step + 1 < NTILE else [None, None]
                for dr in range(2):
                    P2(step, dr, cur[dr], nxt[dr])
                cur = nxt
            wo = [load_w(w_in[l, :, O_MO + h * 256 + j * 128:O_MO + h * 256 + (j + 1) * 128]) for j in range(2)]
            wz = [load_w(w_in[l, :, O_MZ + h * 256 + j * 128:O_MZ + h * 256 + (j + 1) * 128]) for j in range(2)]
            for (t0, w) in TT:
                hg = hg256.next()
                pss = PSE
                for ec in range(2):
                    ps = PSA.next()
                    proj_fm_rng(wo[ec], 128, t0, w, ps)
                    so = e256.next()
                    S.op("act", lambda e, so=so, ps=ps, w=w: e.activation(out=so.ap[:, 0:w], in_=ps.ap[:, 0:w],
                                                                         func=AF.Sigmoid),
                         reads=[ps.buf], writes=[so.buf])
                    hsum = e256.next()
                    S.op("dve", lambda e, hsum=hsum, ec=ec, t0=t0, w=w: e.tensor_tensor(
                        out=hsum.ap[:, 0:w], in0=hD[0].ap[:, ec, t0:t0 + w], in1=hD[1].ap[:, ec, t0:t0 + w], op=ALU.add),
                        reads=[hD[0].buf, hD[1].buf], writes=[hsum.buf])
                    S.op("dve", lambda e, hsum=hsum, so=so, hg=hg, ec=ec, w=w: e.tensor_tensor(
                        out=hg.ap[:, ec, 0:w], in0=hsum.ap[:, 0:w], in1=so.ap[:, 0:w], op=ALU.mult),
                        reads=[hsum.buf, so.buf], writes=[hg.buf])
                    sq = sq256.next()
                    S.op("act", lambda e, sq=sq, hg=hg, ec=ec, w=w: e.activation(out=sq.ap[:, 0:w], in_=hg.ap[:, ec, 0:w],
                                                                                func=AF.Square),
                         reads=[hg.buf], writes=[sq.buf])
                    S.op("pe", lambda e, sq=sq, pss=pss, ec=ec, w=w: e.matmul(pss.ap[:, 0:w], lhsT=ones_b.ap,
                                                                             rhs=sq.ap[:, 0:w],
                                                                             start=(ec == 0), stop=(ec == 1)),
                         reads=[sq.buf, ones_b.buf], writes=[pss.buf])
                rs = e256.next()
                S.op("act", lambda e, rs=rs, pss=pss, w=w: e.activation(out=rs.ap[:, 0:w], in_=pss.ap[:, 0:w], func=AF.Ln,
                                                                       scale=1.0 / 256, bias=EPS),
                     reads=[pss.buf], writes=[rs.buf])
                S.op("act", lambda e, rs=rs, w=w: e.activation(out=rs.ap[:, 0:w], in_=rs.ap[:, 0:w], func=AF.Exp,
                                                              scale=-0.5),
                     reads=[rs.buf], writes=[rs.buf])
                for ec in range(2):
                    ps = PSA.next()
                    proj_fm_rng(wz[ec], 128, t0, w, ps)
                    sz = e256.next()
                    S.op("act", lambda e, sz=sz, ps=ps, w=w: e.activation(out=sz.ap[:, 0:w], in_=ps.ap[:, 0:w],
                                                                         func=AF.Silu),
                         reads=[ps.buf], writes=[sz.buf])
                    S.op("dve", lambda e, hg=hg, ec=ec, rs=rs, w=w: e.tensor_tensor(
                        out=hg.ap[:, ec, 0:w], in0=hg.ap[:, ec, 0:w], in1=rs.ap[:, 0:w], op=ALU.mult),
                        reads=[hg.buf, rs.buf], writes=[hg.buf])
                    yb = yb256.next()
                    S.op("dve", lambda e, hg=hg, ec=ec, sz=sz, yb=yb, h=h, w=w: e.scalar_tensor_tensor(
                        out=yb.ap[:, 0:w], in0=hg.ap[:, ec, 0:w], scalar=vcol(l, V_MLN + h * 2 + ec), in1=sz.ap[:, 0:w],
                        op0=ALU.mult, op1=ALU.mult),
                        reads=[hg.buf, sz.buf, vecs.buf], writes=[yb.buf])
                    r0 = h * 256 + ec * 128
                    S.dma("sp", lambda e, yb=yb, r0=r0, t0=t0, w=w: e.dma_start(out=Yd[1, r0:r0 + 128, t0:t0 + w],
                                                                              in_=yb.ap[:, 0:w]),
                          reads=[yb.buf], writes=[Y_b[1]])

    def phase_att(l):
        AR.reset()
        PSG = Rot(PSB[0:2])
        PSSc = Rot(PSB[2:5])
        PSO = Rot(PSB[5:7])
        PSD = Rot(PSB[7:8])
        KT = AR.bf16(4, T)
        Vtm = AR.bf16(NTILE, 512)
        cosr = Rot([AR.f32(512) for _ in range(2)])
        sinr = Rot([AR.f32(512) for _ in range(2)])
        f512 = Rot([AR.f32(512) for _ in range(8)])
        szs = Rot([AR.f32(512) for _ in range(3)])
        rcs = Rot([AR.f32(512) for _ in range(2)])
        b512 = Rot([AR.bf16(512) for _ in range(5)])
        qTs = Rot([AR.bf16(512) for _ in range(2)])
        PTs = Rot([AR.bf16(512) for _ in range(4)])
        ybs = Rot([AR.bf16(512) for _ in range(2)])
        SC = 128.0 ** -0.5

        def load_cs(ti):
            t0, w = TT[ti]
            ct = cosr.next()
            st = sinr.next()
            S.dma("sp", lambda e: e.dma_start(out=ct.ap, in_=cos_d[:, t0 - TCX:t0 - TCX + w]), writes=[ct.buf])
            S.dma("sp", lambda e: e.dma_start(out=st.ap, in_=sin_d[:, t0 - TCX:t0 - TCX + w]), writes=[st.buf])
            return ct, st

        def norm_rope_gen(ps, w, gcol, rope, dst_ap, dst_buf, cs):
            q32 = f512.next()
            S.op("dve", lambda e: e.tensor_copy(out=q32.ap[:, 0:w], in_=ps.ap[:, 0:w]),
                 reads=[ps.buf], writes=[q32.buf])
            sq = b512.next()
            S.op("dve", lambda e: e.tensor_tensor(out=sq.ap[:, 0:w], in0=q32.ap[:, 0:w], in1=q32.ap[:, 0:w], op=ALU.mult),
                 reads=[q32.buf], writes=[sq.buf])
            yield
            ps2 = PSG.next()
            S.op("pe", lambda e: e.matmul(ps2.ap[:, 0:w], lhsT=ones_b.ap, rhs=sq.ap[:, 0:w], start=True, stop=True),
                 reads=[ones_b.buf, sq.buf], writes=[ps2.buf])
            rs = f512.next()
            S.op("act", lambda e: e.activation(out=rs.ap[:, 0:w], in_=ps2.ap[:, 0:w], func=AF.Ln, scale=1.0 / 128,
                                               bias=EPS), reads=[ps2.buf], writes=[rs.buf])
            S.op("act", lambda e: e.activation(out=rs.ap[:, 0:w], in_=rs.ap[:, 0:w], func=AF.Exp, scale=-0.5),
                 reads=[rs.buf], writes=[rs.buf])
            if not rope:
                S.op("dve", lambda e: e.scalar_tensor_tensor(out=dst_ap, in0=q32.ap[:, 0:w], scalar=gcol, in1=rs.ap[:, 0:w],
                                                             op0=ALU.mult, op1=ALU.mult),
                     reads=[q32.buf, rs.buf, vecs.buf], writes=[dst_buf])
                return
            qn = b512.next()
            S.op("dve", lambda e: e.scalar_tensor_tensor(out=qn.ap[:, 0:w], in0=q32.ap[:, 0:w], scalar=gcol,
                                                         in1=rs.ap[:, 0:w], op0=ALU.mult, op1=ALU.mult),
                 reads=[q32.buf, rs.buf, vecs.buf], writes=[qn.buf])
            ct, st = cs
            t2 = f512.next()
            S.op("dve", lambda e: e.tensor_tensor(out=t2.ap[:, 0:w], in0=qn.ap[:, 0:w], in1=ct.ap[:, 0:w], op=ALU.mult),
                 reads=[qn.buf, ct.buf], writes=[t2.buf])
            yield
            ps3 = PSG.next()
            S.op("pe", lambda e: e.matmul(ps3.ap[:, 0:w], lhsT=RROT, rhs=qn.ap[:, 0:w], start=True, stop=True),
                 reads=[cstb.buf, qn.buf], writes=[ps3.buf])
            t1 = f512.next()
            S.op("dve", lambda e: e.tensor_tensor(out=t1.ap[:, 0:w], in0=ps3.ap[:, 0:w], in1=st.ap[:, 0:w], op=ALU.mult),
                 reads=[ps3.buf, st.buf], writes=[t1.buf])
            S.op("dve", lambda e: e.tensor_tensor(out=dst_ap, in0=t1.ap[:, 0:w], in1=t2.ap[:, 0:w], op=ALU.add),
                 reads=[t1.buf, t2.buf], writes=[dst_buf])

        def norm_rope(*a):
            for _ in norm_rope_gen(*a):
                pass

        for g in range(4):
            wk = load_w(w_in[l, :, O_AK + g * 128:O_AK + (g + 1) * 128])
            for ti, (t0, w) in enumerate(TT):
                ps = PSG.next()
                proj_fm(wk, 128, ti, ps)
                cs = load_cs(ti) if ti > 0 else None
                norm_rope(ps, w, vcol(l, V_KN), ti > 0, KT.ap[:, g, t0:t0 + w], KT.buf, cs)
        wv = [load_w(w_in[l, :, O_AV + g * 128:O_AV + (g + 1) * 128]) for g in range(4)]
        for c in range(NTILE):
            ps = PSG.next()

            def f(e, c=c, ps=ps):
                for g in range(4):
                    for kc in range(NCH):
                        ins = e.matmul(ps.ap[:, g * 128:(g + 1) * 128], lhsT=hxT[:, kc, c * 128:(c + 1) * 128],
                                       rhs=wv[g].ap[:, kc, :], start=(kc == 0), stop=(kc == NCH - 1))
                return ins
            S.op("pe", f, reads=[w.buf for w in wv] + hx_all, writes=[ps.buf])
            evac(Vtm.ap[:, c, :], Vtm.buf, ps, ps.ap[:, 0:512])
        tiles = [(h, ti) for h in range(16) for ti in range(len(TT))]
        ready = {}
        wcur = {}

        def prologue(h, ti):
            t0, w = TT[ti]
            if ti == 0:
                wcur["q"] = load_w(w_in[l, :, O_AQ + h * 128:O_AQ + (h + 1) * 128])
                wcur["z"] = load_w(w_in[l, :, O_AZ + h * 128:O_AZ + (h + 1) * 128])
            wq, wz = wcur["q"], wcur["z"]
            ps = PSG.next()
            proj_fm(wq, 128, ti, ps)
            qt = qTs.next()
            cs = load_cs(ti) if ti > 0 else None
            psz = PSG.next()
            proj_fm(wz, 128, ti, psz)
            yield
            gen = norm_rope_gen(ps, w, vcol(l, V_QN), ti > 0, qt.ap[:, 0:w], qt.buf, cs)
            next(gen, None)
            sz = szs.next()
            S.op("act", lambda e: e.activation(out=sz.ap[:, 0:w], in_=psz.ap[:, 0:w], func=AF.Exp, scale=-1.0),
                 reads=[psz.buf], writes=[sz.buf])
            S.op("act", lambda e: e.activation(out=sz.ap[:, 0:w], in_=sz.ap[:, 0:w], func=AF.Ln, bias=1.0),
                 reads=[sz.buf], writes=[sz.buf])
            S.op("act", lambda e: e.activation(out=sz.ap[:, 0:w], in_=sz.ap[:, 0:w], func=AF.Exp, scale=-1.0),
                 reads=[sz.buf], writes=[sz.buf])
            S.op("dve", lambda e: e.tensor_tensor(out=sz.ap[:, 0:w], in0=psz.ap[:, 0:w], in1=sz.ap[:, 0:w], op=ALU.mult),
                 reads=[psz.buf, sz.buf], writes=[sz.buf])
            ready[(h, ti)] = (qt, sz)
            yield
            for _ in gen:
                yield

        for _ in prologue(*tiles[0]):
            pass
        for idx, (h, ti) in enumerate(tiles):
            g = h // 4
            t0, w = TT[ti]
            qt, sz = ready.pop((h, ti))
            nxt = prologue(*tiles[idx + 1]) if idx + 1 < len(tiles) else None
            keys = list(range(NTILE)) if ti > 0 else [0, 1]
            nk = len(keys)
            pO = PSO.next()
            pD = PSD.next()
            pSs = {}

            def emit_S(ki):
                pS = PSSc.next()
                kc_ = keys[ki]
                S.op("pe", lambda e, pS=pS, kc_=kc_, qt=qt, w=w, g=g: e.matmul(
                    pS.ap[:, 0:w], lhsT=KT.ap[:, g, kc_ * 128:(kc_ + 1) * 128], rhs=qt.ap[:, 0:w],
                    start=True, stop=True), reads=[KT.buf, qt.buf], writes=[pS.buf])
                pSs[ki] = pS
            emit_S(0)
            if nk > 1:
                emit_S(1)
            for ki, kc_ in enumerate(keys):
                pS = pSs.pop(ki)
                PT = PTs.next()
                S.op("act", lambda e, pS=pS, PT=PT, w=w: e.activation(out=PT.ap[:, 0:w], in_=pS.ap[:, 0:w],
                                                                     func=AF.Exp, scale=SC),
                     reads=[pS.buf], writes=[PT.buf])
                if ki + 2 < nk:
                    emit_S(ki + 2)
                fst = (ki == 0)
                lst = (ki == nk - 1)

                def fO(e, PT=PT, kc_=kc_, w=w, fst=fst, lst=lst, pO=pO, pD=pD, g=g):
                    e.matmul(pO.ap[:, 0:w], lhsT=Vtm.ap[:, kc_, g * 128:(g + 1) * 128], rhs=PT.ap[:, 0:w],
                             start=fst, stop=lst)
                    return e.matmul(pD.ap[:, 0:w], lhsT=ones_b.ap, rhs=PT.ap[:, 0:w], start=fst, stop=lst)
                S.op("pe", fO, reads=[Vtm.buf, PT.buf, ones_b.buf], writes=[pO.buf, pD.buf])
                if nxt is not None and ki in (1, 4, 8, 12):
                    next(nxt, None)
            if nxt is not None:
                for _ in nxt:
                    pass
            rc = rcs.next()
            S.op("act", lambda e, rc=rc, pD=pD, w=w: e.activation(out=rc.ap[:, 0:w], in_=pD.ap[:, 0:w], func=AF.Ln),
                 reads=[pD.buf], writes=[rc.buf])
            S.op("act", lambda e, rc=rc, w=w: e.activation(out=rc.ap[:, 0:w], in_=rc.ap[:, 0:w], func=AF.Exp, scale=-1.0),
                 reads=[rc.buf], writes=[rc.buf])
            S.op("dve", lambda e, rc=rc, sz=sz, w=w: e.tensor_tensor(out=rc.ap[:, 0:w], in0=rc.ap[:, 0:w],
                                                                    in1=sz.ap[:, 0:w], op=ALU.mult),
                 reads=[rc.buf, sz.buf], writes=[rc.buf])
            yb = ybs.next()
            S.op("dve", lambda e, yb=yb, pO=pO, rc=rc, w=w: e.tensor_tensor(out=yb.ap[:, 0:w], in0=pO.ap[:, 0:w],
                                                                           in1=rc.ap[:, 0:w], op=ALU.mult),
                 reads=[pO.buf, rc.buf], writes=[yb.buf])
            S.dma("sp", lambda e, yb=yb, h=h, t0=t0, w=w: e.dma_start(out=Yd[2, h * 128:(h + 1) * 128, t0:t0 + w],
                                                                     in_=yb.ap[:, 0:w]),
                  reads=[yb.buf], writes=[Y_b[2]])

    def phase_merge(l, last):
        PSG = Rot(PSB)
        for n in range(3 if "nom1" not in phases else 0):
            S.barrier()
            AR.reset()
            Yt = AR.bf16(NCH, T)
            sgs = Rot([AR.f32(512) for _ in range(3)])
            pbs = Rot([AR.bf16(512) for _ in range(3)])
            for kc in range(NCH):
                S.dma("sp", lambda e, n=n, Yt=Yt, kc=kc: e.dma_start(out=Yt.ap[:, kc, :],
                                                                    in_=Yd[n, kc * 128:(kc + 1) * 128, :]),
                      reads=[Y_b[n]], writes=[Yt.buf])
            for dj in range(NCH):
                wb = load_w(w_br[l, n, :, dj * 128:(dj + 1) * 128])
                wg = load_w(w_in[l, :, O_G + n * D + dj * 128:O_G + n * D + (dj + 1) * 128])
                for ti, (t0, w) in enumerate(TT):
                    psP = PSG.next()

                    def f(e, psP=psP, wb=wb, t0=t0, w=w, Yt=Yt):
                        for kc in range(NCH):
                            ins = e.matmul(psP.ap[:, 0:w], lhsT=wb.ap[:, kc, :], rhs=Yt.ap[:, kc, t0:t0 + w],
                                           start=(kc == 0), stop=(kc == NCH - 1))
                        return ins
                    S.op("pe", f, reads=[wb.buf, Yt.buf], writes=[psP.buf])
                    psG = PSG.next()
                    proj_fm(wg, 128, ti, psG)
                    sg = sgs.next()
                    S.op("act", lambda e, sg=sg, psG=psG, w=w: e.activation(out=sg.ap[:, 0:w], in_=psG.ap[:, 0:w],
                                                                           func=AF.Sigmoid),
                         reads=[psG.buf], writes=[sg.buf])
                    pb = pbs.next()
                    S.op("dve", lambda e, pb=pb, psP=psP, sg=sg, w=w: e.tensor_tensor(out=pb.ap[:, 0:w], in0=psP.ap[:, 0:w],
                                                                                     in1=sg.ap[:, 0:w], op=ALU.mult),
                         reads=[psP.buf, sg.buf], writes=[pb.buf])
                    for mt in range(t0 // 256, (t0 + w) // 256):
                        o = mt * 256 - t0
                        S.dma("sp", lambda e, pb=pb, n=n, dj=dj, mt=mt, o=o: e.dma_start(
                            out=Gd[n, mt, :, dj * 256:(dj + 1) * 256], in_=pb.ap[:, o:o + 256]),
                            reads=[pb.buf], writes=[G_b[n]])
        if "nom2" in phases:
            return
        S.barrier()
        AR.reset()
        wout = hxT[:, :, 0:D]
        wo_b = Buf("wout")
        for dj in range(NCH):
            S.dma("pool", lambda e, dj=dj: e.dma_start(
                out=wout[:, :, dj * 128:(dj + 1) * 128],
                in_=w_out[l, :, dj * 128:(dj + 1) * 128].rearrange("(k p) c -> p k c", p=128)), writes=[wo_b])
        Gts = [Rot([AR.bf16(NCH, 256) for _ in range(k_)]) for k_ in (2, 1, 1)]
        yos = Rot([AR.f32(NCH, 256) for _ in range(2)])
        xts = Rot([AR.f32(NCH, 256) for _ in range(2)])
        sqs = Rot([AR.bf16(256) for _ in range(4)])
        rsds = Rot([AR.f32(256) for _ in range(2)])
        tmps = Rot([AR.f32(256) for _ in range(2)])
        PSG7 = Rot(PSB[0:7])
        mts = range(T // 256) if not last else range(1, T // 256)
        for mt in mts:
            t0 = mt * 256
            i = 1 if mt == 0 else 0
            gt = []
            for n in range(3):
                g_ = Gts[n].next()
                S.dma("sp", lambda e, g_=g_, n=n, mt=mt: e.dma_start(
                    out=g_.ap, in_=Gd[n, mt].rearrange("p (k t) -> p k t", k=NCH)),
                    reads=[G_b[n]], writes=[g_.buf])
                gt.append(g_)
            xt = xts.next()
            if l == 0:
                S.dma("sp", lambda e, xt=xt, t0=t0: e.dma_start(
                    out=xt.ap, in_=xT0[:, t0:t0 + 256].rearrange("(k p) t -> p k t", p=128)), writes=[xt.buf])
            else:
                S.dma("sp", lambda e, xt=xt, mt=mt: e.dma_start(
                    out=xt.ap, in_=xres[mt].rearrange("p (k t) -> p k t", k=NCH)), reads=[xres_b[mt]], writes=[xt.buf])
            S.op("dve", lambda e, gt=gt: e.tensor_tensor(out=gt[0].ap, in0=gt[0].ap, in1=gt[1].ap, op=ALU.add),
                 reads=[gt[0].buf, gt[1].buf], writes=[gt[0].buf])
            S.op("dve", lambda e, gt=gt: e.tensor_tensor(out=gt[0].ap, in0=gt[0].ap, in1=gt[2].ap, op=ALU.add),
                 reads=[gt[0].buf, gt[2].buf], writes=[gt[0].buf])
            mg = gt[0]
            yo = yos.next()
            rsd = rsds.next()
            pss = PSB[7]
            pend = []

            def emit_ones(sq, dj, pss=pss):
                S.op("pe", lambda e: e.matmul(pss.ap[:, 0:256], lhsT=ones_b.ap, rhs=sq.ap,
                                              start=(dj == 0), stop=(dj == NCH - 1)),
                     reads=[sq.buf, ones_b.buf], writes=[pss.buf])
            PSG = PSG7
            for dj in range(NCH):
                ps = PSG.next()

                def f(e, ps=ps, dj=dj, mg=mg):
                    for kc in range(NCH):
                        ins = e.matmul(ps.ap[:, 0:256], lhsT=wout[:, kc, dj * 128:(dj + 1) * 128], rhs=mg.ap[:, kc, :],
                                       start=(kc == 0), stop=(kc == NCH - 1))
                    return ins
                S.op("pe", f, reads=[wo_b, mg.buf], writes=[ps.buf])
                S.op("dve", lambda e, ps=ps, dj=dj, yo=yo: e.tensor_copy(out=yo.ap[:, dj, :], in_=ps.ap[:, 0:256]),
                     reads=[ps.buf], writes=[yo.buf])
                sq = sqs.next()
                S.op("act", lambda e, ps=ps, sq=sq: e.activation(out=sq.ap, in_=ps.ap[:, 0:256], func=AF.Square),
                     reads=[ps.buf], writes=[sq.buf])
                pend.append((sq, dj))
                if len(pend) > 2:
                    emit_ones(*pend.pop(0))
            while pend:
                emit_ones(*pend.pop(0))
            S.op("act", lambda e, pss=pss, rsd=rsd: e.activation(out=rsd.ap, in_=pss.ap[:, 0:256], func=AF.Ln,
                                                                scale=1.0 / D, bias=EPS),
                 reads=[pss.buf], writes=[rsd.buf])
            S.op("act", lambda e, rsd=rsd: e.activation(out=rsd.ap, in_=rsd.ap, func=AF.Exp, scale=-0.5),
                 reads=[rsd.buf], writes=[rsd.buf])
            for dj in range(NCH):
                tp = tmps.next()
                S.op("dve", lambda e, tp=tp, dj=dj, yo=yo, rsd=rsd: e.tensor_tensor(out=tp.ap, in0=yo.ap[:, dj, :],
                                                                                  in1=rsd.ap, op=ALU.mult),
                     reads=[yo.buf, rsd.buf], writes=[tp.buf])
                S.op("dve", lambda e, tp=tp, dj=dj, xt=xt, i=i: e.scalar_tensor_tensor(
                    out=xt.ap[:, dj, :], in0=tp.ap, scalar=Gmod.ap[:, dj, i:i + 1], in1=xt.ap[:, dj, :],
                    op0=ALU.mult, op1=ALU.add), reads=[tp.buf, Gmod.buf, xt.buf], writes=[xt.buf])
            if last:
                S.dma("sp", lambda e, xt=xt, t0=t0: e.dma_start(
                    out=out_d[:, t0 - TCX:t0 - TCX + 256].rearrange("(k p) t -> p k t", p=128), in_=xt.ap),
                    reads=[xt.buf], writes=[out_b])
            else:
                S.dma("sp", lambda e, xt=xt, mt=mt: e.dma_start(
                    out=xres[mt].rearrange("p (k t) -> p k t", k=NCH), in_=xt.ap),
                    reads=[xt.buf], writes=[xres_b[mt]])

    for l in range(n_layers):
        last = (l == n_layers - 1) and not debug
        S.barrier()
        phase0(l)
        S.barrier()
        phaseA(l)
        if "lru" in phases:
            S.barrier()
            phase_lru(l)
        if "ml" in phases:
            S.barrier()
            phase_ml(l)
        if "att" in phases:
            S.barrier()
            phase_att(l)
        if "merge" in phases:
            phase_merge(l, last)
    S.barrier()
    S.finalize()
    return nc


def _fm(v):
    v = np.asarray(v, np.float32)
    lead = v.shape[:-1]
    r = v.reshape(lead + (16, 128))
    return np.moveaxis(r, -1, 0)


def _host_consts():
    s = np.arange(128)
    trif = (s[:, None] <= s[None, :]).astype(np.float32)
    trib = (s[:, None] >= s[None, :]).astype(np.float32)
    R = np.zeros((128, 128), np.float32)
    for m in range(64):
        R[m + 64, m] = -1.0
        R[m, m + 64] = 1.0
    ident = np.eye(128, dtype=np.float32)
    cst = np.concatenate([trif, trib, R, ident], axis=1)
    n_freq = 32
    inv = (1.0 / (np.float32(10000.0) ** (np.arange(n_freq, dtype=np.float32) / np.float32(n_freq)))).astype(np.float32)
    row = np.repeat(np.arange(TLAT // 64), 64).astype(np.float32)
    col = np.tile(np.arange(64), TLAT // 64).astype(np.float32)
    ang = np.concatenate([row[:, None] * inv, col[:, None] * inv], axis=-1).astype(np.float32)
    cos = np.cos(ang).astype(np.float32)
    sin = np.sin(ang).astype(np.float32)
    cosT = np.ascontiguousarray(np.concatenate([cos, cos], axis=1).T)
    sinT = np.ascontiguousarray(np.concatenate([sin, sin], axis=1).T)
    return cst, cosT, sinT


def make_in_maps(inputs, cores):
    f32 = np.float32
    L = 4
    vecs = np.zeros((128, L, NV), f32)
    vecs[:, :, V_ADAB:V_ADAB + 48] = np.moveaxis(np.asarray(inputs["ada_b"], f32).reshape(L, 48, 128), -1, 0)
    vecs[:, :, V_NPRE:V_NPRE + 16] = _fm(inputs["norm_pre"])
    vecs[:, :, V_NPOST:V_NPOST + 16] = _fm(inputs["norm_post"])
    cw = _fm(inputs["lru_conv_w"])
    vecs[:, :, V_CW:V_CW + 64] = np.swapaxes(cw, 2, 3).reshape(128, L, 64)
    vecs[:, :, V_CB:V_CB + 16] = _fm(inputs["lru_conv_b"])
    vecs[:, :, V_BR:V_BR + 32] = _fm(inputs["lru_br"]).reshape(128, L, 32)
    vecs[:, :, V_BI:V_BI + 32] = _fm(inputs["lru_bi"]).reshape(128, L, 32)
    vecs[:, :, V_LAM:V_LAM + 32] = _fm(inputs["lru_lam"]).reshape(128, L, 32)
    vecs[:, :, V_MLN:V_MLN + 16] = _fm(inputs["ml_norm"])
    vecs[:, :, V_QN] = np.asarray(inputs["q_norm"], f32).T
    vecs[:, :, V_KN] = np.asarray(inputs["k_norm"], f32).T
    vecs = np.ascontiguousarray(vecs.reshape(128, L * NV))
    gb = np.asarray(inputs["ml_gate_b"], f32).reshape(L, 32)
    gbB = np.ascontiguousarray(np.broadcast_to(gb.reshape(1, L * 32), (128, L * 32)))
    cst, cosT, sinT = _host_consts()
    shared = {
        "ada_w": np.ascontiguousarray(inputs["ada_w"], dtype=f32),
        "w_in": np.ascontiguousarray(inputs["w_in"], dtype=f32),
        "w_br": np.ascontiguousarray(inputs["w_br"], dtype=f32),
        "w_out": np.ascontiguousarray(inputs["w_out"], dtype=f32),
        "lru_wr": np.ascontiguousarray(inputs["lru_wr"], dtype=f32),
        "lru_wi": np.ascontiguousarray(inputs["lru_wi"], dtype=f32),
        "vecs": vecs, "gbB": gbB, "cst": cst, "cosT": cosT, "sinT": sinT,
    }
    maps = []
    cc = np.asarray(inputs["c_ctx"], f32)
    for b in cores:
        xT = np.ascontiguousarray(np.concatenate([np.asarray(inputs["ctx"][b], f32), np.asarray(inputs["x"][b], f32)],
                                                 axis=0).T)
        cT = np.stack([np.asarray(inputs["c"][b], f32), cc], axis=-1)
        cT = np.ascontiguousarray(np.moveaxis(cT.reshape(16, 128, 2), 1, 0).reshape(128, 32))
        m = dict(shared)
        m["xT0"] = xT
        m["cT"] = cT
        maps.append(m)
    return maps


_NC_CACHE = {}


def kernel(**inputs):
    if "full" not in _NC_CACHE:
        _NC_CACHE["full"] = build(4, False)
    nc = _NC_CACHE["full"]
    B = 4
    maps = make_in_maps(inputs, list(range(B)))
    res = run_bass_kernel_spmd(nc, maps, core_ids=list(range(B)))
    out = np.stack([np.ascontiguousarray(res.results[b]["outT"].T) for b in range(B)], axis=0)
    return out.astype(np.float32)
```

```python
import numpy as np
import ml_dtypes
import concourse.bass as bass
import concourse.mybir as mybir
from concourse.bass_utils import run_bass_kernel_spmd

F32 = mybir.dt.float32
BF16 = mybir.dt.bfloat16
AF = mybir.ActivationFunctionType
ALU = mybir.AluOpType

D = 2048
NCH = 16
TCX = 256
TLAT = 2048
T = TCX + TLAT
NTILE = T // 128
NIN = 25632
EPS = 1e-6
TT = [(0, 256), (256, 512), (768, 512), (1280, 512), (1792, 512)]
O_LX, O_LZ, O_MQ, O_MK, O_MV, O_MO, O_MZ, O_MG, O_AQ, O_AK, O_AV, O_AZ, O_G = (
    0, 2048, 4096, 6144, 8192, 10240, 12288, 14336, 14368, 16416, 16928, 17440, 19488)
V_ADAB, V_NPRE, V_NPOST, V_CW, V_CB, V_BR, V_BI, V_LAM, V_MLN, V_QN, V_KN = (
    0, 48, 64, 80, 144, 160, 192, 224, 256, 272, 273)
NV = 274
ARENA_WORDS = 26112


class Buf:
    __slots__ = ("name", "w", "r", "excl")

    def __init__(self, name="", excl=False):
        self.name = name
        self.w = None
        self.r = {}
        self.excl = excl


class Tl:
    __slots__ = ("ap", "buf")

    def __init__(self, ap, buf=None):
        self.ap = ap
        self.buf = buf if buf is not None else Buf()


class Sched:
    COMPUTE = ("pe", "act", "dve", "pool")

    def __init__(self, nc, n_dma_sems=8):
        self.nc = nc
        self.sems = {}
        self.cnt = {}
        self.prog = {e: [] for e in ("pe", "act", "dve", "pool", "sp")}
        for e in self.COMPUTE:
            self.sems[e] = nc.alloc_semaphore("s_" + e)
            self.cnt[e] = 0
        self.dma_rot = {}
        for q in ("sp", "pool"):
            ids = []
            for i in range(n_dma_sems):
                cid = ("dma", q, i)
                self.sems[cid] = nc.alloc_semaphore("d_%s_%d" % (q, i))
                self.cnt[cid] = 0
                ids.append(cid)
            self.dma_rot[q] = [ids, 0]
        self.known = {e: {} for e in self.prog}

    def _deps(self, reads, writes):
        deps = {}
        for b in reads:
            if b.w is not None:
                c, v = b.w
                if deps.get(c, 0) < v:
                    deps[c] = v
        for b in writes:
            if b.w is not None:
                c, v = b.w
                if deps.get(c, 0) < v:
                    deps[c] = v
            for c, v in b.r.items():
                if deps.get(c, 0) < v:
                    deps[c] = v
        return deps

    def _waits(self, eng, deps):
        kn = self.known[eng]
        waits = []
        for c, v in deps.items():
            if c == eng and eng == "pe":
                continue
            if kn.get(c, 0) < v:
                kn[c] = v
                waits.append((self.sems[c], v))
        return waits

    def op(self, eng, fn, reads=(), writes=()):
        ex = [b for b in reads if b.excl]
        if ex:
            writes = list(writes) + ex
        waits = self._waits(eng, self._deps(reads, writes))
        self.cnt[eng] += 1
        val = self.cnt[eng]
        sem = self.sems[eng]

        def run(e, waits=waits, fn=fn, sem=sem):
            for s, v in waits:
                e.wait_ge(s, v)
            fn(e).then_inc(sem, 1)
        self.prog[eng].append(run)
        for b in reads:
            if b.r.get(eng, 0) < val:
                b.r[eng] = val
        for b in writes:
            b.w = (eng, val)
            b.r = {}
        return val

    def dma(self, q, fn, reads=(), writes=()):
        rot = self.dma_rot[q]
        cid = rot[0][rot[1]]
        rot[1] = (rot[1] + 1) % len(rot[0])
        deps = self._deps(reads, writes)
        if self.cnt[cid] > 0:
            deps[cid] = max(deps.get(cid, 0), self.cnt[cid])
        waits = self._waits(q, deps)
        self.cnt[cid] += 16
        val = self.cnt[cid]
        sem = self.sems[cid]

        def run(e, waits=waits, fn=fn, sem=sem):
            for s, v in waits:
                e.wait_ge(s, v)
            fn(e).then_inc(sem, 16)
        self.prog[q].append(run)
        for b in reads:
            b.r[cid] = val
        for b in writes:
            b.w = (cid, val)
            b.r = {}

    def barrier(self):
        deps = {c: v for c, v in self.cnt.items() if v > 0}
        for eng in self.prog:
            waits = self._waits(eng, dict(deps))

            def run(e, waits=waits):
                for s, v in waits:
                    e.wait_ge(s, v)
            self.prog[eng].append(run)

    def finalize(self):
        nc = self.nc
        prog = self.prog
        with nc.Block() as block:
            @block.tensor
            def _(e):
                for f in prog["pe"]:
                    f(e)

            @block.scalar
            def _(e):
                for f in prog["act"]:
                    f(e)

            @block.vector
            def _(e):
                for f in prog["dve"]:
                    f(e)

            @block.gpsimd
            def _(e):
                for f in prog["pool"]:
                    f(e)

            @block.sync
            def _(e):
                for f in prog["sp"]:
                    f(e)


class Rot:
    def __init__(self, items):
        self.items = items
        self.i = 0

    def next(self):
        t = self.items[self.i]
        self.i = (self.i + 1) % len(self.items)
        return t


class Arena:
    def __init__(self, tensor, nwords):
        self.t = tensor
        self.n = nwords
        self.off = 0

    def reset(self):
        self.off = 0

    def _take(self, nw):
        a = self.off
        assert a + nw <= self.n, ("arena overflow", a + nw, self.n)
        self.off += nw
        return self.t[:, a:a + nw]

    def f32(self, *shape):
        n = int(np.prod(shape))
        ap = self._take(n)
        if len(shape) == 2:
            ap = ap.rearrange("p (a b) -> p a b", a=shape[0])
        elif len(shape) == 3:
            ap = ap.rearrange("p (a b c) -> p a b c", a=shape[0], b=shape[1])
        return Tl(ap)

    def bf16(self, *shape):
        n = int(np.prod(shape))
        nw = (n + 1) // 2
        ap = self._take(nw).bitcast(BF16)[:, 0:n]
        if len(shape) == 2:
            ap = ap.rearrange("p (a b) -> p a b", a=shape[0])
        elif len(shape) == 3:
            ap = ap.rearrange("p (a b c) -> p a b c", a=shape[0], b=shape[1])
        return Tl(ap)


def build(n_layers=4, debug=False, phases=("lru", "ml", "att", "merge")):
    nc = bass.Bass("TRN2", target_bir_lowering=False)
    S = Sched(nc)

    def din(name, shape, dt=F32):
        return nc.dram_tensor(name, list(shape), dt, kind="ExternalInput").ap()

    xT0 = din("xT0", [D, T])
    cTd = din("cT", [128, 32])
    ada_w = din("ada_w", [4, D, 6144])
    w_in = din("w_in", [4, D, NIN])
    w_br = din("w_br", [4, 3, D, D])
    w_out = din("w_out", [4, D, D])
    lru_wr = din("lru_wr", [4, 2, 16, 128, 128])
    lru_wi = din("lru_wi", [4, 2, 16, 128, 128])
    vecs_d = din("vecs", [128, 4 * NV])
    gbB_d = din("gbB", [128, 4 * 32])
    cst_d = din("cst", [128, 4 * 128])
    cos_d = din("cosT", [128, TLAT])
    sin_d = din("sinT", [128, TLAT])
    out_d = nc.dram_tensor("outT", [D, TLAT], F32, kind="ExternalOutput").ap()
    ykind = "ExternalOutput" if debug else "Internal"
    xres = nc.dram_tensor("xres", [T // 256, 128, NCH * 256], F32, kind=ykind).ap()
    Yd = nc.dram_tensor("Ybr", [3, D, T], BF16, kind=ykind).ap()
    Gd = nc.dram_tensor("Gbr", [3, T // 256, 128, NCH * 256], BF16, kind=ykind).ap()
    xres_b = [Buf("xres%d" % i) for i in range(9)]
    Y_b = [Buf("Y%d" % i) for i in range(3)]
    G_b = [Buf("G%d" % i) for i in range(3)]
    out_b = Buf("out")

    hxT_t = nc.alloc_sbuf_tensor("hxT", [128, NCH, T], BF16)
    hxT = hxT_t[:]
    hx_b = [Buf("hx%d" % i) for i in range(len(TT))]
    hx_all = hx_b
    WP = Rot([Tl(nc.alloc_sbuf_tensor("wp%d" % i, [128, NCH, 128], BF16)[:]) for i in range(6)])
    arena_t = nc.alloc_sbuf_tensor("arena", [128, ARENA_WORDS], F32)
    AR = Arena(arena_t, ARENA_WORDS)
    vecs = Tl(nc.alloc_sbuf_tensor("vecs_s", [128, 4 * NV], F32)[:])
    gbB = Tl(nc.alloc_sbuf_tensor("gbB_s", [128, 4 * 32], F32)[:])
    cst = Tl(nc.alloc_sbuf_tensor("cst_s", [128, 4 * 128], F32)[:])
    cstb = Tl(nc.alloc_sbuf_tensor("cstb_s", [128, 4 * 128], BF16)[:])
    ones_b = Tl(nc.alloc_sbuf_tensor("ones_b", [128, 128], BF16)[:])
    ones_f = Tl(nc.alloc_sbuf_tensor("ones_f", [128, 128], F32)[:])
    scT = Tl(nc.alloc_sbuf_tensor("scT", [128, 32], F32)[:])
    modT = Tl(nc.alloc_sbuf_tensor("modT", [128, 48, 2], F32)[:])
    Amod = Tl(nc.alloc_sbuf_tensor("Amod", [128, 16, 2], F32)[:])
    Gmod = Tl(nc.alloc_sbuf_tensor("Gmod", [128, 16, 2], F32)[:])
    lamk = Tl(nc.alloc_sbuf_tensor("lamk", [128, 2, 32], F32)[:])
    TRIF = cst.ap[:, 0:128]
    TRIB = cst.ap[:, 128:256]
    RROT = cstb.ap[:, 256:384]

    psum_t = [nc.alloc_psum_tensor("ps%d" % i, [128, 2, 512], F32) for i in range(4)]
    PSB = [Tl(psum_t[i // 2][:, i % 2, :], Buf("psb%d" % i, excl=True)) for i in range(8)]

    def vcol(l, off, n=1):
        return vecs.ap[:, l * NV + off: l * NV + off + n]

    S.dma("sp", lambda e: e.dma_start(out=vecs.ap, in_=vecs_d), writes=[vecs.buf])
    S.dma("sp", lambda e: e.dma_start(out=gbB.ap, in_=gbB_d), writes=[gbB.buf])
    S.dma("sp", lambda e: e.dma_start(out=cst.ap, in_=cst_d), writes=[cst.buf])
    S.dma("pool", lambda e: e.dma_start(out=cstb.ap, in_=cst_d), writes=[cstb.buf])
    S.dma("sp", lambda e: e.dma_start(out=scT.ap, in_=cTd), writes=[scT.buf])
    S.op("dve", lambda e: e.memset(ones_b.ap, 1.0), writes=[ones_b.buf])
    S.op("dve", lambda e: e.memset(ones_f.ap, 1.0), writes=[ones_f.buf])
    S.op("act", lambda e: e.activation(out=scT.ap, in_=scT.ap, func=AF.Silu), reads=[scT.buf], writes=[scT.buf])

    def load_w(src2d, n=128):
        wt = WP.next()
        S.dma("pool", lambda e: e.dma_start(out=wt.ap[:, :, 0:n], in_=src2d.rearrange("(k p) c -> p k c", p=128)),
              writes=[wt.buf])
        return wt

    def proj_fm(wt, n, ti, ps):
        t0, w = TT[ti]

        def f(e):
            for kc in range(NCH):
                ins = e.matmul(ps.ap[0:n, 0:w], lhsT=wt.ap[:, kc, 0:n], rhs=hxT[:, kc, t0:t0 + w],
                               start=(kc == 0), stop=(kc == NCH - 1))
            return ins
        S.op("pe", f, reads=[wt.buf, hx_b[ti]], writes=[ps.buf])

    def proj_fm_rng(wt, n, t0, w, ps):
        def f(e):
            for kc in range(NCH):
                ins = e.matmul(ps.ap[0:n, 0:w], lhsT=wt.ap[:, kc, 0:n], rhs=hxT[:, kc, t0:t0 + w],
                               start=(kc == 0), stop=(kc == NCH - 1))
            return ins
        S.op("pe", f, reads=[wt.buf] + hx_all, writes=[ps.buf])

    evac_flip = [0]

    def evac(out_ap, out_buf, ps, in_ap, scale=None):
        evac_flip[0] ^= 1
        if evac_flip[0]:
            if scale is None:
                S.op("act", lambda e: e.activation(out=out_ap, in_=in_ap, func=AF.Copy), reads=[ps.buf],
                     writes=[out_buf])
            else:
                S.op("act", lambda e: e.activation(out=out_ap, in_=in_ap, func=AF.Copy, scale=scale),
                     reads=[ps.buf], writes=[out_buf])
        else:
            if scale is None:
                S.op("dve", lambda e: e.tensor_copy(out=out_ap, in_=in_ap), reads=[ps.buf], writes=[out_buf])
            else:
                S.op("dve", lambda e: e.tensor_scalar(out=out_ap, in0=in_ap, scalar1=scale, scalar2=None,
                                                      op0=ALU.mult), reads=[ps.buf], writes=[out_buf])

    def phase0(l):
        AR.reset()
        PSG = Rot(PSB)
        wts = [AR.f32(6144) for _ in range(4)]
        acc = AR.f32(96)
        for kc in range(NCH):
            wt = wts[kc % 4]
            S.dma("sp", lambda e, wt=wt, kc=kc: e.dma_start(out=wt.ap, in_=ada_w[l, kc * 128:(kc + 1) * 128, :]),
                  writes=[wt.buf])
            ps = PSG.next()

            def f(e, wt=wt, kc=kc, ps=ps):
                for j in range(48):
                    ins = e.matmul(ps.ap[:, 2 * j:2 * j + 2], lhsT=wt.ap[:, j * 128:(j + 1) * 128],
                                   rhs=scT.ap[:, 2 * kc:2 * kc + 2], start=True, stop=True)
                return ins
            S.op("pe", f, reads=[wt.buf, scT.buf], writes=[ps.buf])
            if kc == 0:
                S.op("dve", lambda e, ps=ps: e.tensor_copy(out=acc.ap, in_=ps.ap[:, 0:96]), reads=[ps.buf],
                     writes=[acc.buf])
            else:
                S.op("dve", lambda e, ps=ps: e.tensor_tensor(out=acc.ap, in0=ps.ap[:, 0:96], in1=acc.ap, op=ALU.add),
                     reads=[ps.buf, acc.buf], writes=[acc.buf])
        accv = acc.ap.rearrange("p (j i) -> p j i", i=2)
        for i in range(2):
            S.op("dve", lambda e, i=i: e.tensor_tensor(out=modT.ap[:, :, i], in0=accv[:, :, i],
                                                       in1=vcol(l, V_ADAB, 48), op=ALU.add),
                 reads=[acc.buf, vecs.buf], writes=[modT.buf])
        for i in range(2):
            S.op("dve", lambda e, i=i: e.scalar_tensor_tensor(out=Amod.ap[:, :, i], in0=modT.ap[:, 16:32, i],
                                                              scalar=1.0, in1=vcol(l, V_NPRE, 16),
                                                              op0=ALU.add, op1=ALU.mult),
                 reads=[modT.buf, vecs.buf], writes=[Amod.buf])
            S.op("dve", lambda e, i=i: e.tensor_tensor(out=Gmod.ap[:, :, i], in0=modT.ap[:, 32:48, i],
                                                       in1=vcol(l, V_NPOST, 16), op=ALU.mult),
                 reads=[modT.buf, vecs.buf], writes=[Gmod.buf])
        tmp = AR.f32(32)
        S.op("act", lambda e: e.activation(out=tmp.ap, in_=vcol(l, V_LAM, 32), func=AF.Exp, scale=-1.0),
             reads=[vecs.buf], writes=[tmp.buf])
        S.op("act", lambda e: e.activation(out=tmp.ap, in_=tmp.ap, func=AF.Ln, bias=1.0),
             reads=[tmp.buf], writes=[tmp.buf])
        S.op("dve", lambda e: e.tensor_scalar(out=lamk.ap[:, 0, :], in0=tmp.ap, scalar1=-8.0, scalar2=None,
                                              op0=ALU.mult), reads=[tmp.buf], writes=[lamk.buf])
        S.op("dve", lambda e: e.tensor_scalar(out=lamk.ap[:, 1, :], in0=tmp.ap, scalar1=-16.0, scalar2=None,
                                              op0=ALU.mult), reads=[tmp.buf], writes=[lamk.buf])

    def phaseA(l):
        AR.reset()
        PSG = Rot(PSB)
        xts_ = Rot([AR.f32(16, 512) for _ in range(2)])
        rstds_ = Rot([AR.f32(512) for _ in range(2)])
        sqs = Rot([AR.bf16(512) for _ in range(3)])
        tmps = Rot([AR.f32(512) for _ in range(3)])
        for ti, (t0, w) in enumerate(TT):
            i = 1 if ti == 0 else 0
            xt = xts_.next()
            rstd = rstds_.next()
            if l == 0:
                S.dma("sp", lambda e, t0=t0, w=w, xt=xt: e.dma_start(out=xt.ap[:, :, 0:w],
                                                             in_=xT0[:, t0:t0 + w].rearrange("(k p) t -> p k t", p=128)),
                      writes=[xt.buf])
            else:
                for mt in range(t0 // 256, (t0 + w) // 256):
                    o = mt * 256 - t0
                    S.dma("sp", lambda e, mt=mt, o=o, xt=xt: e.dma_start(
                        out=xt.ap[:, :, o:o + 256], in_=xres[mt].rearrange("p (k t) -> p k t", k=NCH)),
                        reads=[xres_b[mt]], writes=[xt.buf])
            ps = PSG.next()
            for kc in range(NCH):
                sq = sqs.next()
                S.op("act", lambda e, sq=sq, kc=kc, w=w, xt=xt: e.activation(out=sq.ap[:, 0:w], in_=xt.ap[:, kc, 0:w],
                                                                            func=AF.Square),
                     reads=[xt.buf], writes=[sq.buf])
                S.op("pe", lambda e, sq=sq, kc=kc, w=w, ps=ps: e.matmul(ps.ap[:, 0:w], lhsT=ones_b.ap, rhs=sq.ap[:, 0:w],
                                                                       start=(kc == 0), stop=(kc == NCH - 1)),
                     reads=[sq.buf, ones_b.buf], writes=[ps.buf])
            S.op("act", lambda e, w=w, ps=ps, rstd=rstd: e.activation(out=rstd.ap[:, 0:w], in_=ps.ap[:, 0:w], func=AF.Ln,
                                                                     scale=1.0 / D, bias=EPS),
                 reads=[ps.buf], writes=[rstd.buf])
            S.op("act", lambda e, w=w, rstd=rstd: e.activation(out=rstd.ap[:, 0:w], in_=rstd.ap[:, 0:w], func=AF.Exp,
                                                              scale=-0.5),
                 reads=[rstd.buf], writes=[rstd.buf])
            for kc in range(NCH):
                tp = tmps.next()
                S.op("dve", lambda e, tp=tp, kc=kc, w=w, xt=xt, rstd=rstd: e.tensor_tensor(
                    out=tp.ap[:, 0:w], in0=xt.ap[:, kc, 0:w], in1=rstd.ap[:, 0:w], op=ALU.mult),
                     reads=[xt.buf, rstd.buf], writes=[tp.buf])
                S.op("act", lambda e, tp=tp, kc=kc, w=w, t0=t0, i=i: e.activation(
                    out=hxT[:, kc, t0:t0 + w], in_=tp.ap[:, 0:w], func=AF.Identity,
                    scale=Amod.ap[:, kc, i:i + 1], bias=modT.ap[:, kc, i:i + 1]),
                    reads=[tp.buf, Amod.buf, modT.buf], writes=[hx_b[ti]])

    def phase_lru(l):
        AR.reset()
        PSG = Rot(PSB)
        XP = 2312
        xsets = Rot([(AR.f32(XP), AR.f32(T), AR.bf16(T)) for _ in range(1)])
        aFs = [AR.f32(T), AR.f32(T)]
        bFs = [AR.f32(T), AR.f32(T)]
        tFs = [AR.f32(T), AR.f32(T)]
        hs = AR.f32(T)
        wgs = Rot([AR.bf16(4, 128) for _ in range(3)])
        szs = Rot([AR.f32(512) for _ in range(5)])
        ybs = Rot([AR.bf16(512) for _ in range(2)])
        for (xp_, _, _) in xsets.items:
            S.op("dve", lambda e, xp_=xp_: e.memset(xp_.ap, 0.0), writes=[xp_.buf])
        segs = [(0, 0, TCX), (259, TCX, TLAT)]

        def stage_x(n):
            xpad, xc, xcb = xsets.next()
            wx = load_w(w_in[l, :, O_LX + n * 128:O_LX + (n + 1) * 128])
            wz = load_w(w_in[l, :, O_LZ + n * 128:O_LZ + (n + 1) * 128])
            wg = wgs.next()
            S.dma("pool", lambda e: e.dma_start(out=wg.ap[:, 0:2, :], in_=lru_wr[l, :, n].rearrange("r c d -> c r d")),
                  writes=[wg.buf])
            S.dma("pool", lambda e: e.dma_start(out=wg.ap[:, 2:4, :], in_=lru_wi[l, :, n].rearrange("r c d -> c r d")),
                  writes=[wg.buf])
            for ti, (t0, w) in enumerate(TT):
                ps = PSG.next()
                proj_fm(wx, 128, ti, ps)
                off = 2 + t0 if ti == 0 else 259 + 2 + (t0 - TCX)
                evac(xpad.ap[:, off:off + w], xpad.buf, ps, ps.ap[:, 0:w])
            for (po, t0, ln) in segs:
                S.op("dve", lambda e, po=po, t0=t0, ln=ln: e.tensor_scalar(
                    out=xc.ap[:, t0:t0 + ln], in0=xpad.ap[:, po:po + ln], scalar1=vcol(l, V_CW + n * 4 + 0),
                    scalar2=vcol(l, V_CB + n), op0=ALU.mult, op1=ALU.add),
                    reads=[xpad.buf, vecs.buf], writes=[xc.buf])
                for k in range(1, 4):
                    S.op("dve", lambda e, po=po, t0=t0, ln=ln, k=k: e.scalar_tensor_tensor(
                        out=xc.ap[:, t0:t0 + ln], in0=xpad.ap[:, po + k:po + k + ln], scalar=vcol(l, V_CW + n * 4 + k),
                        in1=xc.ap[:, t0:t0 + ln], op0=ALU.mult, op1=ALU.add),
                        reads=[xpad.buf, vecs.buf, xc.buf], writes=[xc.buf])
            S.op("act", lambda e: e.activation(out=xcb.ap, in_=xc.ap, func=AF.Copy), reads=[xc.buf], writes=[xcb.buf])
            return (xc, xcb, wz, wg)

        def stage_g(n, ctx):
            xc, xcb, wz, wg = ctx
            szt = []
            for ti, (t0, w) in enumerate(TT):
                ps = PSG.next()
                proj_fm(wz, 128, ti, ps)
                sz = szs.next()
                S.op("act", lambda e, sz=sz, ps=ps, w=w: e.activation(out=sz.ap[:, 0:w], in_=ps.ap[:, 0:w], func=AF.Silu),
                     reads=[ps.buf], writes=[sz.buf])
                szt.append(sz)

            def do_dir(dr):
                aF, bF, tF = aFs[dr], bFs[dr], tFs[dr]
                for ti, (t0, w) in enumerate(TT):
                    psr = PSG.next()
                    S.op("pe", lambda e, psr=psr, dr=dr, t0=t0, w=w: e.matmul(
                        psr.ap[:, 0:w], lhsT=wg.ap[:, dr, :], rhs=xcb.ap[:, t0:t0 + w], start=True, stop=True),
                        reads=[wg.buf, xcb.buf], writes=[psr.buf])
                    S.op("act", lambda e, psr=psr, dr=dr, t0=t0, w=w: e.activation(
                        out=aF.ap[:, t0:t0 + w], in_=psr.ap[:, 0:w], func=AF.Sigmoid,
                        bias=vcol(l, V_BR + dr * 16 + n)), reads=[psr.buf, vecs.buf], writes=[aF.buf])
                    psi = PSG.next()
                    S.op("pe", lambda e, psi=psi, dr=dr, t0=t0, w=w: e.matmul(
                        psi.ap[:, 0:w], lhsT=wg.ap[:, 2 + dr, :], rhs=xcb.ap[:, t0:t0 + w], start=True, stop=True),
                        reads=[wg.buf, xcb.buf], writes=[psi.buf])
                    S.op("act", lambda e, psi=psi, dr=dr, t0=t0, w=w: e.activation(
                        out=bF.ap[:, t0:t0 + w], in_=psi.ap[:, 0:w], func=AF.Sigmoid,
                        bias=vcol(l, V_BI + dr * 16 + n)), reads=[psi.buf, vecs.buf], writes=[bF.buf])
                k1 = lamk.ap[:, 0, dr * 16 + n:dr * 16 + n + 1]
                k2 = lamk.ap[:, 1, dr * 16 + n:dr * 16 + n + 1]
                S.op("act", lambda e, k2=k2: e.activation(out=tF.ap, in_=aF.ap, func=AF.Exp, scale=k2),
                     reads=[aF.buf, lamk.buf], writes=[tF.buf])
                S.op("act", lambda e, k1=k1: e.activation(out=aF.ap, in_=aF.ap, func=AF.Exp, scale=k1),
                     reads=[aF.buf, lamk.buf], writes=[aF.buf])
                S.op("act", lambda e: e.activation(out=tF.ap, in_=tF.ap, func=AF.Sqrt, scale=-1.0, bias=1.0),
                     reads=[tF.buf], writes=[tF.buf])
                S.op("dve", lambda e: e.tensor_tensor(out=bF.ap, in0=bF.ap, in1=tF.ap, op=ALU.mult),
                     reads=[bF.buf, tF.buf], writes=[bF.buf])
                S.op("dve", lambda e: e.tensor_tensor(out=bF.ap, in0=bF.ap, in1=xc.ap, op=ALU.mult),
                     reads=[bF.buf, xc.buf], writes=[bF.buf])
                if dr == 0:
                    S.op("dve", lambda e: e.tensor_tensor_scan(out=hs.ap[:, 0:TCX], data0=aF.ap[:, 0:TCX],
                                                               data1=bF.ap[:, 0:TCX], initial=0.0,
                                                               op0=ALU.mult, op1=ALU.add),
                         reads=[aF.buf, bF.buf], writes=[hs.buf])
                    S.op("dve", lambda e: e.tensor_tensor_scan(out=hs.ap[:, TCX:T], data0=aF.ap[:, TCX:T],
                                                               data1=bF.ap[:, TCX:T], initial=hs.ap[:, TCX - 1:TCX],
                                                               op0=ALU.mult, op1=ALU.add),
                         reads=[aF.buf, bF.buf, hs.buf], writes=[hs.buf])
                else:
                    S.op("dve", lambda e: e.tensor_tensor_scan(out=tF.ap[:, 0:TCX][:, ::-1],
                                                               data0=aF.ap[:, 0:TCX][:, ::-1],
                                                               data1=bF.ap[:, 0:TCX][:, ::-1], initial=0.0,
                                                               op0=ALU.mult, op1=ALU.add),
                         reads=[aF.buf, bF.buf], writes=[tF.buf])
                    S.op("dve", lambda e: e.tensor_tensor_scan(out=tF.ap[:, TCX:T][:, ::-1],
                                                               data0=aF.ap[:, TCX:T][:, ::-1],
                                                               data1=bF.ap[:, TCX:T][:, ::-1],
                                                               initial=tF.ap[:, 0:1],
                                                               op0=ALU.mult, op1=ALU.add),
                         reads=[aF.buf, bF.buf, tF.buf], writes=[tF.buf])
                    S.op("dve", lambda e: e.tensor_tensor(out=hs.ap, in0=hs.ap, in1=tF.ap, op=ALU.add),
                         reads=[hs.buf, tF.buf], writes=[hs.buf])
            for dr in range(2):
                do_dir(dr)
            for ti, (t0, w) in enumerate(TT):
                sz = szt[ti]
                yb = ybs.next()
                S.op("dve", lambda e, sz=sz, yb=yb, t0=t0, w=w: e.tensor_tensor(out=yb.ap[:, 0:w], in0=hs.ap[:, t0:t0 + w],
                                                                               in1=sz.ap[:, 0:w], op=ALU.mult),
                     reads=[hs.buf, sz.buf], writes=[yb.buf])
                S.dma("sp", lambda e, yb=yb, t0=t0, w=w: e.dma_start(out=Yd[0, n * 128:(n + 1) * 128, t0:t0 + w],
                                                                    in_=yb.ap[:, 0:w]),
                      reads=[yb.buf], writes=[Y_b[0]])

        for n in range(NCH):
            stage_g(n, stage_x(n))

    def phase_ml(l):
        AR.reset()
        PSG = Rot([PSB[0], PSB[1], PSB[4], PSB[5]])
        PSE = PSB[0]
        PSA = Rot([PSB[1], PSB[4], PSB[5]])
        PSS = [(PSB[6], PSB[7], psum_t[3]), (PSB[2], PSB[3], psum_t[1])]
        qT = AR.bf16(2, T)
        kT = AR.bf16(2, T)
        ktm = AR.bf16(NTILE, 256)
        vtm = AR.bf16(NTILE, 256)
        hD = [AR.bf16(2, T), AR.bf16(2, T)]
        Gtm = AR.f32(NTILE, 32)
        lftm = AR.f32(NTILE, 32)
        rtm = AR.f32(NTILE, 2, 8)
        sLa = AR.f32(NTILE, 32)
        Ecs = [Rot([AR.f32(128) for _ in range(2)]) for _ in range(2)]
        rhc = Rot([AR.f32(128) for _ in range(2)])
        ST32 = [AR.f32(2, 384), AR.f32(2, 384)]
        STb = [AR.bf16(2, 384), AR.bf16(2, 384)]
        tmpC = Rot([AR.f32(2, 384) for _ in range(2)])
        PTs = Rot([AR.bf16(128) for _ in range(4)])
        kps = Rot([AR.bf16(256) for _ in range(3)])
        t128 = Rot([AR.f32(128) for _ in range(4)])
        qss = [Rot([AR.bf16(2, 128) for _ in range(2)]) for _ in range(2)]
        e256 = Rot([AR.f32(512) for _ in range(4)])
        sq256 = Rot([AR.bf16(512) for _ in range(1)])
        yb256 = Rot([AR.bf16(512) for _ in range(2)])
        hg256 = Rot([AR.f32(2, 512) for _ in range(1)])

        wgt = load_w(w_in[l, :, O_MG:O_MG + 32], n=32)
        for c in range(NTILE):
            ps = PSG.next()

            def f(e, c=c, ps=ps):
                for kc in range(NCH):
                    ins = e.matmul(ps.ap[:, 0:32], lhsT=hxT[:, kc, c * 128:(c + 1) * 128], rhs=wgt.ap[:, kc, 0:32],
                                   start=(kc == 0), stop=(kc == NCH - 1))
                return ins
            S.op("pe", f, reads=[wgt.buf] + hx_all, writes=[ps.buf])
            S.op("dve", lambda e, c=c, ps=ps: e.tensor_tensor(out=Gtm.ap[:, c, :], in0=ps.ap[:, 0:32],
                                                              in1=gbB.ap[:, l * 32:(l + 1) * 32], op=ALU.add),
                 reads=[ps.buf, gbB.buf], writes=[Gtm.buf])
        S.op("act", lambda e: e.activation(out=lftm.ap, in_=Gtm.ap, func=AF.Exp, scale=-1.0),
             reads=[Gtm.buf], writes=[lftm.buf])
        S.op("act", lambda e: e.activation(out=lftm.ap, in_=lftm.ap, func=AF.Ln, bias=1.0),
             reads=[lftm.buf], writes=[lftm.buf])
        lf2 = lftm.ap.rearrange("p c g -> p (c g)")
        for (a, b) in ((0, 288), (288, 576)):
            ps = PSG.next()
            S.op("pe", lambda e, ps=ps, a=a, b=b: e.matmul(ps.ap[:, 0:b - a], lhsT=ones_f.ap, rhs=lf2[:, a:b],
                                                           start=True, stop=True),
                 reads=[ones_f.buf, lftm.buf], writes=[ps.buf])
            S.op("act", lambda e, ps=ps, a=a, b=b: e.activation(
                out=sLa.ap.rearrange("p c g -> p (c g)")[:, a:b], in_=ps.ap[:, 0:b - a], func=AF.Exp, scale=-1.0),
                reads=[ps.buf], writes=[sLa.buf])
        for dr, tri in ((0, TRIF), (1, TRIB)):
            for (a, b) in ((0, 288), (288, 576)):
                ps = PSG.next()
                S.op("pe", lambda e, ps=ps, a=a, b=b, tri=tri: e.matmul(ps.ap[:, 0:b - a], lhsT=tri, rhs=lf2[:, a:b],
                                                                        start=True, stop=True),
                     reads=[cst.buf, lftm.buf], writes=[ps.buf])
                c0 = a // 32
                S.op("dve", lambda e, ps=ps, dr=dr, c0=c0: e.tensor_tensor(
                    out=rtm.ap[:, c0:c0 + 9, dr, :],
                    in0=ps.ap[:, 0:288].rearrange("p (c g) -> p c g", g=32)[:, :, dr * 16 + 8:dr * 16 + 16],
                    in1=Gtm.ap[:, c0:c0 + 9, dr * 16:dr * 16 + 8], op=ALU.add),
                    reads=[ps.buf, Gtm.buf], writes=[rtm.buf])
        S.op("act", lambda e: e.activation(out=rtm.ap, in_=rtm.ap, func=AF.Exp), reads=[rtm.buf], writes=[rtm.buf])

        order_b = [1, 0] + list(range(NTILE - 1, 1, -1))
        masks = (TRIF, TRIB)
        for h in range(8):
            wq = [load_w(w_in[l, :, O_MQ + h * 256 + j * 128:O_MQ + h * 256 + (j + 1) * 128]) for j in range(2)]
            wk = [load_w(w_in[l, :, O_MK + h * 256 + j * 128:O_MK + h * 256 + (j + 1) * 128]) for j in range(2)]
            for j in range(2):
                for ti, (t0, w) in enumerate(TT):
                    ps = PSG.next()
                    proj_fm(wq[j], 128, ti, ps)
                    evac(qT.ap[:, j, t0:t0 + w], qT.buf, ps, ps.ap[:, 0:w])
                    ps = PSG.next()
                    proj_fm(wk[j], 128, ti, ps)
                    evac(kT.ap[:, j, t0:t0 + w], kT.buf, ps, ps.ap[:, 0:w], scale=0.0625)
            for c in range(NTILE):
                ps = PSG.next()

                def f(e, c=c, ps=ps, wk=wk):
                    for j in range(2):
                        for kc in range(NCH):
                            ins = e.matmul(ps.ap[:, j * 128:(j + 1) * 128], lhsT=hxT[:, kc, c * 128:(c + 1) * 128],
                                           rhs=wk[j].ap[:, kc, :], start=(kc == 0), stop=(kc == NCH - 1))
                    return ins
                S.op("pe", f, reads=[wk[0].buf, wk[1].buf] + hx_all, writes=[ps.buf])
                evac(ktm.ap[:, c, :], ktm.buf, ps, ps.ap[:, 0:256], scale=0.0625)
            wv = [load_w(w_in[l, :, O_MV + h * 256 + j * 128:O_MV + h * 256 + (j + 1) * 128]) for j in range(2)]
            for c in range(NTILE):
                ps = PSG.next()

                def f(e, c=c, ps=ps, wv=wv):
                    for j in range(2):
                        for kc in range(NCH):
                            ins = e.matmul(ps.ap[:, j * 128:(j + 1) * 128], lhsT=hxT[:, kc, c * 128:(c + 1) * 128],
                                           rhs=wv[j].ap[:, kc, :], start=(kc == 0), stop=(kc == NCH - 1))
                    return ins
                S.op("pe", f, reads=[wv[0].buf, wv[1].buf] + hx_all, writes=[ps.buf])
                evac(vtm.ap[:, c, :], vtm.buf, ps, ps.ap[:, 0:256])
            def dmm(info, dr):
                kp, c = info["kp"], info["c"]
                pb0, pb1, pst = PSS[dr]

                def fD(e, kp=kp, c=c, pst=pst):
                    for j in range(2):
                        e.matmul(pst[:, j, 0:256], lhsT=kp.ap[:, j * 128:(j + 1) * 128], rhs=vtm.ap[:, c, :],
                                 start=True, stop=True)
                        ins = e.matmul(pst[:, j, 256:384], lhsT=kp.ap[:, j * 128:(j + 1) * 128], rhs=ones_b.ap,
                                       start=True, stop=True)
                    return ins
                S.op("pe", fD, reads=[kp.buf, vtm.buf, ones_b.buf], writes=[pb0.buf, pb1.buf])

            def P1(step, dr):
                c = step if dr == 0 else order_b[step]
                cs = slice(c * 128, (c + 1) * 128)
                fcol = dr * 16 + 8 + h
                rcol = rtm.ap[:, c, dr, h:h + 1]
                rh = rhc.next()
                S.op("act", lambda e: e.activation(out=rh.ap, in_=masks[dr], func=AF.Identity,
                                                   scale=lftm.ap[:, c, fcol:fcol + 1]),
                     reads=[cst.buf, lftm.buf], writes=[rh.buf])
                S.op("pe", lambda e: e.matmul(PSE.ap[:, 0:128], lhsT=ones_f.ap, rhs=rh.ap, start=True, stop=True),
                     reads=[ones_f.buf, rh.buf], writes=[PSE.buf])
                Ec = Ecs[dr].next()
                S.op("act", lambda e: e.activation(out=Ec.ap, in_=PSE.ap[:, 0:128], func=AF.Exp, scale=-1.0),
                     reads=[PSE.buf], writes=[Ec.buf])
                qs = qss[dr].next()
                S.op("dve", lambda e: e.tensor_tensor(out=qs.ap, in0=qT.ap[:, :, cs],
                                                      in1=Ec.ap.unsqueeze(1).broadcast_to([128, 2, 128]), op=ALU.mult),
                     reads=[qT.buf, Ec.buf], writes=[qs.buf])
                pa = PSA.next()

                def fS(e):
                    for j in range(2):
                        ins = e.matmul(pa.ap[:, 0:128], lhsT=kT.ap[:, j, cs], rhs=qs.ap[:, j, :],
                                       start=(j == 0), stop=(j == 1))
                    return ins
                S.op("pe", fS, reads=[kT.buf, qs.buf], writes=[pa.buf])
                PT = PTs.next()
                S.op("dve", lambda e: e.scalar_tensor_tensor(out=PT.ap, in0=pa.ap[:, 0:128], scalar=rcol, in1=masks[dr],
                                                             op0=ALU.mult, op1=ALU.mult),
                     reads=[pa.buf, rtm.buf, cst.buf], writes=[PT.buf])
                info = dict(c=c, cs=cs, fcol=fcol, qs=qs, pa=pa, PT=PT)
                if step < NTILE - 1:
                    kp = kps.next()
                    S.op("act", lambda e: e.activation(out=kp.ap, in_=ktm.ap[:, c, :], func=AF.Identity, scale=rcol),
                         reads=[ktm.buf, rtm.buf], writes=[kp.buf])
                    info["kp"] = kp
                return info

            def P2(step, dr, info, nxt_info):
                c, cs, fcol, qs, pa, PT = (info[k] for k in ("c", "cs", "fcol", "qs", "pa", "PT"))
                first = (step == 0)
                stb = STb[dr]

                def fU(e):
                    for ec in range(2):
                        o = pa.ap[:, 128 + ec * 128:256 + ec * 128]
                        ins = e.matmul(o, lhsT=vtm.ap[:, c, ec * 128:(ec + 1) * 128], rhs=PT.ap, start=True, stop=first)
                        if not first:
                            for j in range(2):
                                ins = e.matmul(o, lhsT=stb.ap[:, j, ec * 128:(ec + 1) * 128], rhs=qs.ap[:, j, :],
                                               start=False, stop=(j == 1))
                    o = pa.ap[:, 384:512]
                    ins = e.matmul(o, lhsT=ones_b.ap, rhs=PT.ap, start=True, stop=first)
                    if not first:
                        for j in range(2):
                            ins = e.matmul(o, lhsT=stb.ap[:, j, 256:384], rhs=qs.ap[:, j, :], start=False, stop=(j == 1))
                    return ins
                S.op("pe", fU, reads=[vtm.buf, PT.buf, stb.buf, qs.buf, ones_b.buf], writes=[pa.buf])
                if step < NTILE - 1:
                    pb0, pb1, pst = PSS[dr]
                    sl = sLa.ap[:, c, fcol:fcol + 1]
                    st32 = ST32[dr]
                    if first:
                        S.op("dve", lambda e: e.tensor_scalar(out=stb.ap, in0=pst[:, :, 0:384], scalar1=sl, scalar2=None,
                                                              op0=ALU.mult),
                             reads=[pb0.buf, pb1.buf, sLa.buf], writes=[stb.buf])
                        S.op("act", lambda e: e.activation(out=st32.ap, in_=pst[:, :, 0:384], func=AF.Identity, scale=sl),
                             reads=[pb0.buf, pb1.buf, sLa.buf], writes=[st32.buf])
                    else:
                        tc_ = tmpC.next()
                        S.op("dve", lambda e: e.tensor_tensor(out=tc_.ap, in0=pst[:, :, 0:384], in1=st32.ap, op=ALU.add),
                             reads=[pb0.buf, pb1.buf, st32.buf], writes=[tc_.buf])
                        S.op("dve", lambda e: e.tensor_scalar(out=stb.ap, in0=tc_.ap, scalar1=sl, scalar2=None,
                                                              op0=ALU.mult),
                             reads=[tc_.buf, sLa.buf], writes=[stb.buf])
                        S.op("act", lambda e: e.activation(out=st32.ap, in_=tc_.ap, func=AF.Identity, scale=sl),
                             reads=[tc_.buf, sLa.buf], writes=[st32.buf])
                    if nxt_info is not None and "kp" in nxt_info:
                        dmm(nxt_info, dr)
                am = t128.next()
                S.op("dve", lambda e: e.tensor_scalar(out=am.ap, in0=pa.ap[:, 384:512], scalar1=-1.0, scalar2=1.0,
                                                      op0=ALU.mult, op1=ALU.max),
                     reads=[pa.buf], writes=[am.buf])
                S.op("dve", lambda e: e.scalar_tensor_tensor(out=am.ap, in0=pa.ap[:, 384:512], scalar=1.0, in1=am.ap,
                                                             op0=ALU.max, op1=ALU.max),
                     reads=[pa.buf, am.buf], writes=[am.buf])
                S.op("act", lambda e: e.activation(out=am.ap, in_=am.ap, func=AF.Ln), reads=[am.buf], writes=[am.buf])
                S.op("act", lambda e: e.activation(out=am.ap, in_=am.ap, func=AF.Exp, scale=-1.0),
                     reads=[am.buf], writes=[am.buf])
                hd = hD[dr]
                S.op("dve", lambda e: e.tensor_tensor(
                    out=hd.ap[:, :, cs], in0=pa.ap[:, 128:384].rearrange("p (a b) -> p a b", a=2),
                    in1=am.ap.unsqueeze(1).broadcast_to([128, 2, 128]), op=ALU.mult),
                    reads=[pa.buf, am.buf], writes=[hd.buf])

            cur = [P1(0, 0), P1(0, 1)]
            for dr in range(2):
                dmm(cur[dr], dr)
            for step in range(NTILE):
                nxt = [P1(step + 1, 0), P1(step + 1, 1)] if step + 1 < NTILE else [None, None]
                for dr in range(2):
                    P2(step, dr, cur[dr], nxt[dr])
                cur = nxt
            wo = [load_w(w_in[l, :, O_MO + h * 256 + j * 128:O_MO + h * 256 + (j + 1) * 128]) for j in range(2)]
            wz = [load_w(w_in[l, :, O_MZ + h * 256 + j * 128:O_MZ + h * 256 + (j + 1) * 128]) for j in range(2)]
            for (t0, w) in TT:
                hg = hg256.next()
                pss = PSE
                for ec in range(2):
                    ps = PSA.next()
                    proj_fm_rng(wo[ec], 128, t0, w, ps)
                    so = e256.next()
                    S.op("act", lambda e, so=so, ps=ps, w=w: e.activation(out=so.ap[:, 0:w], in_=ps.ap[:, 0:w],
                                                                         func=AF.Sigmoid),
                         reads=[ps.buf], writes=[so.buf])
                    hsum = e256.next()
                    S.op("dve", lambda e, hsum=hsum, ec=ec, t0=t0, w=w: e.tensor_tensor(
                        out=hsum.ap[:, 0:w], in0=hD[0].ap[:, ec, t0:t0 + w], in1=hD[1].ap[:, ec, t0:t0 + w], op=ALU.add),
                        reads=[hD[0].buf, hD[1].buf], writes=[hsum.buf])
                    S.op("dve", lambda e, hsum=hsum, so=so, hg=hg, ec=ec, w=w: e.tensor_tensor(
                        out=hg.ap[:, ec, 0:w], in0=hsum.ap[:, 0:w], in1=so.ap[:, 0:w], op=ALU.mult),
                        reads=[hsum.buf, so.buf], writes=[hg.buf])
                    sq = sq256.next()
                    S.op("act", lambda e, sq=sq, hg=hg, ec=ec, w=w: e.activation(out=sq.ap[:, 0:w], in_=hg.ap[:, ec, 0:w],
                                                                                func=AF.Square),
                         reads=[hg.buf], writes=[sq.buf])
                    S.op("pe", lambda e, sq=sq, pss=pss, ec=ec, w=w: e.matmul(pss.ap[:, 0:w], lhsT=ones_b.ap,
                                                                             rhs=sq.ap[:, 0:w],
                                                                             start=(ec == 0), stop=(ec == 1)),
                         reads=[sq.buf, ones_b.buf], writes=[pss.buf])
                rs = e256.next()
                S.op("act", lambda e, rs=rs, pss=pss, w=w: e.activation(out=rs.ap[:, 0:w], in_=pss.ap[:, 0:w], func=AF.Ln,
                                                                       scale=1.0 / 256, bias=EPS),
                     reads=[pss.buf], writes=[rs.buf])
                S.op("act", lambda e, rs=rs, w=w: e.activation(out=rs.ap[:, 0:w], in_=rs.ap[:, 0:w], func=AF.Exp,
                                                              scale=-0.5),
                     reads=[rs.buf], writes=[rs.buf])
                for ec in range(2):
                    ps = PSA.next()
                    proj_fm_rng(wz[ec], 128, t0, w, ps)
                    sz = e256.next()
                    S.op("act", lambda e, sz=sz, ps=ps, w=w: e.activation(out=sz.ap[:, 0:w], in_=ps.ap[:, 0:w],
                                                                         func=AF.Silu),
                         reads=[ps.buf], writes=[sz.buf])
                    S.op("dve", lambda e, hg=hg, ec=ec, rs=rs, w=w: e.tensor_tensor(
                        out=hg.ap[:, ec, 0:w], in0=hg.ap[:, ec, 0:w], in1=rs.ap[:, 0:w], op=ALU.mult),
                        reads=[hg.buf, rs.buf], writes=[hg.buf])
                    yb = yb256.next()
                    S.op("dve", lambda e, hg=hg, ec=ec, sz=sz, yb=yb, h=h, w=w: e.scalar_tensor_tensor(
                        out=yb.ap[:, 0:w], in0=hg.ap[:, ec, 0:w], scalar=vcol(l, V_MLN + h * 2 + ec), in1=sz.ap[:, 0:w],
                        op0=ALU.mult, op1=ALU.mult),
                        reads=[hg.buf, sz.buf, vecs.buf], writes=[yb.buf])
                    r0 = h * 256 + ec * 128
                    S.dma("sp", lambda e, yb=yb, r0=r0, t0=t0, w=w: e.dma_start(out=Yd[1, r0:r0 + 128, t0:t0 + w],
                                                                              in_=yb.ap[:, 0:w]),
                          reads=[yb.buf], writes=[Y_b[1]])

    def phase_att(l):
        AR.reset()
        PSG = Rot(PSB[0:2])
        PSSc = Rot(PSB[2:5])
        PSO = Rot(PSB[5:7])
        PSD = Rot(PSB[7:8])
        KT = AR.bf16(4, T)
        Vtm = AR.bf16(NTILE, 512)
        cosr = Rot([AR.f32(512) for _ in range(2)])
        sinr = Rot([AR.f32(512) for _ in range(2)])
        f512 = Rot([AR.f32(512) for _ in range(8)])
        szs = Rot([AR.f32(512) for _ in range(3)])
        rcs = Rot([AR.f32(512) for _ in range(2)])
        b512 = Rot([AR.bf16(512) for _ in range(5)])
        qTs = Rot([AR.bf16(512) for _ in range(2)])
        PTs = Rot([AR.bf16(512) for _ in range(6)])
        ybs = Rot([AR.bf16(512) for _ in range(2)])
        SC = 128.0 ** -0.5

        def load_cs(ti):
            t0, w = TT[ti]
            ct = cosr.next()
            st = sinr.next()
            S.dma("sp", lambda e: e.dma_start(out=ct.ap, in_=cos_d[:, t0 - TCX:t0 - TCX + w]), writes=[ct.buf])
            S.dma("sp", lambda e: e.dma_start(out=st.ap, in_=sin_d[:, t0 - TCX:t0 - TCX + w]), writes=[st.buf])
            return ct, st

        def norm_rope_gen(ps, w, gcol, rope, dst_ap, dst_buf, cs):
            q32 = f512.next()
            S.op("dve", lambda e: e.tensor_copy(out=q32.ap[:, 0:w], in_=ps.ap[:, 0:w]),
                 reads=[ps.buf], writes=[q32.buf])
            sq = b512.next()
            S.op("dve", lambda e: e.tensor_tensor(out=sq.ap[:, 0:w], in0=q32.ap[:, 0:w], in1=q32.ap[:, 0:w], op=ALU.mult),
                 reads=[q32.buf], writes=[sq.buf])
            yield
            ps2 = PSG.next()
            S.op("pe", lambda e: e.matmul(ps2.ap[:, 0:w], lhsT=ones_b.ap, rhs=sq.ap[:, 0:w], start=True, stop=True),
                 reads=[ones_b.buf, sq.buf], writes=[ps2.buf])
            rs = f512.next()
            S.op("act", lambda e: e.activation(out=rs.ap[:, 0:w], in_=ps2.ap[:, 0:w], func=AF.Ln, scale=1.0 / 128,
                                               bias=EPS), reads=[ps2.buf], writes=[rs.buf])
            S.op("act", lambda e: e.activation(out=rs.ap[:, 0:w], in_=rs.ap[:, 0:w], func=AF.Exp, scale=-0.5),
                 reads=[rs.buf], writes=[rs.buf])
            if not rope:
                S.op("dve", lambda e: e.scalar_tensor_tensor(out=dst_ap, in0=q32.ap[:, 0:w], scalar=gcol, in1=rs.ap[:, 0:w],
                                                             op0=ALU.mult, op1=ALU.mult),
                     reads=[q32.buf, rs.buf, vecs.buf], writes=[dst_buf])
                return
            qn = b512.next()
            S.op("dve", lambda e: e.scalar_tensor_tensor(out=qn.ap[:, 0:w], in0=q32.ap[:, 0:w], scalar=gcol,
                                                         in1=rs.ap[:, 0:w], op0=ALU.mult, op1=ALU.mult),
                 reads=[q32.buf, rs.buf, vecs.buf], writes=[qn.buf])
            ct, st = cs
            t2 = f512.next()
            S.op("dve", lambda e: e.tensor_tensor(out=t2.ap[:, 0:w], in0=qn.ap[:, 0:w], in1=ct.ap[:, 0:w], op=ALU.mult),
                 reads=[qn.buf, ct.buf], writes=[t2.buf])
            yield
            ps3 = PSG.next()
            S.op("pe", lambda e: e.matmul(ps3.ap[:, 0:w], lhsT=RROT, rhs=qn.ap[:, 0:w], start=True, stop=True),
                 reads=[cstb.buf, qn.buf], writes=[ps3.buf])
            t1 = f512.next()
            S.op("dve", lambda e: e.tensor_tensor(out=t1.ap[:, 0:w], in0=ps3.ap[:, 0:w], in1=st.ap[:, 0:w], op=ALU.mult),
                 reads=[ps3.buf, st.buf], writes=[t1.buf])
            S.op("dve", lambda e: e.tensor_tensor(out=dst_ap, in0=t1.ap[:, 0:w], in1=t2.ap[:, 0:w], op=ALU.add),
                 reads=[t1.buf, t2.buf], writes=[dst_buf])

        def norm_rope(*a):
            for _ in norm_rope_gen(*a):
                pass

        for g in range(4):
            wk = load_w(w_in[l, :, O_AK + g * 128:O_AK + (g + 1) * 128])
            for ti, (t0, w) in enumerate(TT):
                ps = PSG.next()
                proj_fm(wk, 128, ti, ps)
                cs = load_cs(ti) if ti > 0 else None
                norm_rope(ps, w, vcol(l, V_KN), ti > 0, KT.ap[:, g, t0:t0 + w], KT.buf, cs)
        wv = [load_w(w_in[l, :, O_AV + g * 128:O_AV + (g + 1) * 128]) for g in range(4)]
        for c in range(NTILE):
            ps = PSG.next()

            def f(e, c=c, ps=ps):
                for g in range(4):
                    for kc in range(NCH):
                        ins = e.matmul(ps.ap[:, g * 128:(g + 1) * 128], lhsT=hxT[:, kc, c * 128:(c + 1) * 128],
                                       rhs=wv[g].ap[:, kc, :], start=(kc == 0), stop=(kc == NCH - 1))
                return ins
            S.op("pe", f, reads=[w.buf for w in wv] + hx_all, writes=[ps.buf])
            evac(Vtm.ap[:, c, :], Vtm.buf, ps, ps.ap[:, 0:512])
        tiles = [(h, ti) for h in range(16) for ti in range(len(TT))]
        ready = {}
        wcur = {}

        def prologue(h, ti):
            t0, w = TT[ti]
            if ti == 0:
                wcur["q"] = load_w(w_in[l, :, O_AQ + h * 128:O_AQ + (h + 1) * 128])
                wcur["z"] = load_w(w_in[l, :, O_AZ + h * 128:O_AZ + (h + 1) * 128])
            wq, wz = wcur["q"], wcur["z"]
            ps = PSG.next()
            proj_fm(wq, 128, ti, ps)
            qt = qTs.next()
            cs = load_cs(ti) if ti > 0 else None
            psz = PSG.next()
            proj_fm(wz, 128, ti, psz)
            yield
            gen = norm_rope_gen(ps, w, vcol(l, V_QN), ti > 0, qt.ap[:, 0:w], qt.buf, cs)
            next(gen, None)
            sz = szs.next()
            S.op("act", lambda e: e.activation(out=sz.ap[:, 0:w], in_=psz.ap[:, 0:w], func=AF.Exp, scale=-1.0),
                 reads=[psz.buf], writes=[sz.buf])
            S.op("act", lambda e: e.activation(out=sz.ap[:, 0:w], in_=sz.ap[:, 0:w], func=AF.Ln, bias=1.0),
                 reads=[sz.buf], writes=[sz.buf])
            S.op("act", lambda e: e.activation(out=sz.ap[:, 0:w], in_=sz.ap[:, 0:w], func=AF.Exp, scale=-1.0),
                 reads=[sz.buf], writes=[sz.buf])
            S.op("dve", lambda e: e.tensor_tensor(out=sz.ap[:, 0:w], in0=psz.ap[:, 0:w], in1=sz.ap[:, 0:w], op=ALU.mult),
                 reads=[psz.buf, sz.buf], writes=[sz.buf])
            ready[(h, ti)] = (qt, sz)
            yield
            for _ in gen:
                yield

        for _ in prologue(*tiles[0]):
            pass
        for idx, (h, ti) in enumerate(tiles):
            g = h // 4
            t0, w = TT[ti]
            qt, sz = ready.pop((h, ti))
            nxt = prologue(*tiles[idx + 1]) if idx + 1 < len(tiles) else None
            keys = list(range(NTILE)) if ti > 0 else [0, 1]
            nk = len(keys)
            pO = PSO.next()
            pD = PSD.next()
            pSs = {}

            def emit_S(ki):
                pS = PSSc.next()
                kc_ = keys[ki]
                S.op("pe", lambda e, pS=pS, kc_=kc_, qt=qt, w=w, g=g: e.matmul(
                    pS.ap[:, 0:w], lhsT=KT.ap[:, g, kc_ * 128:(kc_ + 1) * 128], rhs=qt.ap[:, 0:w],
                    start=True, stop=True), reads=[KT.buf, qt.buf], writes=[pS.buf])
                pSs[ki] = pS
            emit_S(0)
            if nk > 1:
                emit_S(1)
            for ki, kc_ in enumerate(keys):
                pS = pSs.pop(ki)
                PT = PTs.next()
                S.op("act", lambda e, pS=pS, PT=PT, w=w: e.activation(out=PT.ap[:, 0:w], in_=pS.ap[:, 0:w],
                                                                     func=AF.Exp, scale=SC),
                     reads=[pS.buf], writes=[PT.buf])
                if ki + 2 < nk:
                    emit_S(ki + 2)
                fst = (ki == 0)
                lst = (ki == nk - 1)

                def fO(e, PT=PT, kc_=kc_, w=w, fst=fst, lst=lst, pO=pO, pD=pD, g=g):
                    e.matmul(pO.ap[:, 0:w], lhsT=Vtm.ap[:, kc_, g * 128:(g + 1) * 128], rhs=PT.ap[:, 0:w],
                             start=fst, stop=lst)
                    return e.matmul(pD.ap[:, 0:w], lhsT=ones_b.ap, rhs=PT.ap[:, 0:w], start=fst, stop=lst)
                S.op("pe", fO, reads=[Vtm.buf, PT.buf, ones_b.buf], writes=[pO.buf, pD.buf])
                if nxt is not None and ki in (1, 4, 8, 12):
                    next(nxt, None)
            if nxt is not None:
                for _ in nxt:
                    pass
            rc = rcs.next()
            S.op("act", lambda e, rc=rc, pD=pD, w=w: e.activation(out=rc.ap[:, 0:w], in_=pD.ap[:, 0:w], func=AF.Ln),
                 reads=[pD.buf], writes=[rc.buf])
            S.op("act", lambda e, rc=rc, w=w: e.activation(out=rc.ap[:, 0:w], in_=rc.ap[:, 0:w], func=AF.Exp, scale=-1.0),
                 reads=[rc.buf], writes=[rc.buf])
            S.op("dve", lambda e, rc=rc, sz=sz, w=w: e.tensor_tensor(out=rc.ap[:, 0:w], in0=rc.ap[:, 0:w],
                                                                    in1=sz.ap[:, 0:w], op=ALU.mult),
                 reads=[rc.buf, sz.buf], writes=[rc.buf])
            yb = ybs.next()
            S.op("dve", lambda e, yb=yb, pO=pO, rc=rc, w=w: e.tensor_tensor(out=yb.ap[:, 0:w], in0=pO.ap[:, 0:w],
                                                                           in1=rc.ap[:, 0:w], op=ALU.mult),
                 reads=[pO.buf, rc.buf], writes=[yb.buf])
            S.dma("sp", lambda e, yb=yb, h=h, t0=t0, w=w: e.dma_start(out=Yd[2, h * 128:(h + 1) * 128, t0:t0 + w],
                                                                     in_=yb.ap[:, 0:w]),
                  reads=[yb.buf], writes=[Y_b[2]])

    def phase_merge(l, last):
        PSG = Rot(PSB)
        for n in range(3 if "nom1" not in phases else 0):
            S.barrier()
            AR.reset()
            Yt = AR.bf16(NCH, T)
            sgs = Rot([AR.f32(512) for _ in range(3)])
            pbs = Rot([AR.bf16(512) for _ in range(3)])
            for kc in range(NCH):
                S.dma("sp", lambda e, n=n, Yt=Yt, kc=kc: e.dma_start(out=Yt.ap[:, kc, :],
                                                                    in_=Yd[n, kc * 128:(kc + 1) * 128, :]),
                      reads=[Y_b[n]], writes=[Yt.buf])
            for dj in range(NCH):
                wb = load_w(w_br[l, n, :, dj * 128:(dj + 1) * 128])
                wg = load_w(w_in[l, :, O_G + n * D + dj * 128:O_G + n * D + (dj + 1) * 128])
                for ti, (t0, w) in enumerate(TT):
                    psP = PSG.next()

                    def f(e, psP=psP, wb=wb, t0=t0, w=w, Yt=Yt):
                        for kc in range(NCH):
                            ins = e.matmul(psP.ap[:, 0:w], lhsT=wb.ap[:, kc, :], rhs=Yt.ap[:, kc, t0:t0 + w],
                                           start=(kc == 0), stop=(kc == NCH - 1))
                        return ins
                    S.op("pe", f, reads=[wb.buf, Yt.buf], writes=[psP.buf])
                    psG = PSG.next()
                    proj_fm(wg, 128, ti, psG)
                    sg = sgs.next()
                    S.op("act", lambda e, sg=sg, psG=psG, w=w: e.activation(out=sg.ap[:, 0:w], in_=psG.ap[:, 0:w],
                                                                           func=AF.Sigmoid),
                         reads=[psG.buf], writes=[sg.buf])
                    pb = pbs.next()
                    S.op("dve", lambda e, pb=pb, psP=psP, sg=sg, w=w: e.tensor_tensor(out=pb.ap[:, 0:w], in0=psP.ap[:, 0:w],
                                                                                     in1=sg.ap[:, 0:w], op=ALU.mult),
                         reads=[psP.buf, sg.buf], writes=[pb.buf])
                    for mt in range(t0 // 256, (t0 + w) // 256):
                        o = mt * 256 - t0
                        S.dma("sp", lambda e, pb=pb, n=n, dj=dj, mt=mt, o=o: e.dma_start(
                            out=Gd[n, mt, :, dj * 256:(dj + 1) * 256], in_=pb.ap[:, o:o + 256]),
                            reads=[pb.buf], writes=[G_b[n]])
        if "nom2" in phases:
            return
        S.barrier()
        AR.reset()
        wout = hxT[:, :, 0:D]
        wo_b = Buf("wout")
        for dj in range(NCH):
            S.dma("pool", lambda e, dj=dj: e.dma_start(
                out=wout[:, :, dj * 128:(dj + 1) * 128],
                in_=w_out[l, :, dj * 128:(dj + 1) * 128].rearrange("(k p) c -> p k c", p=128)), writes=[wo_b])
        Gts = [Rot([AR.bf16(NCH, 256) for _ in range(k_)]) for k_ in (2, 1, 1)]
        yos = Rot([AR.f32(NCH, 256) for _ in range(2)])
        xts = Rot([AR.f32(NCH, 256) for _ in range(2)])
        sqs = Rot([AR.bf16(256) for _ in range(4)])
        rsds = Rot([AR.f32(256) for _ in range(2)])
        tmps = Rot([AR.f32(256) for _ in range(2)])
        PSG7 = Rot(PSB[0:7])
        mts = range(T // 256) if not last else range(1, T // 256)
        for mt in mts:
            t0 = mt * 256
            i = 1 if mt == 0 else 0
            gt = []
            for n in range(3):
                g_ = Gts[n].next()
                S.dma("sp", lambda e, g_=g_, n=n, mt=mt: e.dma_start(
                    out=g_.ap, in_=Gd[n, mt].rearrange("p (k t) -> p k t", k=NCH)),
                    reads=[G_b[n]], writes=[g_.buf])
                gt.append(g_)
            xt = xts.next()
            if l == 0:
                S.dma("sp", lambda e, xt=xt, t0=t0: e.dma_start(
                    out=xt.ap, in_=xT0[:, t0:t0 + 256].rearrange("(k p) t -> p k t", p=128)), writes=[xt.buf])
            else:
                S.dma("sp", lambda e, xt=xt, mt=mt: e.dma_start(
                    out=xt.ap, in_=xres[mt].rearrange("p (k t) -> p k t", k=NCH)), reads=[xres_b[mt]], writes=[xt.buf])
            S.op("dve", lambda e, gt=gt: e.tensor_tensor(out=gt[0].ap, in0=gt[0].ap, in1=gt[1].ap, op=ALU.add),
                 reads=[gt[0].buf, gt[1].buf], writes=[gt[0].buf])
            S.op("dve", lambda e, gt=gt: e.tensor_tensor(out=gt[0].ap, in0=gt[0].ap, in1=gt[2].ap, op=ALU.add),
                 reads=[gt[0].buf, gt[2].buf], writes=[gt[0].buf])
            mg = gt[0]
            yo = yos.next()
            rsd = rsds.next()
            pss = PSB[7]
            pend = []

            def emit_ones(sq, dj, pss=pss):
                S.op("pe", lambda e: e.matmul(pss.ap[:, 0:256], lhsT=ones_b.ap, rhs=sq.ap,
                                              start=(dj == 0), stop=(dj == NCH - 1)),
                     reads=[sq.buf, ones_b.buf], writes=[pss.buf])
            PSG = PSG7
            for dj in range(NCH):
                ps = PSG.next()

                def f(e, ps=ps, dj=dj, mg=mg):
                    for kc in range(NCH):
                        ins = e.matmul(ps.ap[:, 0:256], lhsT=wout[:, kc, dj * 128:(dj + 1) * 128], rhs=mg.ap[:, kc, :],
                                       start=(kc == 0), stop=(kc == NCH - 1))
                    return ins
                S.op("pe", f, reads=[wo_b, mg.buf], writes=[ps.buf])
                S.op("dve", lambda e, ps=ps, dj=dj, yo=yo: e.tensor_copy(out=yo.ap[:, dj, :], in_=ps.ap[:, 0:256]),
                     reads=[ps.buf], writes=[yo.buf])
                sq = sqs.next()
                S.op("act", lambda e, ps=ps, sq=sq: e.activation(out=sq.ap, in_=ps.ap[:, 0:256], func=AF.Square),
                     reads=[ps.buf], writes=[sq.buf])
                pend.append((sq, dj))
                if len(pend) > 2:
                    emit_ones(*pend.pop(0))
            while pend:
                emit_ones(*pend.pop(0))
            S.op("act", lambda e, pss=pss, rsd=rsd: e.activation(out=rsd.ap, in_=pss.ap[:, 0:256], func=AF.Ln,
                                                                scale=1.0 / D, bias=EPS),
                 reads=[pss.buf], writes=[rsd.buf])
            S.op("act", lambda e, rsd=rsd: e.activation(out=rsd.ap, in_=rsd.ap, func=AF.Exp, scale=-0.5),
                 reads=[rsd.buf], writes=[rsd.buf])
            for dj in range(NCH):
                tp = tmps.next()
                S.op("dve", lambda e, tp=tp, dj=dj, yo=yo, rsd=rsd: e.tensor_tensor(out=tp.ap, in0=yo.ap[:, dj, :],
                                                                                  in1=rsd.ap, op=ALU.mult),
                     reads=[yo.buf, rsd.buf], writes=[tp.buf])
                S.op("dve", lambda e, tp=tp, dj=dj, xt=xt, i=i: e.scalar_tensor_tensor(
                    out=xt.ap[:, dj, :], in0=tp.ap, scalar=Gmod.ap[:, dj, i:i + 1], in1=xt.ap[:, dj, :],
                    op0=ALU.mult, op1=ALU.add), reads=[tp.buf, Gmod.buf, xt.buf], writes=[xt.buf])
            if last:
                S.dma("sp", lambda e, xt=xt, t0=t0: e.dma_start(
                    out=out_d[:, t0 - TCX:t0 - TCX + 256].rearrange("(k p) t -> p k t", p=128), in_=xt.ap),
                    reads=[xt.buf], writes=[out_b])
            else:
                S.dma("sp", lambda e, xt=xt, mt=mt: e.dma_start(
                    out=xres[mt].rearrange("p (k t) -> p k t", k=NCH), in_=xt.ap),
                    reads=[xt.buf], writes=[xres_b[mt]])

    for l in range(n_layers):
        last = (l == n_layers - 1) and not debug
        S.barrier()
        phase0(l)
        S.barrier()
        phaseA(l)
        if "lru" in phases:
            S.barrier()
            phase_lru(l)
        if "ml" in phases:
            S.barrier()
            phase_ml(l)
        if "att" in phases:
            S.barrier()
            phase_att(l)
        if "merge" in phases:
            phase_merge(l, last)
    S.barrier()
    S.finalize()
    return nc


def _fm(v):
    v = np.asarray(v, np.float32)
    lead = v.shape[:-1]
    r = v.reshape(lead + (16, 128))
    return np.moveaxis(r, -1, 0)


def _host_consts():
    s = np.arange(128)
    trif = (s[:, None] <= s[None, :]).astype(np.float32)
    trib = (s[:, None] >= s[None, :]).astype(np.float32)
    R = np.zeros((128, 128), np.float32)
    for m in range(64):
        R[m + 64, m] = -1.0
        R[m, m + 64] = 1.0
    ident = np.eye(128, dtype=np.float32)
    cst = np.concatenate([trif, trib, R, ident], axis=1)
    n_freq = 32
    inv = (1.0 / (np.float32(10000.0) ** (np.arange(n_freq, dtype=np.float32) / np.float32(n_freq)))).astype(np.float32)
    row = np.repeat(np.arange(TLAT // 64), 64).astype(np.float32)
    col = np.tile(np.arange(64), TLAT // 64).astype(np.float32)
    ang = np.concatenate([row[:, None] * inv, col[:, None] * inv], axis=-1).astype(np.float32)
    cos = np.cos(ang).astype(np.float32)
    sin = np.sin(ang).astype(np.float32)
    cosT = np.ascontiguousarray(np.concatenate([cos, cos], axis=1).T)
    sinT = np.ascontiguousarray(np.concatenate([sin, sin], axis=1).T)
    return cst, cosT, sinT


def make_in_maps(inputs, cores):
    f32 = np.float32
    L = 4
    vecs = np.zeros((128, L, NV), f32)
    vecs[:, :, V_ADAB:V_ADAB + 48] = np.moveaxis(np.asarray(inputs["ada_b"], f32).reshape(L, 48, 128), -1, 0)
    vecs[:, :, V_NPRE:V_NPRE + 16] = _fm(inputs["norm_pre"])
    vecs[:, :, V_NPOST:V_NPOST + 16] = _fm(inputs["norm_post"])
    cw = _fm(inputs["lru_conv_w"])
    vecs[:, :, V_CW:V_CW + 64] = np.swapaxes(cw, 2, 3).reshape(128, L, 64)
    vecs[:, :, V_CB:V_CB + 16] = _fm(inputs["lru_conv_b"])
    vecs[:, :, V_BR:V_BR + 32] = _fm(inputs["lru_br"]).reshape(128, L, 32)
    vecs[:, :, V_BI:V_BI + 32] = _fm(inputs["lru_bi"]).reshape(128, L, 32)
    vecs[:, :, V_LAM:V_LAM + 32] = _fm(inputs["lru_lam"]).reshape(128, L, 32)
    vecs[:, :, V_MLN:V_MLN + 16] = _fm(inputs["ml_norm"])
    vecs[:, :, V_QN] = np.asarray(inputs["q_norm"], f32).T
    vecs[:, :, V_KN] = np.asarray(inputs["k_norm"], f32).T
    vecs = np.ascontiguousarray(vecs.reshape(128, L * NV))
    gb = np.asarray(inputs["ml_gate_b"], f32).reshape(L, 32)
    gbB = np.ascontiguousarray(np.broadcast_to(gb.reshape(1, L * 32), (128, L * 32)))
    cst, cosT, sinT = _host_consts()
    shared = {
        "ada_w": np.ascontiguousarray(inputs["ada_w"], dtype=f32),
        "w_in": np.ascontiguousarray(inputs["w_in"], dtype=f32),
        "w_br": np.ascontiguousarray(inputs["w_br"], dtype=f32),
        "w_out": np.ascontiguousarray(inputs["w_out"], dtype=f32),
        "lru_wr": np.ascontiguousarray(inputs["lru_wr"], dtype=f32),
        "lru_wi": np.ascontiguousarray(inputs["lru_wi"], dtype=f32),
        "vecs": vecs, "gbB": gbB, "cst": cst, "cosT": cosT, "sinT": sinT,
    }
    maps = []
    cc = np.asarray(inputs["c_ctx"], f32)
    for b in cores:
        xT = np.ascontiguousarray(np.concatenate([np.asarray(inputs["ctx"][b], f32), np.asarray(inputs["x"][b], f32)],
                                                 axis=0).T)
        cT = np.stack([np.asarray(inputs["c"][b], f32), cc], axis=-1)
        cT = np.ascontiguousarray(np.moveaxis(cT.reshape(16, 128, 2), 1, 0).reshape(128, 32))
        m = dict(shared)
        m["xT0"] = xT
        m["cT"] = cT
        maps.append(m)
    return maps


_NC_CACHE = {}


def kernel(**inputs):
    if "full" not in _NC_CACHE:
        _NC_CACHE["full"] = build(4, False)
    nc = _NC_CACHE["full"]
    B = 4
    maps = make_in_maps(inputs, list(range(B)))
    res = run_bass_kernel_spmd(nc, maps, core_ids=list(range(B)))
    out = np.stack([np.ascontiguousarray(res.results[b]["outT"].T) for b in range(B)], axis=0)
    return out.astype(np.float32)
```
